# Optimizing a Trainium2 kernel written in Bass

```python
import jax, jax.numpy as jnp
from jax import lax
import numpy as np

D_MODEL = 1024
BATCH = 2
SEQ = 8192
DEPTH = 1

HEAD_DIM = 64
N_RWKV_HEADS = 8
RWKV_WIDTH = N_RWKV_HEADS * HEAD_DIM
DECAY_LORA = 64
AAA_LORA = 64
GATE_LORA = 128
GN_EPS = 64e-5
N_Q_HEADS = 8
N_KV_HEADS = 2
Q_PER_KV = N_Q_HEADS // N_KV_HEADS
ATTN_WIDTH = N_Q_HEADS * HEAD_DIM
KV_WIDTH = N_KV_HEADS * HEAD_DIM
WINDOW = 128
BLOCK = 128
RWKV_COLS = 3 * RWKV_WIDTH + DECAY_LORA + AAA_LORA + GATE_LORA
ATTN_COLS = ATTN_WIDTH + 2 * KV_WIDTH
IN_COLS = RWKV_COLS + ATTN_COLS
MIX_WIDTH = RWKV_WIDTH + ATTN_WIDTH
D_FF = 2816
NORM_EPS = 1e-5

kernel_name = 'hymba_rwkv7_swa_sink_macaron'


def rms_norm(x, g):
    xf = x.astype(jnp.float32)
    y = xf * lax.rsqrt(jnp.mean(xf * xf, axis=-1, keepdims=True) + NORM_EPS)
    return (y * g.astype(jnp.float32)).astype(x.dtype)


def swiglu(x, w_gate, w_up, w_down):
    return (jax.nn.silu(x @ w_gate) * (x @ w_up)) @ w_down


def rwkv7_time_mix(p, shift_mix, w0, w2, a0, a2, g2, k_k, k_a, r_k, ln_w, ln_b):
    f32 = jnp.float32
    B, S, _ = p.shape
    H, N = N_RWKV_HEADS, HEAD_DIM
    p = p.astype(f32)
    p_prev = jnp.pad(p, ((0, 0), (1, 0), (0, 0)))[:, :-1]
    p = p + (p_prev - p) * shift_mix.astype(f32)
    cuts = [RWKV_WIDTH, 2 * RWKV_WIDTH, 3 * RWKV_WIDTH,
            3 * RWKV_WIDTH + DECAY_LORA, 3 * RWKV_WIDTH + DECAY_LORA + AAA_LORA]
    r, k, v, w_lo, a_lo, g_lo = jnp.split(p, cuts, axis=-1)
    w = -jax.nn.softplus(-(w0.astype(f32) + jnp.tanh(w_lo) @ w2.astype(f32))) - 0.5
    a = jax.nn.sigmoid(a0.astype(f32) + a_lo @ a2.astype(f32))
    g = jax.nn.sigmoid(g_lo) @ g2.astype(f32)
    kk = (k * k_k.astype(f32)).reshape(B, S, H, N)
    kk = kk / jnp.maximum(jnp.sqrt(jnp.sum(kk * kk, axis=-1, keepdims=True)), 1e-12)
    k = k * (1.0 + (a - 1.0) * k_a.astype(f32))
    rh, kh, vh, ah = [t.reshape(B, S, H, N) for t in (r, k, v, a)]
    decay = jnp.exp(-jnp.exp(w)).reshape(B, S, H, N)
    bh = kk * ah

    def step(state, inp):
        r_t, w_t, k_t, v_t, kk_t, b_t = inp
        sa = jnp.einsum('bhvk,bhk->bhv', state, -kk_t)
        state = (state * w_t[:, :, None, :] + sa[..., None] * b_t[:, :, None, :]
                 + v_t[..., None] * k_t[:, :, None, :])
        return state, jnp.einsum('bhvk,bhk->bhv', state, r_t)

    seq_first = lambda t: jnp.moveaxis(t, 1, 0)
    s0 = jnp.zeros((B, H, N, N), f32)
    _, o = lax.scan(step, s0, tuple(seq_first(t) for t in (rh, decay, kh, vh, kk, bh)))
    o = jnp.moveaxis(o, 0, 1)
    mu = jnp.mean(o, axis=-1, keepdims=True)
    var = jnp.mean(jnp.square(o - mu), axis=-1, keepdims=True)
    o = ((o - mu) * lax.rsqrt(var + GN_EPS)).reshape(B, S, RWKV_WIDTH)
    o = o * ln_w.astype(f32) + ln_b.astype(f32)
    bonus = jnp.sum(rh * kh * r_k.astype(f32), axis=-1, keepdims=True) * vh
    return (o + bonus.reshape(B, S, RWKV_WIDTH)) * g


def sliding_window_sink_attention(q, k, v, sinks):
    f32 = jnp.float32
    B, S, _ = q.shape
    nb = S // BLOCK
    qb = q.astype(f32).reshape(B, nb, BLOCK, N_KV_HEADS, Q_PER_KV, HEAD_DIM)
    kb = k.astype(f32).reshape(B, nb, BLOCK, N_KV_HEADS, HEAD_DIM)
    vb = v.astype(f32).reshape(B, nb, BLOCK, N_KV_HEADS, HEAD_DIM)

    def with_prev_block(t):
        prev = jnp.pad(t, ((0, 0), (1, 0), (0, 0), (0, 0), (0, 0)))[:, :-1]
        return jnp.concatenate([prev, t], axis=2)

    kb, vb = with_prev_block(kb), with_prev_block(vb)
    scores = jnp.einsum('bnqhgd,bnkhd->bnhgqk', qb, kb) * (HEAD_DIM ** -0.5)
    qi = jnp.arange(BLOCK)[:, None]
    kj = jnp.arange(2 * BLOCK)[None, :]
    dist = qi + BLOCK - kj
    band = (dist >= 0) & (dist < WINDOW)
    valid = band[None] & ((jnp.arange(nb)[:, None, None] > 0) | (kj[None] >= BLOCK))
    scores = jnp.where(valid[None, :, None, None], scores, -jnp.inf)
    sink = jnp.broadcast_to(sinks.astype(f32).reshape(N_KV_HEADS, Q_PER_KV, 1, 1),
                            scores.shape[:-1] + (1,))
    probs = jax.nn.softmax(jnp.concatenate([scores, sink], axis=-1), axis=-1)[..., :-1]
    out = jnp.einsum('bnhgqk,bnkhd->bnqhgd', probs, vb)
    return out.reshape(B, S, ATTN_WIDTH)


def setup_inputs(seed: int = 0) -> dict:
    key = jax.random.key(seed)
    ks = jax.random.split(key, 26)
    L, D, F, R = DEPTH, D_MODEL, D_FF, RWKV_WIDTH
    nrm = lambda i, shape, scale: scale * jax.random.normal(ks[i], shape, jnp.float32)
    chan = jnp.arange(R, dtype=jnp.float32) / (R - 1)
    w0_base = -5.5 + 5.0 * chan ** 0.85
    return {
        'x': nrm(0, (BATCH, SEQ, D), 1.0),
        'norm_ffn1': 1.0 + nrm(1, (L, D), 0.01),
        'ffn1_gate': nrm(2, (L, D, F), D ** -0.5),
        'ffn1_up': nrm(3, (L, D, F), D ** -0.5),
        'ffn1_down': nrm(4, (L, F, D), F ** -0.5),
        'norm_mix': 1.0 + nrm(5, (L, D), 0.01),
        'w_in': nrm(6, (L, D, IN_COLS), D ** -0.5),
        'b_in_attn': nrm(7, (L, ATTN_COLS), 0.01),
        'rwkv_shift_mix': jax.random.uniform(ks[8], (L, RWKV_COLS), jnp.float32),
        'rwkv_w0': w0_base[None, :] + nrm(9, (L, R), 0.1),
        'rwkv_w2': nrm(10, (L, DECAY_LORA, R), 0.1 * DECAY_LORA ** -0.5),
        'rwkv_a0': nrm(11, (L, R), 0.1),
        'rwkv_a2': nrm(12, (L, AAA_LORA, R), AAA_LORA ** -0.5),
        'rwkv_g2': nrm(13, (L, GATE_LORA, R), GATE_LORA ** -0.5),
        'rwkv_k_k': 0.85 + nrm(14, (L, R), 0.05),
        'rwkv_k_a': 1.0 + nrm(15, (L, R), 0.05),
        'rwkv_r_k': nrm(16, (L, N_RWKV_HEADS, HEAD_DIM), 0.1),
        'rwkv_ln_w': 1.0 + nrm(17, (L, R), 0.01),
        'rwkv_ln_b': nrm(18, (L, R), 0.01),
        'attn_sinks': nrm(19, (L, N_Q_HEADS), 1.0),
        'w_out': nrm(20, (L, MIX_WIDTH, D), 0.5 * MIX_WIDTH ** -0.5),
        'norm_ffn2': 1.0 + nrm(21, (L, D), 0.01),
        'ffn2_gate': nrm(22, (L, D, F), D ** -0.5),
        'ffn2_up': nrm(23, (L, D, F), D ** -0.5),
        'ffn2_down': nrm(24, (L, F, D), F ** -0.5),
        'norm_final': 1.0 + nrm(25, (D,), 0.01),
    }


def reference(x, norm_ffn1, ffn1_gate, ffn1_up, ffn1_down, norm_mix, w_in, b_in_attn,
              rwkv_shift_mix, rwkv_w0, rwkv_w2, rwkv_a0, rwkv_a2, rwkv_g2, rwkv_k_k, rwkv_k_a,
              rwkv_r_k, rwkv_ln_w, rwkv_ln_b, attn_sinks, w_out, norm_ffn2, ffn2_gate, ffn2_up,
              ffn2_down, norm_final):
    for l in range(DEPTH):
        h = rms_norm(x, norm_ffn1[l])
        x = x + 0.5 * swiglu(h, ffn1_gate[l], ffn1_up[l], ffn1_down[l])
        h = rms_norm(x, norm_mix[l])
        p = h @ w_in[l]
        p_rwkv = p[..., :RWKV_COLS]
        p_attn = p[..., RWKV_COLS:] + b_in_attn[l]
        q, k, v = jnp.split(p_attn, [ATTN_WIDTH, ATTN_WIDTH + KV_WIDTH], axis=-1)
        o_rwkv = rwkv7_time_mix(p_rwkv, rwkv_shift_mix[l], rwkv_w0[l], rwkv_w2[l], rwkv_a0[l],
                                rwkv_a2[l], rwkv_g2[l], rwkv_k_k[l], rwkv_k_a[l], rwkv_r_k[l],
                                rwkv_ln_w[l], rwkv_ln_b[l])
        o_attn = sliding_window_sink_attention(q, k, v, attn_sinks[l])
        mixed = jnp.concatenate([o_rwkv.astype(x.dtype), o_attn.astype(x.dtype)], axis=-1)
        x = x + mixed @ w_out[l]
        h = rms_norm(x, norm_ffn2[l])
        x = x + 0.5 * swiglu(h, ffn2_gate[l], ffn2_up[l], ffn2_down[l])
    return rms_norm(x, norm_final)
```

```python
import contextlib
import numpy as np
import concourse.bass as bass
import concourse.mybir as mybir
from concourse.bass_utils import run_bass_kernel_spmd

F32 = mybir.dt.float32
BF16 = mybir.dt.bfloat16
AF = mybir.ActivationFunctionType
ALU = mybir.AluOpType
AX = mybir.AxisListType

D = 1024
DFF = 2816
NTOK = 2048
NT = 17
NTH = NT * 128
NCORES = 8
SEM_LIM = 20000
NLANES = 12


class _Op:
    __slots__ = ("q", "fn", "deps", "need_inc", "semidx", "semval", "lane", "laneval", "idx")


class Prog:
    CE = ("pe", "act", "dve", "pool")
    QS = ("pe", "act", "dve", "pool", "sp")

    def __init__(self, nc):
        self.nc = nc
        self.ops = {q: [] for q in self.QS}
        self.last_w = {}
        self.readers = {}
        self.lane_rr = {q: 0 for q in self.QS}
        self.lane_cnt = {}
        self.lane_last = {}
        self.enabled = True

    def _deps(self, q, reads, writes, is_dma):
        deps = set()
        for r in reads:
            ev = self.last_w.get(r)
            if ev is not None:
                deps.add(ev)
        for w in writes:
            ev = self.last_w.get(w)
            if ev is not None:
                deps.add(ev)
            for ev in self.readers.get(w, ()):
                deps.add(ev)
        if q == "pe" and not is_dma:
            deps = {e for e in deps if not (e[0] == "c" and e[1] == "pe")}
        return deps

    def _commit(self, ev, reads, writes):
        for r in reads:
            self.readers.setdefault(r, []).append(ev)
        for w in writes:
            self.last_w[w] = ev
            self.readers[w] = []

    def op(self, q, name, reads, writes, *args, **kw):
        if not self.enabled:
            return None
        o = _Op()
        fn = (name, args, kw)
        writes = list(writes) + [r for r in reads if isinstance(r, tuple) and r[0] == "ps"]
        o.q, o.fn, o.need_inc, o.lane = q, fn, False, None
        o.deps = self._deps(q, reads, writes, False)
        o.idx = len(self.ops[q])
        self.ops[q].append(o)
        self._commit(("c", q, o.idx), reads, writes)
        return o

    def dma(self, q, reads, writes, out, in_):
        if not self.enabled:
            return _Op()
        o = _Op()
        fn = ("dma_start", (), {"out": out, "in_": in_})
        o.q, o.fn, o.need_inc = q, fn, False
        o.deps = self._deps(q, reads, writes, True)
        lane = self.lane_rr[q]
        self.lane_rr[q] = (lane + 1) % NLANES
        key = (q, lane)
        prev = self.lane_last.get(key)
        if prev is not None:
            o.deps.add(prev)
        cnt = self.lane_cnt.get(key, 0) + 16
        self.lane_cnt[key] = cnt
        o.lane, o.laneval = key, cnt
        o.idx = len(self.ops[q])
        self.ops[q].append(o)
        ev = ("d", key, cnt)
        self.lane_last[key] = ev
        self._commit(ev, reads, writes)
        return o

    def coll(self, q, reads, writes, *args, **kw):
        o = self.dma(q, reads, writes, None, None)
        o.fn = ("collective_compute", args, kw)
        return o

    def barrier(self, q):
        if not self.enabled:
            return
        o = _Op()
        o.q, o.fn, o.need_inc, o.lane = q, None, False, None
        o.deps = set(self.lane_last.values())
        for e in self.CE:
            if self.ops[e]:
                last = [x for x in self.ops[e] if x.fn is not None and x.lane is None]
                if last:
                    o.deps.add(("c", e, last[-1].idx))
        o.idx = len(self.ops[q])
        self.ops[q].append(o)

    def wait_all(self, q):
        o = _Op()
        o.q, o.fn, o.need_inc, o.lane = q, None, False, None
        o.deps = set(self.lane_last.values())
        o.idx = len(self.ops[q])
        self.ops[q].append(o)

    def emit(self):
        nc = self.nc
        for q in self.QS:
            for o in self.ops[q]:
                for ev in o.deps:
                    if ev[0] == "c":
                        self.ops[ev[1]][ev[2]].need_inc = True
        nsem = {}
        for e in self.CE:
            cnt = 0
            for o in self.ops[e]:
                if o.need_inc:
                    o.semidx, o.semval = cnt // SEM_LIM, cnt % SEM_LIM + 1
                    cnt += 1
            nsem[e] = (cnt + SEM_LIM - 1) // SEM_LIM
        with contextlib.ExitStack() as st:
            sems = {}
            for e in self.CE:
                for k in range(nsem[e]):
                    sems[("c", e, k)] = st.enter_context(nc.semaphore(f"s_{e}_{k}"))
            for key in self.lane_cnt:
                sems[("d",) + key] = st.enter_context(nc.semaphore(f"l_{key[0]}_{key[1]}"))
            block = st.enter_context(nc.Block())

            def run(q, eng):
                waited = {}
                for o in self.ops[q]:
                    need = {}
                    for ev in o.deps:
                        if ev[0] == "c":
                            src = self.ops[ev[1]][ev[2]]
                            k, v = ("c", ev[1], src.semidx), src.semval
                        else:
                            k, v = ("d",) + ev[1], ev[2]
                        if need.get(k, 0) < v:
                            need[k] = v
                    for k, v in need.items():
                        if waited.get(k, 0) < v:
                            eng.wait_ge(sems[k], v)
                            waited[k] = v
                    if o.fn is None:
                        continue
                    name, args, kw = o.fn
                    ins = getattr(eng, name)(*args, **kw)
                    if o.lane is not None:
                        ins.then_inc(sems[("d",) + o.lane], 16)
                    elif o.need_inc:
                        ins.then_inc(sems[("c", q, o.semidx)], 1)

            if self.ops["pe"]:
                @block.tensor
                def _(eng):
                    run("pe", eng)
            if self.ops["act"]:
                @block.scalar
                def _(eng):
                    run("act", eng)
            if self.ops["dve"]:
                @block.vector
                def _(eng):
                    run("dve", eng)
            if self.ops["pool"]:
                @block.gpsimd
                def _(eng):
                    run("pool", eng)
            if self.ops["sp"]:
                @block.sync
                def _(eng):
                    run("sp", eng)


FFN_GROUPS = [(0, 6), (6, 6), (12, 5), (17, 5)]
SB_BASE = 16512
SB_LIMIT = 16512 + 212800
_DTSZ = {F32: 4, BF16: 2}
NEG = -30000.0


class Arena:
    def __init__(self, nc):
        self.nc, self.off, self.n = nc, SB_BASE, 0

    def alloc(self, name, shape, dt=F32):
        n = _DTSZ[dt]
        for d in shape[1:]:
            n *= d
        off = (self.off + 31) // 32 * 32
        assert off + n <= SB_LIMIT, (name, off + n - SB_LIMIT)
        self.off = off + n
        self.n += 1
        return self.nc.alloc_sbuf_tensor_at(f"{name}_{self.n}", list(shape), dt, offset=off)


def build(stage="full"):
    nc = bass.Bass("TRN2", target_bir_lowering=False)
    P = Prog(nc)
    es = contextlib.ExitStack()
    A = Arena(nc)
    sb = A.alloc

    def dram(name, shape, kind="ExternalInput", dt=F32):
        return nc.dram_tensor(name, list(shape), dt, kind=kind)

    xs = dram("xs", [NTH, D])
    out = dram("out", [NTOK, D], kind="ExternalOutput")
    ident_d = dram("ident", [128, 128])
    maskA_d = dram("maskA", [128, 256])
    mask1_d = dram("mask1", [128, 256])
    wdram = {}
    for nm, shp in (("ffn1_gate", [D, DFF]), ("ffn1_up", [D, DFF]), ("ffn1_down", [DFF, D]),
                    ("ffn2_gate", [D, DFF]), ("ffn2_up", [D, DFF]), ("ffn2_down", [DFF, D]),
                    ("w_in", [D, 2560]), ("w_out", [D, D])):
        wdram[nm] = dram(nm, shp)
    vec_d = {}
    for nm in ("norm_ffn1", "norm_mix", "norm_ffn2", "norm_final"):
        vec_d[nm] = dram(nm, [1, D])
    b_attn_d = dram("b_in_attn", [768, 1])
    sinks_d = dram("attn_sinks", [1, 8])
    rw_d = {}
    rw_d["rwkv_shift_mix"] = dram("rwkv_shift_mix", [1792, 1])
    for nm in ("rwkv_w0", "rwkv_a0", "rwkv_k_k", "rwkv_k_a", "rwkv_r_k", "rwkv_ln_w", "rwkv_ln_b"):
        rw_d[nm] = dram(nm, [512, 1])
    rw_d["rwkv_w2"] = dram("rwkv_w2", [64, 512])
    rw_d["rwkv_a2"] = dram("rwkv_a2", [64, 512])
    rw_d["rwkv_g2"] = dram("rwkv_g2", [128, 512])
    cmask_d = dram("cmask", [128, 384])
    bones_d = dram("bones", [128, 128])
    scanm_d = dram("scanm", [1, 512])
    fm_d = dram("fm", [1, 8])
    use_cc = "ccin" not in stage
    if use_cc:
        cc_in = [dram(f"cc_in{j}", [128, 256], kind="Internal") for j in range(4)]
        cc_out = [dram(f"cc_out{j}", [NCORES * 128, 256], kind="Internal") for j in range(4)]

        def ccg_view(j):
            return cc_out[j].ap().rearrange("(r p) f -> p r f", p=128)
    else:
        ccg_d = dram("ccg", [NCORES * 128, 1024])
        sf_dbg = dram("sf_dbg", [128, 1024], kind="ExternalOutput")

        def ccg_view(j):
            return ccg_d.ap()[:, j * 256:(j + 1) * 256].rearrange("(r p) f -> p r f", p=128)

    X = sb("X", [128, NT, D])
    HT = sb("HT", [128, 8, NTH], BF16)
    gbc = sb("gbc", [128, D])
    hb = [sb(f"hb{i}", [128, D], BF16) for i in range(2)]
    junk = sb("junk", [128, D], BF16)
    ss = sb("ss", [128, 4 * NT])
    rstd = sb("rstd", [128, 4 * NT])
    eps_t = sb("eps_t", [128, 1])
    ident_f = sb("ident_f", [128, 128])
    ident_b = sb("ident_b", [128, 128], BF16)
    phase_mark = A.off

    PSALL = es.enter_context(nc.psum_tensor("psall", [128, 4096], F32))
    PSALLB = PSALL.bitcast(BF16)

    def psf(bank, a=0, b=512):
        return PSALL[:, bank * 512 + a:bank * 512 + b]

    def psb(bank, a=0, b=1024):
        return PSALLB[:, bank * 1024 + a:bank * 1024 + b]

    def barrier():
        for q in ("pe", "act", "dve", "pool", "sp"):
            P.barrier(q)

    def mark(name):
        if ("stop_" + name + "_") in (stage + "_"):
            P.enabled = False

    P.dma("sp", [], ["ident_f"], ident_f[:], ident_d.ap())
    P.op("act", "copy", ["ident_f"], ["ident_b"], out=ident_b[:], in_=ident_f[:])
    P.op("dve", "memset", [], ["ss"], ss[:], 0.0)
    P.op("dve", "memset", [], ["eps_t"], eps_t[:], 1e-5)

    norm_ctr = [0]

    def load_gain(name):
        P.dma("sp", [], ["gbc"], gbc[:], vec_d[name].ap().partition_broadcast(128))

    def rmsnorm_tile(i, dst_fn):
        k = norm_ctr[0]
        norm_ctr[0] += 1
        col = k % (4 * NT)
        P.op("act", "activation", [("X", i), "ss"], ["junk", ("ss", col)],
             out=junk[:], in_=X[:, i, :], func=AF.Square, accum_out=ss[:, col:col + 1])
        P.op("act", "activation", [("ss", col), "eps_t"], [("rstd", col)],
             out=rstd[:, col:col + 1], in_=ss[:, col:col + 1], func=AF.Sqrt, bias=eps_t[:, 0:1], scale=1.0 / D)
        P.op("dve", "reciprocal", [("rstd", col)], [("rstd", col)],
             out=rstd[:, col:col + 1], in_=rstd[:, col:col + 1])
        dst_fn(i, col)

    def norm_to_HT(i, col):
        b = i % 2
        P.op("dve", "scalar_tensor_tensor", [("X", i), ("rstd", col), "gbc"], [("hb", b)],
             out=hb[b][:], in0=X[:, i, :], scalar=rstd[:, col:col + 1],
             in1=gbc[:], op0=ALU.mult, op1=ALU.mult)
        bank = 4 + (i % 4)
        for c in range(8):
            P.op("pe", "transpose", [("hb", b), "ident_b"], [("ps", bank)],
                 out=psb(bank, c * 128, (c + 1) * 128),
                 in_=hb[b][:, c * 128:(c + 1) * 128], identity=ident_b[:])
        P.op("act", "copy", [("ps", bank)], [("HT", i)],
             out=HT[:, :, i * 128:(i + 1) * 128],
             in_=psb(bank).rearrange("p (c t) -> p c t", c=8))

    def ffn(prefix, tiles):
        A.off = phase_mark
        Wg = [sb(f"Wg{i}", [128, 8, 768], BF16) for i in range(2)]
        Wu = [sb(f"Wu{i}", [128, 8, 768], BF16) for i in range(2)]
        Wd = [sb(f"Wd{i}", [128, 6, D], BF16) for i in range(2)]
        sg = [sb(f"sg{i}", [128, 256]) for i in range(2)]
        aT = [sb(f"aT{i}", [128, 256], BF16) for i in range(3)]
        gate, up, down = wdram[prefix + "_gate"], wdram[prefix + "_up"], wdram[prefix + "_down"]
        blocks = []
        t = list(tiles)
        if len(t) % 2 == 1:
            blocks.append(t[:1])
            t = t[1:]
        for j in range(0, len(t), 2):
            blocks.append(t[j:j + 2])
        cnt = 0
        for gi, (f0, nf) in enumerate(FFN_GROUPS):
            wb = gi % 2
            w = nf * 128
            for c in range(8):
                P.dma("pool", [], [("Wg", wb)], Wg[wb][:, c, 0:w],
                      gate.ap()[c * 128:(c + 1) * 128, f0 * 128:f0 * 128 + w])
            for c in range(8):
                P.dma("pool", [], [("Wu", wb)], Wu[wb][:, c, 0:w],
                      up.ap()[c * 128:(c + 1) * 128, f0 * 128:f0 * 128 + w])
            for c in range(nf):
                P.dma("pool", [], [("Wd", wb)], Wd[wb][:, c, :],
                      down.ap()[(f0 + c) * 128:(f0 + c + 1) * 128, :])
            for blk in blocks:
                T = 128 * len(blk)
                tok0 = blk[0] * 128
                htr = [("HT", t_) for t_ in blk]
                for fc in range(nf):
                    k = cnt
                    cnt += 1
                    gb = 4 + (k % 2)
                    ub = 6 + (k % 2)
                    for kc in range(8):
                        P.op("pe", "matmul", [("Wg", wb)] + htr, [("ps", gb)],
                             psf(gb, 0, T), lhsT=Wg[wb][:, kc, fc * 128:(fc + 1) * 128],
                             rhs=HT[:, kc, tok0:tok0 + T], start=(kc == 0), stop=(kc == 7))
                    for kc in range(8):
                        P.op("pe", "matmul", [("Wu", wb)] + htr, [("ps", ub)],
                             psf(ub, 0, T), lhsT=Wu[wb][:, kc, fc * 128:(fc + 1) * 128],
                             rhs=HT[:, kc, tok0:tok0 + T], start=(kc == 0), stop=(kc == 7))
                    s_ = k % 2
                    a_ = k % 3
                    P.op("act", "activation", [("ps", gb)], [("sg", s_)],
                         out=sg[s_][:, 0:T], in_=psf(gb, 0, T), func=AF.Silu)
                    P.op("dve", "tensor_tensor", [("sg", s_), ("ps", ub)], [("aT", a_)],
                         out=aT[a_][:, 0:T], in0=sg[s_][:, 0:T], in1=psf(ub, 0, T), op=ALU.mult)
                    for ti, tt in enumerate(blk):
                        for dh in range(2):
                            bank = 2 * ti + dh
                            P.op("pe", "matmul", [("aT", a_), ("Wd", wb)], [("ps", bank)],
                                 psf(bank), lhsT=aT[a_][:, ti * 128:(ti + 1) * 128],
                                 rhs=Wd[wb][:, fc, dh * 512:(dh + 1) * 512],
                                 start=(fc == 0), stop=(fc == nf - 1))
                for ti, tt in enumerate(blk):
                    for dh in range(2):
                        bank = 2 * ti + dh
                        P.op("dve", "scalar_tensor_tensor", [("ps", bank), ("X", tt)], [("X", tt)],
                             out=X[:, tt, dh * 512:(dh + 1) * 512], in0=psf(bank), scalar=0.5,
                             in1=X[:, tt, dh * 512:(dh + 1) * 512], op0=ALU.mult, op1=ALU.add)

    load_gain("norm_ffn1")
    for i in range(NT):
        P.dma("sp", [], [("X", i)], X[:, i, :], xs.ap()[i * 128:(i + 1) * 128, :])
    if not stage.startswith(("noffn", "mix")):
        for i in range(NT):
            rmsnorm_tile(i, norm_to_HT)
        ffn("ffn1", range(NT))
        barrier()

    w_in, w_out = wdram["w_in"], wdram["w_out"]
    if not stage.startswith("noffn"):
        load_gain("norm_mix")
        for i in range(NT):
            rmsnorm_tile(i, norm_to_HT)
        A.off = phase_mark
        WT = [sb(f"WT{i}", [128, 8, 128], BF16) for i in range(3)]
        wt_ctr = [0]

        def load_wtile(colspec):
            i = wt_ctr[0] % 3
            wt_ctr[0] += 1
            for kc in range(8):
                off = 0
                for (c0, n) in colspec:
                    P.dma("pool", [], [("WT", i)], WT[i][:, kc, off:off + n],
                          w_in.ap()[kc * 128:(kc + 1) * 128, c0:c0 + n])
                    off += n
            return i

        def proj(wi, tok0, T, bank, off=0):
            for kc in range(8):
                P.op("pe", "matmul", [("WT", wi)] + [("HT", t_) for t_ in range(tok0 // 128, (tok0 + T + 127) // 128)],
                     [("ps", bank)], psf(bank, off, off + T), lhsT=WT[wi][:, kc, :],
                     rhs=HT[:, kc, tok0:tok0 + T], start=(kc == 0), stop=(kc == 7))

        WO = [sb(f"WO{i}", [128, D], BF16) for i in range(2)]
        wo_ctr = [0]

        def load_wout(row0):
            i = wo_ctr[0] % 2
            wo_ctr[0] += 1
            P.dma("pool", [], [("WO", i)], WO[i][:], w_out.ap()[row0:row0 + 128, :])
            return i

        def wout_accum(srcs):
            for n in range(1, NT):
                for dh in range(2):
                    bank = 2 * (n % 2) + dh
                    for j, (fn, wi, res) in enumerate(srcs):
                        P.op("pe", "matmul", res(n) + [("WO", wi)], [("ps", bank)], psf(bank),
                             lhsT=fn(n), rhs=WO[wi][:, dh * 512:(dh + 1) * 512],
                             start=(j == 0), stop=(j == len(srcs) - 1))
                    P.op("dve", "tensor_tensor", [("ps", bank), ("X", n)], [("X", n)],
                         out=X[:, n, dh * 512:(dh + 1) * 512], in0=psf(bank),
                         in1=X[:, n, dh * 512:(dh + 1) * 512], op=ALU.add)

        mark_mix = A.off
        do_attn = 'noattn' not in stage
        do_rwkv = 'norwkv' not in stage
        ACOL = 1792
        battn = sb("battn", [128, 8])
        for c in range(4):
            P.dma("sp", [], ["battn"], battn[:, c:c + 1], b_attn_d.ap()[c * 128:(c + 1) * 128, :])
        for g in range(2):
            for hf in range(2):
                P.dma("sp", [], ["battn"], battn[hf * 64:(hf + 1) * 64, 4 + g:5 + g],
                      b_attn_d.ap()[512 + g * 64:512 + (g + 1) * 64, :])
        P.dma("sp", [], ["battn"], battn[:, 6:7], b_attn_d.ap()[640:768, :])
        sinks = sb("sinks", [128, 8])
        P.dma("sp", [], ["sinks"], sinks[:], sinks_d.ap().partition_broadcast(128))
        maskA = sb("maskA", [128, 256])
        mask1 = sb("mask1", [128, 256])
        P.dma("sp", [], ["maskA"], maskA[:], maskA_d.ap())
        P.dma("sp", [], ["mask1"], mask1[:], mask1_d.ap())
        mark("a1")
        VT = sb("VT", [128, NTH], BF16)
        Vtok = sb("Vtok", [128, NT, 128], BF16)
        Vpad = [sb(f"Vpad{i}", [128, NT, 128], BF16) for i in range(2)]
        Qg = sb("Qg", [128, 2, NTOK], BF16)
        KP = [sb(f"KP{i}", [128, NTH], BF16) for i in range(2)]
        for par in range(2):
            P.op("dve", "memset", [], [("KP", par)], KP[par][:], 0.0)
        AO = sb("AO", [128, 2, NTOK], BF16)
        sm = [sb(f"sm{i}", [128, 4, 256]) for i in range(2)]
        Pb = [sb(f"Pb{i}", [128, 4, 256], BF16) for i in range(2)]
        PT = [sb(f"PT{i}", [128, 8, 128], BF16) for i in range(2)]
        st = [sb(f"st{i}", [128, 16]) for i in range(2)]

        tok_blocks = [(0, 128)] + [(128 + 512 * j, 512) for j in range(4)]

        def proj_to(colspec, dst_fn, bias_ap, res, blocks):
            wi = load_wtile(colspec)
            for bi, (tok0, T) in enumerate(blocks):
                bank = 4 + (bi % 4)
                proj(wi, tok0, T, bank)
                P.op("act", "activation", [("ps", bank), "battn"], [res],
                     out=dst_fn(tok0, T), in_=psf(bank, 0, T), func=AF.Identity, bias=bias_ap, scale=1.0)

        proj_to([(ACOL + 640, 128)], lambda t0, T: VT[:, t0:t0 + T], battn[:, 6:7], "VT", tok_blocks)
        mark("a2")
        for n in range(NT):
            bank = 4 + (n % 4)
            P.op("pe", "transpose", ["VT", "ident_b"], [("ps", bank)], out=psb(bank, 0, 128),
                 in_=VT[:, n * 128:(n + 1) * 128], identity=ident_b[:])
            P.op("act", "copy", [("ps", bank)], [("Vtok", n)], out=Vtok[:, n, :], in_=psb(bank, 0, 128))
        mark("a3")
        for g in range(2 if do_attn else 0):
            for par in range(2):
                P.op("dve", "memset", [], [("Vpad", par)], Vpad[par][:], 0.0)
                P.op("dve", "tensor_copy", [("Vtok", n) for n in range(NT)], [("Vpad", par)],
                     out=Vpad[par][:, :, par * 64:(par + 1) * 64], in_=Vtok[:, :, g * 64:(g + 1) * 64])
            for lc in range(2):
                proj_to([(ACOL + (2 * g + lc) * 128, 128)],
                        lambda t0, T, lc=lc: Qg[:, lc, t0 - 128:t0 - 128 + T], battn[:, 2 * g + lc:2 * g + lc + 1],
                        ("Qg", lc), tok_blocks[1:])
            wi_k = load_wtile([(ACOL + 512 + g * 64, 64), (ACOL + 512 + g * 64, 64)])
            for bi, (tok0, T_) in enumerate(tok_blocks):
                bank = 4 + (bi % 4)
                proj(wi_k, tok0, T_, bank)
                for par in range(2):
                    ps_ = slice(par * 64, (par + 1) * 64)
                    P.op("act", "activation", [("ps", bank), "battn"], [("KP", par)],
                         out=KP[par][ps_, tok0:tok0 + T_], in_=psf(bank, 0, T_)[ps_], func=AF.Identity,
                         bias=battn[ps_, 4 + g:5 + g], scale=1.0)
            mark("a4")
            for n in range(1, NT):
                if n == 2:
                    mark("a5")
                u = n % 2
                q0 = (n - 1) * 128
                sb0 = 4 + 2 * u
                for j in range(4):
                    lc, par = j // 2, j % 2
                    bank = sb0 + j // 2
                    P.op("pe", "matmul", [("Qg", lc), ("KP", par)], [("ps", bank)],
                         psf(bank, (j % 2) * 256, (j % 2) * 256 + 256),
                         lhsT=Qg[:, lc, q0:q0 + 128],
                         rhs=KP[par][:, (n - 1) * 128:(n + 1) * 128], start=True, stop=True)
                mark("b1")
                mk = mask1 if n == 1 else maskA
                mkn = "mask1" if n == 1 else "maskA"
                for hb_ in range(2):
                    P.op("dve", "scalar_tensor_tensor", [("ps", sb0 + hb_), mkn], [("sm", u)],
                         out=sm[u][:, 2 * hb_:2 * hb_ + 2, :],
                         in0=psf(sb0 + hb_).rearrange("p (h k) -> p h k", h=2),
                         scalar=0.125, in1=mk[:].unsqueeze(1).to_broadcast([128, 2, 256]),
                         op0=ALU.mult, op1=ALU.add)
                mark("b2")
                mx, dd, es_, den = st[u][:, 0:4], st[u][:, 4:8], st[u][:, 8:12], st[u][:, 12:16]
                P.op("dve", "tensor_reduce", [("sm", u)], [("st", u)], out=mx, in_=sm[u][:], axis=AX.X, op=ALU.max)
                P.op("dve", "tensor_tensor", [("st", u), "sinks"], [("st", u)], out=mx, in0=mx,
                     in1=sinks[:, 4 * g:4 * g + 4], op=ALU.max)
                P.op("dve", "tensor_tensor", [("st", u), "sinks"], [("st", u)], out=dd,
                     in0=sinks[:, 4 * g:4 * g + 4], in1=mx, op=ALU.subtract)
                mark("b3")
                P.op("dve", "tensor_tensor", [("sm", u), ("st", u)], [("sm", u)], out=sm[u][:], in0=sm[u][:],
                     in1=mx.unsqueeze(2).to_broadcast([128, 4, 256]), op=ALU.subtract)
                mark("b4")
                P.op("act", "activation", [("sm", u)], [("sm", u)], out=sm[u][:], in_=sm[u][:], func=AF.Exp)
                P.op("act", "activation", [("st", u)], [("st", u)], out=es_, in_=dd, func=AF.Exp)
                mark("b5")
                P.op("dve", "tensor_reduce", [("sm", u)], [("st", u)], out=den, in_=sm[u][:], axis=AX.X, op=ALU.add)
                P.op("dve", "tensor_tensor", [("st", u)], [("st", u)], out=den, in0=den, in1=es_, op=ALU.add)
                P.op("dve", "reciprocal", [("st", u)], [("st", u)], out=den, in_=den)
                P.op("dve", "tensor_tensor", [("sm", u), ("st", u)], [("Pb", u)], out=Pb[u][:], in0=sm[u][:],
                     in1=den.unsqueeze(2).to_broadcast([128, 4, 256]), op=ALU.mult)
                mark("b6")
                tb = sb0
                for j in range(4):
                    for kh in range(2):
                        P.op("pe", "transpose", [("Pb", u), "ident_b"], [("ps", tb)],
                             out=psb(tb, (j * 2 + kh) * 128, (j * 2 + kh + 1) * 128),
                             in_=Pb[u][:, j, kh * 128:(kh + 1) * 128], identity=ident_b[:])
                P.op("act", "copy", [("ps", tb)], [("PT", u)], out=PT[u][:],
                     in_=psb(tb).rearrange("p (c t) -> p c t", c=8))
                mark("b7")
                ab = sb0 + 1
                for lc in range(2):
                    idx = 0
                    for par in range(2):
                        j = lc * 2 + par
                        for kh in range(2):
                            P.op("pe", "matmul", [("Vpad", par), ("PT", u)], [("ps", ab)],
                                 psf(ab, lc * 128, (lc + 1) * 128),
                                 lhsT=Vpad[par][:, n - 1 + kh, :], rhs=PT[u][:, j * 2 + kh, :],
                                 start=(idx == 0), stop=(idx == 3))
                            idx += 1
                mark("b8")
                P.op("act", "copy", [("ps", ab)], [("AO", n)], out=AO[:, :, q0:q0 + 128],
                     in_=psf(ab, 0, 256).rearrange("p (c t) -> p c t", c=2))
            mark("a6")
            wis = [load_wout(512 + (2 * g + lc) * 128) for lc in range(2)]
            wout_accum([(lambda n, lc=lc: AO[:, lc, (n - 1) * 128:n * 128], wis[lc], lambda n: [("AO", n)]) for lc in range(2)])
        barrier()
        A.off = mark_mix
        C0 = float(np.exp(-0.5))
        pv = sb("pv", [128, 48])
        SMX, W0, A0, KK_, KA_, RK_, LNW, LNB = 0, 14, 18, 22, 26, 30, 34, 38
        for c in range(14):
            P.dma("sp", [], ["pv"], pv[:, SMX + c:SMX + c + 1], rw_d["rwkv_shift_mix"].ap()[c * 128:(c + 1) * 128, :])
        for nm, col in (("rwkv_w0", W0), ("rwkv_a0", A0), ("rwkv_k_k", KK_), ("rwkv_k_a", KA_),
                        ("rwkv_r_k", RK_), ("rwkv_ln_w", LNW), ("rwkv_ln_b", LNB)):
            for j in range(4):
                P.dma("sp", [], ["pv"], pv[:, col + j:col + j + 1], rw_d[nm].ap()[j * 128:(j + 1) * 128, :])
        W2P = sb("W2P", [128, 512], BF16)
        A2P = sb("A2P", [128, 512], BF16)
        G2 = sb("G2", [128, 512], BF16)
        P.op("dve", "memset", [], ["W2A2"], W2P[:], 0.0)
        P.op("dve", "memset", [], ["W2A2"], A2P[:], 0.0)
        P.dma("pool", [], ["W2A2"], W2P[0:64, :], rw_d["rwkv_w2"].ap())
        P.dma("pool", [], ["W2A2"], A2P[64:128, :], rw_d["rwkv_a2"].ap())
        P.dma("pool", [], ["G2"], G2[:], rw_d["rwkv_g2"].ap())
        MK = sb("MK", [128, 384])
        P.dma("sp", [], ["MK"], MK[:], cmask_d.ap())
        bones_f = sb("bones_f", [128, 128])
        bones_b = sb("bones_b", [128, 128], BF16)
        bones_s = sb("bones_s", [128, 128])
        P.dma("sp", [], ["bones_f"], bones_f[:], bones_d.ap())
        P.op("act", "copy", ["bones_f"], ["bones_b"], out=bones_b[:], in_=bones_f[:])
        P.op("act", "mul", ["bones_f"], ["bones_s"], out=bones_s[:], in_=bones_f[:], mul=1.0 / 64)
        scanm = sb("scanm", [128, 512])
        P.dma("sp", [], ["scanm"], scanm[:], scanm_d.ap().partition_broadcast(128))
        fm = sb("fm", [128, 8])
        P.dma("sp", [], ["fm"], fm[:], fm_d.ap().partition_broadcast(128))
        gneps = sb("gneps", [128, 1])
        P.op("dve", "memset", [], ["gneps"], gneps[:], 64e-5)
        WA = sb("WA", [128, NTOK], BF16)
        GL = sb("GL", [128, NTOK], BF16)
        pf = [sb(f"pf{i}", [128, 513]) for i in range(2)]
        ga_off = A.off
        T = [sb(f"T{i}", [128, 512]) for i in range(10)]
        tb16 = [sb(f"tb16_{i}", [128, 512], BF16) for i in range(2)]
        A_save = A.off
        A.off = ga_off
        GA = sb("GA", [128, 8, 256])
        A.off = A_save
        GA_RES = [("T", i) for i in range(4)]
        AR = sb("AR", [128, 8, 256], BF16)
        EXP = {nm: sb(nm, [128, 8, 128], BF16) for nm in ("BT", "KT", "BH", "KH", "VE")}
        P.op("dve", "memset", [], ["AR"], AR[:], 0.0)
        for nm in EXP:
            P.op("dve", "memset", [], [nm], EXP[nm][:], 0.0)
        NB = [sb(f"NB{i}", [128, 384], BF16) for i in range(2)]
        KB = [sb(f"KB{i}", [128, 256], BF16) for i in range(2)]
        Mb = [sb(f"Mb{i}", [128, 128], BF16) for i in range(2)]
        TK = [sb(f"TK{i}", [128, 256], BF16) for i in range(2)]
        Zt = [sb(f"Z{i}", [128, 256], BF16) for i in range(3)]
        NM = [sb(f"NM{i}", [128, 256], BF16) for i in range(3)]
        Qt = [sb(f"Qt{i}", [128, 128], BF16) for i in range(2)]
        GT = [sb(f"GT{i}", [128, 128], BF16) for i in range(2)]
        Sb = [sb(f"Sb{i}", [128, 256], BF16) for i in range(2)]
        SF = sb("SF", [128, 256])
        OL = sb("OL", [128, NTOK])
        QH = sb("QH", [128, NTOK], BF16)
        BON = sb("BON", [128, NTOK], BF16)
        MX = nc.alloc_sbuf_tensor_at("MX_alias", [128, NTOK], BF16, offset=SB_BASE)
        Sst = sb("Sst", [128, 128])
        Sstb = sb("Sstb", [128, 128], BF16)
        Pti = sb("Pti", [128, 128])
        dtmp = sb("dtmp", [128, 128])
        bank_ctr = [0]

        def nb():
            bank_ctr[0] = (bank_ctr[0] + 1) % 8
            return bank_ctr[0]

        def v3(ap):
            return ap.rearrange("p (c t) -> p c t", c=8)

        pf_ctr = [0]

        def proj_shift(wi, smx_col, tb, dst, dst_res):
            u = pf_ctr[0] % 2
            pf_ctr[0] += 1
            tok0 = 128 + tb * 512
            bank = nb()
            proj(wi, tok0, 512, bank)
            P.op("act", "copy", [("ps", bank)], [("pf", u)], out=pf[u][:, 1:513], in_=psf(bank))
            bank2 = nb()
            for kc in range(8):
                P.op("pe", "matmul", [("WT", wi), ("HT", tok0 // 128 - 1)], [("ps", bank2)], psf(bank2, 0, 1),
                     lhsT=WT[wi][:, kc, :], rhs=HT[:, kc, tok0 - 1:tok0], start=(kc == 0), stop=(kc == 7))
            P.op("act", "copy", [("ps", bank2)], [("pf", u)], out=pf[u][:, 0:1], in_=psf(bank2, 0, 1))
            P.op("dve", "tensor_tensor", [("pf", u)], [dst_res], out=dst, in0=pf[u][:, 0:512], in1=pf[u][:, 1:513],
                 op=ALU.subtract)
            P.op("dve", "scalar_tensor_tensor", [("pf", u), dst_res, "pv"], [dst_res], out=dst, in0=dst,
                 scalar=pv[:, SMX + smx_col:SMX + smx_col + 1], in1=pf[u][:, 1:513], op0=ALU.mult, op1=ALU.add)

        mark("r1")
        wl1 = load_wtile([(1536, 128)])
        wl2 = load_wtile([(1664, 128)])
        for tb in range(4):
            cs_ = slice(tb * 512, (tb + 1) * 512)
            proj_shift(wl1, 12, tb, T[0][:], ("T", 0))
            P.op("act", "activation", [("T", 0)], [("WA", tb)], out=WA[0:64, cs_], in_=T[0][0:64, :], func=AF.Tanh)
            P.op("act", "copy", [("T", 0)], [("WA", tb)], out=WA[64:128, cs_], in_=T[0][64:128, :])
            proj_shift(wl2, 13, tb, T[1][:], ("T", 1))
            P.op("act", "activation", [("T", 1)], [("GL", tb)], out=GL[:, cs_], in_=T[1][:], func=AF.Sigmoid)

        mark("r2")
        def ew(q, name, reads, writes, **kw):
            P.op(q, name, reads, writes, **kw)

        chunk_ctr = [0]
        for j in range(4 if do_rwkv else 0):
            wr_ = load_wtile([(j * 128, 128)])
            wk_ = load_wtile([(512 + j * 128, 128)])
            wv_ = load_wtile([(1024 + j * 128, 128)])
            for tb in range(4):
                cs_ = slice(tb * 512, (tb + 1) * 512)
                r_s, k_s, v_s, sgw, cs, a_t, kkn, kp, Ep, tmp = [T[i][:] for i in range(10)]
                R_ = lambda i: ("T", i)
                proj_shift(wr_, j, tb, r_s, R_(0))
                proj_shift(wk_, 4 + j, tb, k_s, R_(1))
                proj_shift(wv_, 8 + j, tb, v_s, R_(2))
                bw = nb()
                P.op("pe", "matmul", ["W2A2", ("WA", tb)], [("ps", bw)], psf(bw), lhsT=W2P[:, j * 128:(j + 1) * 128],
                     rhs=WA[:, cs_], start=True, stop=True)
                P.op("act", "activation", [("ps", bw), "pv"], [R_(3)], out=sgw, in_=psf(bw), func=AF.Sigmoid,
                     bias=pv[:, W0 + j:W0 + j + 1], scale=1.0)
                ba = nb()
                P.op("pe", "matmul", ["W2A2", ("WA", tb)], [("ps", ba)], psf(ba), lhsT=A2P[:, j * 128:(j + 1) * 128],
                     rhs=WA[:, cs_], start=True, stop=True)
                P.op("act", "activation", [("ps", ba), "pv"], [R_(5)], out=a_t, in_=psf(ba), func=AF.Sigmoid,
                     bias=pv[:, A0 + j:A0 + j + 1], scale=1.0)
                P.op("dve", "tensor_scalar", [R_(1), "pv"], [R_(6)], out=kkn, in0=k_s, scalar1=pv[:, KK_ + j:KK_ + j + 1],
                     scalar2=None, op0=ALU.mult)
                P.op("act", "activation", [R_(6)], [("tb16", 0)], out=tb16[0][:], in_=kkn, func=AF.Square)
                bq = nb()
                P.op("pe", "matmul", ["bones_b", ("tb16", 0)], [("ps", bq)], psf(bq), lhsT=bones_b[:], rhs=tb16[0][:],
                     start=True, stop=True)
                P.op("dve", "tensor_scalar", [("ps", bq)], [R_(9)], out=tmp, in0=psf(bq), scalar1=1e-24, scalar2=None,
                     op0=ALU.max)
                P.op("act", "activation", [R_(9)], [R_(9)], out=tmp, in_=tmp, func=AF.Sqrt)
                P.op("dve", "reciprocal", [R_(9)], [R_(9)], out=tmp, in_=tmp)
                P.op("dve", "tensor_tensor", [R_(6), R_(9)], [R_(6)], out=kkn, in0=kkn, in1=tmp, op=ALU.mult)
                P.op("dve", "tensor_scalar", [R_(5), "pv"], [R_(7)], out=kp, in0=a_t, scalar1=-1.0,
                     scalar2=pv[:, KA_ + j:KA_ + j + 1], op0=ALU.add, op1=ALU.mult)
                P.op("dve", "scalar_tensor_tensor", [R_(7), R_(1)], [R_(7)], out=kp, in0=kp, scalar=1.0, in1=k_s,
                     op0=ALU.add, op1=ALU.mult)
                b_t = k_s
                P.op("dve", "tensor_tensor", [R_(6), R_(5)], [R_(1)], out=b_t, in0=kkn, in1=a_t, op=ALU.mult)
                P.op("dve", "scalar_tensor_tensor", [R_(0), R_(7), "pv"], [("tb16", 1)], out=tb16[1][:], in0=r_s,
                     scalar=pv[:, RK_ + j:RK_ + j + 1], in1=kp, op0=ALU.mult, op1=ALU.mult)
                bb = nb()
                P.op("pe", "matmul", ["bones_b", ("tb16", 1)], [("ps", bb)], psf(bb), lhsT=bones_b[:], rhs=tb16[1][:],
                     start=True, stop=True)
                P.op("dve", "tensor_tensor", [("ps", bb), R_(2)], [("BON", tb)], out=BON[:, cs_], in0=psf(bb), in1=v_s,
                     op=ALU.mult)
                P.op("dve", "tensor_tensor_scan", [R_(3), "scanm"], [R_(4)], out=cs, data0=scanm[:], data1=sgw, initial=0.0,
                     op0=ALU.mult, op1=ALU.add)
                P.op("dve", "tensor_tensor", [R_(4), R_(3)], [R_(3)], out=sgw, in0=cs, in1=sgw, op=ALU.subtract)
                P.op("act", "activation", [R_(3)], [R_(3)], out=sgw, in_=sgw, func=AF.Exp, scale=-C0)
                Eex = sgw
                P.op("act", "activation", [R_(4)], [R_(8)], out=Ep, in_=cs, func=AF.Exp, scale=-C0)
                P.op("act", "activation", [R_(4)], [R_(4)], out=cs, in_=cs, func=AF.Exp, scale=C0)
                Em = cs
                Eh = a_t
                P.op("dve", "tensor_tensor", [R_(4), R_(8)], [R_(5)], out=v3(Eh), in0=v3(Em),
                     in1=v3(Ep)[:, :, 63:64].to_broadcast([128, 8, 64]), op=ALU.mult)
                mark("r3")
                for hf in range(2):
                    ps_ = slice(hf * 64, (hf + 1) * 64)
                    cA = slice(hf * 64, (hf + 1) * 64)
                    cR = slice(128 + hf * 64, 128 + (hf + 1) * 64)
                    q = "dve" if hf == 0 else "pool"
                    ew("dve", "scalar_tensor_tensor", [R_(6), R_(3)], ["AR"], out=AR[ps_, :, cA], in0=v3(kkn)[ps_], scalar=-1.0,
                       in1=v3(Eex)[ps_], op0=ALU.mult, op1=ALU.mult)
                    ew(q, "tensor_tensor", [R_(0), R_(8)], ["AR"], out=AR[ps_, :, cR], in0=v3(r_s)[ps_], in1=v3(Ep)[ps_],
                       op=ALU.mult)
                    ew(q, "tensor_tensor", [R_(1), R_(4)], ["BT"], out=EXP["BT"][ps_, :, cA], in0=v3(b_t)[ps_],
                       in1=v3(Em)[ps_], op=ALU.mult)
                    ew(q, "tensor_tensor", [R_(7), R_(4)], ["KT"], out=EXP["KT"][ps_, :, cA], in0=v3(kp)[ps_],
                       in1=v3(Em)[ps_], op=ALU.mult)
                    ew(q, "tensor_tensor", [R_(1), R_(5)], ["BH"], out=EXP["BH"][ps_, :, cA], in0=v3(b_t)[ps_],
                       in1=v3(Eh)[ps_], op=ALU.mult)
                    ew(q, "tensor_tensor", [R_(7), R_(5)], ["KH"], out=EXP["KH"][ps_, :, cA], in0=v3(kp)[ps_],
                       in1=v3(Eh)[ps_], op=ALU.mult)
                    ew(q, "tensor_copy", [R_(2)], ["VE"], out=EXP["VE"][ps_, :, cA], in_=v3(v_s)[ps_])
                mark("r4")
                for c in range(8):
                    kc_ = chunk_ctr[0]
                    chunk_ctr[0] += 1
                    u = kc_ % 2
                    first = (tb == 0 and c == 0)
                    last = (tb == 3 and c == 7)
                    tcol = tb * 512 + c * 64
                    ARc = AR[:, c, :]
                    BTc, KTc, BHc, KHc, VEc = [EXP[nm][:, c, :] for nm in ("BT", "KT", "BH", "KH", "VE")]
                    if first:
                        P.op("dve", "memset", [], [("Sb", 1 - u)], Sb[1 - u][:, 0:128], 0.0)
                        P.op("dve", "tensor_copy", ["ident_b"], [("Sb", 1 - u)], out=Sb[1 - u][:, 128:256], in_=ident_b[:])
                    Sprev, Snew = Sb[1 - u], Sb[u]
                    b1, b2, b3 = nb(), nb(), nb()
                    P.op("pe", "matmul", ["BT", "AR"], [("ps", b1)], psf(b1, 0, 256), lhsT=BTc, rhs=ARc, start=True, stop=True)
                    P.op("pe", "matmul", ["KT", "AR"], [("ps", b2)], psf(b2, 0, 256), lhsT=KTc, rhs=ARc, start=True, stop=True)
                    P.op("pe", "matmul", ["AR", "BT"], [("ps", b3)], psf(b3, 0, 128), lhsT=ARc[:, 0:128], rhs=BTc,
                         start=True, stop=True)
                    P.op("dve", "tensor_tensor", [("ps", b1), "MK"], [("NB", u)], out=NB[u][:, 0:256], in0=psf(b1, 0, 256),
                         in1=MK[:, 0:256], op=ALU.mult)
                    P.op("dve", "tensor_tensor", [("ps", b2), "MK"], [("KB", u)], out=KB[u][:], in0=psf(b2, 0, 256),
                         in1=MK[:, 0:256], op=ALU.mult)
                    P.op("dve", "tensor_tensor", [("ps", b3), "MK"], [("Mb", u)], out=Mb[u][:], in0=psf(b3, 0, 128),
                         in1=MK[:, 256:384], op=ALU.mult)
                    mark("c1")
                    b4 = nb()
                    for ti, src in enumerate((VEc, KHc, ARc[:, 0:128], BHc)):
                        P.op("pe", "transpose", ["VE", "KH", "AR", "BH", "ident_b"], [("ps", b4)],
                             out=psb(b4, ti * 128, (ti + 1) * 128), in_=src, identity=ident_b[:])
                    z0 = kc_ % 3
                    P.op("act", "copy", [("ps", b4)], [("TK", u)], out=TK[u][:], in_=psb(b4, 0, 256))
                    P.op("act", "copy", [("ps", b4)], [("Z", z0)], out=Zt[z0][:, 128:256], in_=psb(b4, 256, 384))
                    P.op("act", "copy", [("ps", b4)], [("NB", u)], out=NB[u][:, 256:384], in_=psb(b4, 384, 512))
                    Vtok_, KHtok = TK[u][:, 0:128], TK[u][:, 128:256]
                    b5 = nb()
                    P.op("pe", "matmul", [("KB", u), ("TK", u)], [("ps", b5)], psf(b5, 0, 128), lhsT=KB[u][:, 0:128],
                         rhs=Vtok_, start=True, stop=True)
                    P.op("act", "copy", [("ps", b5)], [("Z", z0)], out=Zt[z0][:, 0:128], in_=psf(b5, 0, 128))
                    mark("c2")
                    Ncur, Mcur, nres = NB[u][:, 0:128], Mb[u][:], [("NB", u), ("Mb", u)]
                    zc = z0
                    nmi = (2 * kc_) % 3
                    for lvl in range(6):
                        bz = nb()
                        P.op("pe", "matmul", nres + [("Z", zc)], [("ps", bz)], psf(bz, 0, 256), lhsT=Ncur, rhs=Zt[zc][:],
                             start=True, stop=True)
                        zn = (zc + 1) % 3
                        P.op("dve", "tensor_tensor", [("ps", bz), ("Z", zc)], [("Z", zn)], out=Zt[zn][:], in0=psf(bz, 0, 256),
                             in1=Zt[zc][:], op=ALU.add)
                        zc = zn
                        if lvl < 5:
                            bs = nb()
                            P.op("pe", "matmul", nres, [("ps", bs)], psf(bs, 0, 128), lhsT=Mcur, rhs=Ncur, start=True, stop=True)
                            P.op("pe", "matmul", nres, [("ps", bs)], psf(bs, 128, 256), lhsT=Ncur, rhs=Mcur, start=True, stop=True)
                            nmi = (nmi + 1) % 3
                            P.op("act", "copy", [("ps", bs)], [("NM", nmi)], out=NM[nmi][:], in_=psf(bs, 0, 256))
                            Ncur, Mcur, nres = NM[nmi][:, 0:128], NM[nmi][:, 128:256], [("NM", nmi)]
                    mark("c3")
                    Zf = Zt[zc]
                    Uloc, TA = Zf[:, 0:128], Zf[:, 128:256]
                    bqg = nb()
                    P.op("pe", "matmul", [("Z", zc), ("NB", u)], [("ps", bqg)], psf(bqg, 0, 256), lhsT=TA, rhs=NB[u][:, 128:384],
                         start=True, stop=True)
                    P.op("dve", "tensor_tensor", [("ps", bqg), "AR"], [("Qt", u)], out=Qt[u][:], in0=psf(bqg, 0, 128),
                         in1=ARc[:, 128:256], op=ALU.add)
                    P.op("dve", "scalar_tensor_tensor", [("ps", bqg), "ident_f", R_(8)], [("GT", u)], out=GT[u][:], in0=ident_f[:],
                         scalar=Ep[:, c * 64 + 63:c * 64 + 64], in1=psf(bqg, 128, 256), op0=ALU.mult, op1=ALU.add)
                    mark("c4")
                    bo = nb()
                    P.op("pe", "matmul", [("Z", zc), ("NB", u)], [("ps", bo)], psf(bo, 0, 128), lhsT=Uloc, rhs=NB[u][:, 128:256],
                         start=True, stop=False)
                    P.op("pe", "matmul", [("TK", u), ("KB", u)], [("ps", bo)], psf(bo, 0, 128), lhsT=Vtok_, rhs=KB[u][:, 128:256],
                         start=False, stop=False)
                    P.op("pe", "matmul", [("Sb", 1 - u), ("Qt", u)], [("ps", bo)], psf(bo, 0, 128), lhsT=Sprev[:, 0:128],
                         rhs=Qt[u][:], start=False, stop=True)
                    bh_ = nb()
                    P.op("pe", "matmul", [("Sb", 1 - u), ("Qt", u)], [("ps", bh_)], psf(bh_, 0, 128), lhsT=Sprev[:, 128:256],
                         rhs=Qt[u][:], start=True, stop=True)
                    for hf in range(2):
                        ps_ = slice(hf * 64, (hf + 1) * 64)
                        P.op("act", "copy", [("ps", bo)], [("OL", tb)], out=OL[ps_, tcol:tcol + 64],
                             in_=psf(bo, hf * 64, (hf + 1) * 64)[ps_])
                        P.op("act", "copy", [("ps", bh_)], [("QH", tb)], out=QH[ps_, tcol:tcol + 64],
                             in_=psf(bh_, hf * 64, (hf + 1) * 64)[ps_])
                    mark("c5")
                    bsn = nb()
                    P.op("pe", "matmul", [("GT", u), ("Sb", 1 - u)], [("ps", bsn)], psf(bsn, 0, 128), lhsT=GT[u][:],
                         rhs=Sprev[:, 0:128], start=True, stop=False)
                    P.op("pe", "matmul", [("NB", u), ("Z", zc)], [("ps", bsn)], psf(bsn, 0, 128), lhsT=NB[u][:, 256:384],
                         rhs=Uloc, start=False, stop=False)
                    P.op("pe", "matmul", [("TK", u)], [("ps", bsn)], psf(bsn, 0, 128), lhsT=KHtok, rhs=Vtok_,
                         start=False, stop=True)
                    P.op("pe", "matmul", [("GT", u), ("Sb", 1 - u)], [("ps", bsn)], psf(bsn, 128, 256), lhsT=GT[u][:],
                         rhs=Sprev[:, 128:256], start=True, stop=True)
                    P.op("act", "copy", [("ps", bsn)], [("Sb", u)], out=Snew[:], in_=psf(bsn, 0, 256))
                    if last and "nosf" not in stage:
                        P.op("dve", "tensor_copy", [("ps", bsn)], ["SF"], out=SF[:], in_=psf(bsn, 0, 256))
                    mark(f"k{kc_}")
            mark("r5")
            if use_cc:
                P.dma("sp", ["SF"], [("ccin", j)], cc_in[j].ap(), SF[:])
                P.coll("pool", [("ccin", j)], [("ccg", j)], "AllGather", ALU.bypass,
                       replica_groups=[list(range(NCORES))], ins=[cc_in[j].ap()], outs=[cc_out[j].ap()])
            else:
                P.dma("sp", ["SF"], [("sfdbg", j)], sf_dbg.ap()[:, j * 256:(j + 1) * 256], SF[:])
            P.dma("sp", [("ccg", j)], GA_RES, GA[:], ccg_view(j))
            P.op("dve", "memset", [], ["Sst"], Sst[:], 0.0)
            for i in range(7):
                bt_ = nb()
                P.op("pe", "transpose", GA_RES + ["ident_f"], [("ps", bt_)], out=psf(bt_, 0, 128), in_=GA[:, i, 128:256],
                     identity=ident_f[:])
                P.op("act", "copy", [("ps", bt_)], ["Pti"], out=Pti[:], in_=psf(bt_, 0, 128))
                bm_ = nb()
                P.op("pe", "matmul", ["Pti", "Sst"], [("ps", bm_)], psf(bm_, 0, 128), lhsT=Pti[:], rhs=Sst[:], start=True, stop=True)
                P.op("dve", "tensor_tensor", [("ps", bm_)] + GA_RES, ["dtmp"], out=dtmp[:], in0=psf(bm_, 0, 128),
                     in1=GA[:, i, 0:128], op=ALU.add)
                P.op("dve", "tensor_tensor", ["dtmp", "Sst"], ["dtmp"], out=dtmp[:], in0=dtmp[:], in1=Sst[:], op=ALU.subtract)
                P.op("dve", "scalar_tensor_tensor", ["dtmp", "Sst", "fm"], ["Sst"], out=Sst[:], in0=dtmp[:], scalar=fm[:, i:i + 1],
                     in1=Sst[:], op0=ALU.mult, op1=ALU.add)
            mark("r6")
            P.op("act", "copy", ["Sst"], ["Sstb"], out=Sstb[:], in_=Sst[:])
            for tb in range(4):
                cs_ = slice(tb * 512, (tb + 1) * 512)
                O_, cen, sq = T[4][:], T[5][:], T[6][:]
                bf_ = nb()
                P.op("pe", "matmul", ["Sstb", ("QH", tb)], [("ps", bf_)], psf(bf_), lhsT=Sstb[:], rhs=QH[:, cs_], start=True, stop=True)
                P.op("dve", "tensor_tensor", [("ps", bf_), ("OL", tb)], [("T", 4)], out=O_, in0=psf(bf_), in1=OL[:, cs_], op=ALU.add)
                bmu = nb()
                P.op("pe", "matmul", ["bones_s", ("T", 4)], [("ps", bmu)], psf(bmu), lhsT=bones_s[:], rhs=O_, start=True, stop=True)
                P.op("dve", "tensor_tensor", [("ps", bmu), ("T", 4)], [("T", 5)], out=cen, in0=O_, in1=psf(bmu), op=ALU.subtract)
                P.op("act", "activation", [("T", 5)], [("T", 6)], out=sq, in_=cen, func=AF.Square)
                bvar = nb()
                P.op("pe", "matmul", ["bones_s", ("T", 6)], [("ps", bvar)], psf(bvar), lhsT=bones_s[:], rhs=sq, start=True, stop=True)
                P.op("act", "activation", [("ps", bvar), "gneps"], [("T", 6)], out=sq, in_=psf(bvar), func=AF.Sqrt,
                     bias=gneps[:, 0:1], scale=1.0)
                P.op("dve", "reciprocal", [("T", 6)], [("T", 6)], out=sq, in_=sq)
                P.op("dve", "tensor_tensor", [("T", 5), ("T", 6)], [("T", 5)], out=cen, in0=cen, in1=sq, op=ALU.mult)
                P.op("dve", "tensor_scalar", [("T", 5), "pv"], [("T", 5)], out=cen, in0=cen, scalar1=pv[:, LNW + j:LNW + j + 1],
                     scalar2=pv[:, LNB + j:LNB + j + 1], op0=ALU.mult, op1=ALU.add)
                P.op("dve", "tensor_tensor", [("T", 5), ("BON", tb)], [("T", 5)], out=cen, in0=cen, in1=BON[:, cs_], op=ALU.add)
                bg_ = nb()
                P.op("pe", "matmul", ["G2", ("GL", tb)], [("ps", bg_)], psf(bg_), lhsT=G2[:, j * 128:(j + 1) * 128], rhs=GL[:, cs_],
                     start=True, stop=True)
                P.op("dve", "tensor_tensor", [("T", 5), ("ps", bg_)], [("MX", tb), ("X", 0)], out=MX[:, cs_], in0=cen, in1=psf(bg_), op=ALU.mult)
            mark("r7")
            wi_ = load_wout(j * 128)
            wout_accum([(lambda n: MX[:, (n - 1) * 128:n * 128], wi_, lambda n: [("MX", (n - 1) // 4)])])
        barrier()

    if not stage.startswith(("noffn", "mix")):
        load_gain("norm_ffn2")
        for i in range(1, NT):
            rmsnorm_tile(i, norm_to_HT)
        ffn("ffn2", range(1, NT))

    P.enabled = True
    barrier()
    A.off = phase_mark
    ob = [sb(f"ob{i}", [128, D]) for i in range(2)]
    load_gain("norm_final")

    def norm_to_out(i, col):
        b = i % 2
        P.op("dve", "scalar_tensor_tensor", [("X", i), ("rstd", col), "gbc"], [("ob", b)],
             out=ob[b][:], in0=X[:, i, :], scalar=rstd[:, col:col + 1],
             in1=gbc[:], op0=ALU.mult, op1=ALU.mult)
        P.dma("sp", [("ob", b)], [("out", i)], out.ap()[(i - 1) * 128:i * 128, :], ob[b][:])

    for i in range(1, NT):
        rmsnorm_tile(i, norm_to_out)
    P.wait_all("sp")

    P.emit()
    es.close()
    return nc


_NC_CACHE = {}


def _prep_inputs(inputs):
    x = np.ascontiguousarray(inputs["x"], dtype=np.float32)
    B, S, _ = x.shape
    shared = {"ident": np.eye(128, dtype=np.float32)}
    qi = np.arange(128)[:, None]
    kj = np.arange(256)[None, :]
    valid = np.where(kj < 128, kj > qi, (kj - 128) <= qi)
    maskA = np.where(valid, 0.0, NEG).astype(np.float32)
    mask0 = maskA.copy()
    mask0[:, :128] = NEG
    shared["maskA"] = maskA
    shared["w_in"] = np.ascontiguousarray(inputs["w_in"][0], dtype=np.float32)
    shared["w_out"] = np.ascontiguousarray(inputs["w_out"][0], dtype=np.float32)
    shared["b_in_attn"] = np.ascontiguousarray(inputs["b_in_attn"].reshape(768, 1), dtype=np.float32)
    idx = np.arange(128)
    hh, ii = idx // 64, idx % 64
    same = hh[:, None] == hh[None, :]
    su = same & (ii[:, None] < ii[None, :])
    ui = same & (ii[:, None] <= ii[None, :])
    sl = same & (ii[:, None] > ii[None, :])
    shared["cmask"] = np.concatenate([su, ui, sl], axis=1).astype(np.float32)
    shared["bones"] = same.astype(np.float32)
    scanm = np.ones((1, 512), np.float32)
    scanm[:, ::64] = 0.0
    shared["scanm"] = scanm
    shared["rwkv_shift_mix"] = np.ascontiguousarray(inputs["rwkv_shift_mix"].reshape(1792, 1), dtype=np.float32)
    for nm in ("rwkv_w0", "rwkv_a0", "rwkv_k_k", "rwkv_k_a", "rwkv_r_k", "rwkv_ln_w", "rwkv_ln_b"):
        shared[nm] = np.ascontiguousarray(inputs[nm].reshape(512, 1), dtype=np.float32)
    shared["rwkv_w2"] = np.ascontiguousarray(inputs["rwkv_w2"][0], dtype=np.float32)
    shared["rwkv_a2"] = np.ascontiguousarray(inputs["rwkv_a2"][0], dtype=np.float32)
    shared["rwkv_g2"] = np.ascontiguousarray(inputs["rwkv_g2"][0], dtype=np.float32)
    shared["attn_sinks"] = np.ascontiguousarray(inputs["attn_sinks"].reshape(1, 8), dtype=np.float32)
    for nm in ("ffn1_gate", "ffn1_up", "ffn1_down", "ffn2_gate", "ffn2_up", "ffn2_down"):
        shared[nm] = np.ascontiguousarray(inputs[nm][0], dtype=np.float32)
    for nm in ("norm_ffn1", "norm_mix", "norm_ffn2"):
        shared[nm] = np.ascontiguousarray(inputs[nm].reshape(1, D), dtype=np.float32)
    shared["norm_final"] = np.ascontiguousarray(inputs["norm_final"].reshape(1, D), dtype=np.float32)
    in_maps = []
    for c in range(NCORES):
        b, s = c // 4, c % 4
        xs = np.zeros((NTH, D), np.float32)
        if s > 0:
            xs[:128] = x[b, s * NTOK - 128:s * NTOK]
        xs[128:] = x[b, s * NTOK:(s + 1) * NTOK]
        m = dict(shared)
        m["xs"] = xs
        m["mask1"] = mask0 if s == 0 else maskA
        fmv = np.zeros((1, 8), np.float32)
        for i in range(8):
            if i // 4 == b and i % 4 < s:
                fmv[0, i] = 1.0
        m["fm"] = fmv
        in_maps.append(m)
    return in_maps


STAGE = "fullccin"


def kernel(**inputs):
    if STAGE not in _NC_CACHE:
        _NC_CACHE[STAGE] = build(STAGE)
    nc = _NC_CACHE[STAGE]
    in_maps = _prep_inputs(inputs)
    if "ccin" in STAGE:
        for m in in_maps:
            m["ccg"] = np.zeros((NCORES * 128, 1024), np.float32)
        res = run_bass_kernel_spmd(nc, in_maps, core_ids=list(range(NCORES)))
        ccg = np.concatenate([np.asarray(r["sf_dbg"], dtype=np.float32).reshape(128, 1024) for r in res.results], axis=0)
        for m in in_maps:
            m["ccg"] = ccg
    res = run_bass_kernel_spmd(nc, in_maps, core_ids=list(range(NCORES)))
    outs = [np.asarray(r["out"], dtype=np.float32).reshape(NTOK, D) for r in res.results]
    full = np.stack([np.concatenate(outs[b * 4:(b + 1) * 4], axis=0) for b in range(2)], axis=0)
    return full
```

```python
import contextlib
import numpy as np
import concourse.bass as bass
import concourse.mybir as mybir
from concourse.bass_utils import run_bass_kernel_spmd

F32 = mybir.dt.float32
BF16 = mybir.dt.bfloat16
AF = mybir.ActivationFunctionType
ALU = mybir.AluOpType
AX = mybir.AxisListType

D = 1024
DFF = 2816
NTOK = 2048
NT = 17
NTH = NT * 128
NCORES = 8
SEM_LIM = 20000
NLANES = 12
CC_INC = 1


class _Op:
    __slots__ = ("q", "fn", "deps", "need_inc", "semidx", "semval", "lane", "laneval", "idx")


class Prog:
    CE = ("pe", "act", "dve", "pool")
    QS = ("pe", "act", "dve", "pool", "sp")

    def __init__(self, nc):
        self.nc = nc
        self.ops = {q: [] for q in self.QS}
        self.last_w = {}
        self.readers = {}
        self.lane_rr = {q: 0 for q in self.QS}
        self.lane_cnt = {}
        self.lane_last = {}
        self.enabled = True
        self.capture = None
        self._open_grp = None

    def _deps(self, q, reads, writes, is_dma):
        deps = set()
        for r in reads:
            ev = self.last_w.get(r)
            if ev is not None:
                deps.add(ev)
        for w in writes:
            ev = self.last_w.get(w)
            if ev is not None:
                deps.add(ev)
            for ev in self.readers.get(w, ()):
                deps.add(ev)
        if q == "pe" and not is_dma:
            deps = {e for e in deps if not (e[0] == "c" and e[1] == "pe")}
        return deps

    def _commit(self, ev, reads, writes):
        for r in reads:
            self.readers.setdefault(r, []).append(ev)
        for w in writes:
            self.last_w[w] = ev
            self.readers[w] = []

    def replay(self, th):
        kind, a, kw = th
        if kind == "op":
            self.op(*a, **kw)
        elif kind == "grp":
            for it in a:
                self.op(*it[1], **it[2])
        else:
            self.dma(*a)

    def op(self, q, name, reads, writes, *args, **kw):
        if not self.enabled:
            return None
        if self.capture is not None:
            item = ("op", (q, name, reads, writes) + args, kw)
            if self._open_grp is not None:
                self._open_grp.append(item)
                if kw.get("stop", True):
                    self._open_grp = None
            elif name == "matmul" and not kw.get("stop", True):
                self._open_grp = [item]
                self.capture.append(("grp", self._open_grp, {}))
            else:
                self.capture.append(item)
            return None
        o = _Op()
        fn = (name, args, kw)
        writes = list(writes) + [r for r in reads if isinstance(r, tuple) and r[0] == "ps"]
        o.q, o.fn, o.need_inc, o.lane = q, fn, False, None
        o.deps = self._deps(q, reads, writes, False)
        o.idx = len(self.ops[q])
        self.ops[q].append(o)
        self._commit(("c", q, o.idx), reads, writes)
        return o

    def dma(self, q, reads, writes, out, in_):
        if not self.enabled:
            return _Op()
        if self.capture is not None:
            self.capture.append(("dma", (q, reads, writes, out, in_), {}))
            return _Op()
        o = _Op()
        fn = ("dma_start", (), {"out": out, "in_": in_})
        o.q, o.fn, o.need_inc = q, fn, False
        o.deps = self._deps(q, reads, writes, True)
        lane = self.lane_rr[q]
        self.lane_rr[q] = (lane + 1) % NLANES
        key = (q, lane)
        prev = self.lane_last.get(key)
        if prev is not None:
            o.deps.add(prev)
        cnt = self.lane_cnt.get(key, 0) + 16
        self.lane_cnt[key] = cnt
        o.lane, o.laneval = key, cnt
        o.idx = len(self.ops[q])
        self.ops[q].append(o)
        ev = ("d", key, cnt)
        self.lane_last[key] = ev
        self._commit(ev, reads, writes)
        return o

    def coll(self, q, reads, writes, *args, **kw):
        if not self.enabled:
            return _Op()
        o = _Op()
        o.q, o.fn, o.need_inc = q, ("collective_compute", args, kw), False
        o.deps = self._deps(q, reads, writes, True)
        key = ("cc", len(self.lane_cnt))
        self.lane_cnt[key] = CC_INC
        o.lane, o.laneval = key, CC_INC
        o.idx = len(self.ops[q])
        self.ops[q].append(o)
        ev = ("d", key, CC_INC)
        self.lane_last[key] = ev
        self._commit(ev, reads, writes)
        return o

    def barrier(self, q):
        if not self.enabled:
            return
        o = _Op()
        o.q, o.fn, o.need_inc, o.lane = q, None, False, None
        o.deps = set(self.lane_last.values())
        for e in self.CE:
            if self.ops[e]:
                last = [x for x in self.ops[e] if x.fn is not None and x.lane is None]
                if last:
                    o.deps.add(("c", e, last[-1].idx))
        o.idx = len(self.ops[q])
        self.ops[q].append(o)

    def wait_all(self, q):
        o = _Op()
        o.q, o.fn, o.need_inc, o.lane = q, None, False, None
        o.deps = set(self.lane_last.values())
        o.idx = len(self.ops[q])
        self.ops[q].append(o)

    def emit(self):
        nc = self.nc
        for q in self.QS:
            for o in self.ops[q]:
                for ev in o.deps:
                    if ev[0] == "c":
                        self.ops[ev[1]][ev[2]].need_inc = True
        nsem = {}
        for e in self.CE:
            cnt = 0
            for o in self.ops[e]:
                if o.need_inc:
                    o.semidx, o.semval = cnt // SEM_LIM, cnt % SEM_LIM + 1
                    cnt += 1
            nsem[e] = (cnt + SEM_LIM - 1) // SEM_LIM
        with contextlib.ExitStack() as st:
            sems = {}
            for e in self.CE:
                for k in range(nsem[e]):
                    sems[("c", e, k)] = st.enter_context(nc.semaphore(f"s_{e}_{k}"))
            for key in self.lane_cnt:
                sems[("d",) + key] = st.enter_context(nc.semaphore(f"l_{key[0]}_{key[1]}"))
            block = st.enter_context(nc.Block())

            def run(q, eng):
                waited = {}
                for o in self.ops[q]:
                    need = {}
                    for ev in o.deps:
                        if ev[0] == "c":
                            src = self.ops[ev[1]][ev[2]]
                            k, v = ("c", ev[1], src.semidx), src.semval
                        else:
                            k, v = ("d",) + ev[1], ev[2]
                        if need.get(k, 0) < v:
                            need[k] = v
                    for k, v in need.items():
                        if waited.get(k, 0) < v:
                            eng.wait_ge(sems[k], v)
                            waited[k] = v
                    if o.fn is None:
                        continue
                    name, args, kw = o.fn
                    ins = getattr(eng, name)(*args, **kw)
                    if o.lane is not None:
                        ins.then_inc(sems[("d",) + o.lane], CC_INC if o.lane[0] == "cc" else 16)
                    elif o.need_inc:
                        ins.then_inc(sems[("c", q, o.semidx)], 1)

            if self.ops["pe"]:
                @block.tensor
                def _(eng):
                    run("pe", eng)
            if self.ops["act"]:
                @block.scalar
                def _(eng):
                    run("act", eng)
            if self.ops["dve"]:
                @block.vector
                def _(eng):
                    run("dve", eng)
            if self.ops["pool"]:
                @block.gpsimd
                def _(eng):
                    run("pool", eng)
            if self.ops["sp"]:
                @block.sync
                def _(eng):
                    run("sp", eng)


FFN_GROUPS = [(0, 6), (6, 6), (12, 5), (17, 5)]
SB_BASE = 16512
SB_LIMIT = 16512 + 212800
_DTSZ = {F32: 4, BF16: 2}
NEG = -30000.0


class Arena:
    def __init__(self, nc):
        self.nc, self.off, self.n = nc, SB_BASE, 0

    def alloc(self, name, shape, dt=F32):
        n = _DTSZ[dt]
        for d in shape[1:]:
            n *= d
        off = (self.off + 31) // 32 * 32
        assert off + n <= SB_LIMIT, (name, off + n - SB_LIMIT)
        self.off = off + n
        self.n += 1
        return self.nc.alloc_sbuf_tensor_at(f"{name}_{self.n}", list(shape), dt, offset=off)


def build(stage="full"):
    nc = bass.Bass("TRN2", target_bir_lowering=False)
    P = Prog(nc)
    es = contextlib.ExitStack()
    A = Arena(nc)
    sb = A.alloc

    def dram(name, shape, kind="ExternalInput", dt=F32):
        return nc.dram_tensor(name, list(shape), dt, kind=kind)

    xs = dram("xs", [NTH, D])
    out = dram("out", [NTOK, D], kind="ExternalOutput")
    ident_d = dram("ident", [128, 128])
    maskA_d = dram("maskA", [128, 256])
    mask1_d = dram("mask1", [128, 256])
    wdram = {}
    for nm, shp in (("ffn1_gate", [D, DFF]), ("ffn1_up", [D, DFF]), ("ffn1_down", [DFF, D]),
                    ("ffn2_gate", [D, DFF]), ("ffn2_up", [D, DFF]), ("ffn2_down", [DFF, D]),
                    ("w_in", [D, 2560]), ("w_out", [D, D])):
        wdram[nm] = dram(nm, shp)
    vec_d = {}
    for nm in ("norm_ffn1", "norm_mix", "norm_ffn2", "norm_final"):
        vec_d[nm] = dram(nm, [1, D])
    b_attn_d = dram("b_in_attn", [768, 1])
    sinks_d = dram("attn_sinks", [1, 8])
    rw_d = {}
    rw_d["rwkv_shift_mix"] = dram("rwkv_shift_mix", [1792, 1])
    for nm in ("rwkv_w0", "rwkv_a0", "rwkv_k_k", "rwkv_k_a", "rwkv_r_k", "rwkv_ln_w", "rwkv_ln_b"):
        rw_d[nm] = dram(nm, [512, 1])
    rw_d["rwkv_w2"] = dram("rwkv_w2", [64, 512])
    rw_d["rwkv_a2"] = dram("rwkv_a2", [64, 512])
    rw_d["rwkv_g2"] = dram("rwkv_g2", [128, 512])
    cmask_d = dram("cmask", [128, 384])
    bones_d = dram("bones", [128, 128])
    scanm_d = dram("scanm", [1, 512])
    fm_d = dram("fm", [1, 8])
    use_cc = "ccin" not in stage
    if use_cc:
        cc_in = [dram(f"cc_in{j}", [128, 256], kind="Internal") for j in range(4)]
        cc_out = [dram(f"cc_out{j}", [4 * 128, 256], kind="Internal") for j in range(4)]

        def ccg_view(j):
            return cc_out[j].ap().rearrange("(r p) f -> p r f", p=128)
    else:
        ccg_d = dram("ccg", [4 * 128, 1024])
        sf_dbg = dram("sf_dbg", [128, 1024], kind="ExternalOutput")

        def ccg_view(j):
            return ccg_d.ap()[:, j * 256:(j + 1) * 256].rearrange("(r p) f -> p r f", p=128)

    X = sb("X", [128, NT, D])
    HT = sb("HT", [128, 8, NTH], BF16)
    ss = sb("ss", [128, 4 * NT])
    rstd = sb("rstd", [128, 4 * NT])
    eps_t = sb("eps_t", [128, 1])
    ident_f = sb("ident_f", [128, 128])
    ident_b = sb("ident_b", [128, 128], BF16)
    norm_mark = A.off
    gbc = sb("gbc", [128, D])
    hb = [sb(f"hb{i}", [128, D], BF16) for i in range(2)]
    junk = sb("junk", [128, D], BF16)
    phase_mark = A.off

    PSALL = es.enter_context(nc.psum_tensor("psall", [128, 4096], F32))
    PSALLB = PSALL.bitcast(BF16)

    def psf(bank, a=0, b=512):
        return PSALL[:, bank * 512 + a:bank * 512 + b]

    def psb(bank, a=0, b=1024):
        return PSALLB[:, bank * 1024 + a:bank * 1024 + b]

    def barrier():
        for q in ("pe", "act", "dve", "pool", "sp"):
            P.barrier(q)

    def mark(name):
        if ("stop_" + name + "_") in (stage + "_"):
            P.enabled = False

    P.dma("sp", [], ["ident_f"], ident_f[:], ident_d.ap())
    P.op("act", "copy", ["ident_f"], ["ident_b"], out=ident_b[:], in_=ident_f[:])
    P.op("dve", "memset", [], ["ss"], ss[:], 0.0)
    P.op("dve", "memset", [], ["eps_t"], eps_t[:], 1e-5)

    norm_ctr = [0]

    def load_gain(name):
        P.dma("sp", [], ["gbc"], gbc[:], vec_d[name].ap().partition_broadcast(128))

    def rmsnorm_tile(i, dst_fn):
        k = norm_ctr[0]
        norm_ctr[0] += 1
        col = k % (4 * NT)
        P.op("act", "activation", [("X", i), "ss"], ["junk", ("ss", col)],
             out=junk[:], in_=X[:, i, :], func=AF.Square, accum_out=ss[:, col:col + 1])
        P.op("act", "activation", [("ss", col), "eps_t"], [("rstd", col)],
             out=rstd[:, col:col + 1], in_=ss[:, col:col + 1], func=AF.Sqrt, bias=eps_t[:, 0:1], scale=1.0 / D)
        P.op("dve", "reciprocal", [("rstd", col)], [("rstd", col)],
             out=rstd[:, col:col + 1], in_=rstd[:, col:col + 1])
        dst_fn(i, col)

    def norm_to_HT(i, col):
        b = i % 2
        P.op("dve", "scalar_tensor_tensor", [("X", i), ("rstd", col), "gbc"], [("hb", b)],
             out=hb[b][:], in0=X[:, i, :], scalar=rstd[:, col:col + 1],
             in1=gbc[:], op0=ALU.mult, op1=ALU.mult)
        bank = 4 + (i % 4)
        for c in range(8):
            P.op("pe", "transpose", [("hb", b), "ident_b"], [("ps", bank)],
                 out=psb(bank, c * 128, (c + 1) * 128),
                 in_=hb[b][:, c * 128:(c + 1) * 128], identity=ident_b[:])
        P.op("act", "copy", [("ps", bank)], [("HT", i)],
             out=HT[:, :, i * 128:(i + 1) * 128],
             in_=psb(bank).rearrange("p (c t) -> p c t", c=8))

    def ffn(prefix, tiles):
        A.off = phase_mark
        Wg = [sb(f"Wg{i}", [128, 8, 768], BF16) for i in range(2)]
        Wu = [sb(f"Wu{i}", [128, 8, 768], BF16) for i in range(2)]
        Wd = [sb(f"Wd{i}", [128, 6, D], BF16) for i in range(2)]
        sg = [sb(f"sg{i}", [128, 256]) for i in range(2)]
        aT = [sb(f"aT{i}", [128, 256], BF16) for i in range(3)]
        gate, up, down = wdram[prefix + "_gate"], wdram[prefix + "_up"], wdram[prefix + "_down"]
        blocks = []
        t = list(tiles)
        if len(t) % 2 == 1:
            blocks.append(t[:1])
            t = t[1:]
        for j in range(0, len(t), 2):
            blocks.append(t[j:j + 2])
        cnt = 0
        for gi, (f0, nf) in enumerate(FFN_GROUPS):
            wb = gi % 2
            w = nf * 128
            for c in range(8):
                P.dma("pool", [], [("Wg", wb)], Wg[wb][:, c, 0:w],
                      gate.ap()[c * 128:(c + 1) * 128, f0 * 128:f0 * 128 + w])
            for c in range(8):
                P.dma("pool", [], [("Wu", wb)], Wu[wb][:, c, 0:w],
                      up.ap()[c * 128:(c + 1) * 128, f0 * 128:f0 * 128 + w])
            for c in range(nf):
                P.dma("pool", [], [("Wd", wb)], Wd[wb][:, c, :],
                      down.ap()[(f0 + c) * 128:(f0 + c + 1) * 128, :])
            for blk in blocks:
                T = 128 * len(blk)
                tok0 = blk[0] * 128
                htr = [("HT", t_) for t_ in blk]

                def gu(fc, k):
                    gb = 4 + (k % 2)
                    ub = 6 + (k % 2)
                    for kc in range(8):
                        P.op("pe", "matmul", [("Wg", wb)] + htr, [("ps", gb)],
                             psf(gb, 0, T), lhsT=Wg[wb][:, kc, fc * 128:(fc + 1) * 128],
                             rhs=HT[:, kc, tok0:tok0 + T], start=(kc == 0), stop=(kc == 7))
                    for kc in range(8):
                        P.op("pe", "matmul", [("Wu", wb)] + htr, [("ps", ub)],
                             psf(ub, 0, T), lhsT=Wu[wb][:, kc, fc * 128:(fc + 1) * 128],
                             rhs=HT[:, kc, tok0:tok0 + T], start=(kc == 0), stop=(kc == 7))
                    s_ = k % 2
                    a_ = k % 3
                    P.op("act", "activation", [("ps", gb)], [("sg", s_)],
                         out=sg[s_][:, 0:T], in_=psf(gb, 0, T), func=AF.Silu)
                    P.op("dve", "tensor_tensor", [("sg", s_), ("ps", ub)], [("aT", a_)],
                         out=aT[a_][:, 0:T], in0=sg[s_][:, 0:T], in1=psf(ub, 0, T), op=ALU.mult)

                def dn_(fc, k):
                    a_ = k % 3
                    for ti, tt in enumerate(blk):
                        for dh in range(2):
                            bank = 2 * ti + dh
                            P.op("pe", "matmul", [("aT", a_), ("Wd", wb)], [("ps", bank)],
                                 psf(bank), lhsT=aT[a_][:, ti * 128:(ti + 1) * 128],
                                 rhs=Wd[wb][:, fc, dh * 512:(dh + 1) * 512],
                                 start=(fc == 0), stop=(fc == nf - 1))

                ks = list(range(cnt, cnt + nf))
                cnt += nf
                gu(0, ks[0])
                for fc in range(nf):
                    if fc + 1 < nf:
                        gu(fc + 1, ks[fc + 1])
                    dn_(fc, ks[fc])
                for ti, tt in enumerate(blk):
                    for dh in range(2):
                        bank = 2 * ti + dh
                        P.op("dve", "scalar_tensor_tensor", [("ps", bank), ("X", tt)], [("X", tt)],
                             out=X[:, tt, dh * 512:(dh + 1) * 512], in0=psf(bank), scalar=0.5,
                             in1=X[:, tt, dh * 512:(dh + 1) * 512], op0=ALU.mult, op1=ALU.add)

    load_gain("norm_ffn1")
    for i in range(NT):
        P.dma("sp", [], [("X", i)], X[:, i, :], xs.ap()[i * 128:(i + 1) * 128, :])
    if not stage.startswith(("noffn", "mix")):
        for i in range(NT):
            rmsnorm_tile(i, norm_to_HT)
        ffn("ffn1", range(NT))
        barrier()

    w_in, w_out = wdram["w_in"], wdram["w_out"]
    if not stage.startswith("noffn"):
        load_gain("norm_mix")
        for i in range(NT):
            rmsnorm_tile(i, norm_to_HT)
        barrier()
        A.off = norm_mark
        WT = [sb(f"WT{i}", [128, 8, 128], BF16) for i in range(3)]
        wt_ctr = [0]

        def load_wtile(colspec):
            i = wt_ctr[0] % 3
            wt_ctr[0] += 1
            for kc in range(8):
                off = 0
                for (c0, n) in colspec:
                    P.dma("pool", [], [("WT", i)], WT[i][:, kc, off:off + n],
                          w_in.ap()[kc * 128:(kc + 1) * 128, c0:c0 + n])
                    off += n
            return i

        def proj(wi, tok0, T, bank, off=0):
            for kc in range(8):
                P.op("pe", "matmul", [("WT", wi)] + [("HT", t_) for t_ in range(tok0 // 128, (tok0 + T + 127) // 128)],
                     [("ps", bank)], psf(bank, off, off + T), lhsT=WT[wi][:, kc, :],
                     rhs=HT[:, kc, tok0:tok0 + T], start=(kc == 0), stop=(kc == 7))

        WO = [sb(f"WO{i}", [128, D], BF16) for i in range(2)]
        wo_ctr = [0]

        def load_wout(row0):
            i = wo_ctr[0] % 2
            wo_ctr[0] += 1
            P.dma("pool", [], [("WO", i)], WO[i][:], w_out.ap()[row0:row0 + 128, :])
            return i

        def wout_accum(srcs):
            for n in range(1, NT):
                for dh in range(2):
                    bank = 2 * (n % 2) + dh
                    for j, (fn, wi, res) in enumerate(srcs):
                        P.op("pe", "matmul", res(n) + [("WO", wi)], [("ps", bank)], psf(bank),
                             lhsT=fn(n), rhs=WO[wi][:, dh * 512:(dh + 1) * 512],
                             start=(j == 0), stop=(j == len(srcs) - 1))
                    P.op("dve", "tensor_tensor", [("ps", bank), ("X", n)], [("X", n)],
                         out=X[:, n, dh * 512:(dh + 1) * 512], in0=psf(bank),
                         in1=X[:, n, dh * 512:(dh + 1) * 512], op=ALU.add)

        mark_mix = A.off
        do_attn = 'noattn' not in stage
        do_rwkv = 'norwkv' not in stage
        ACOL = 1792
        battn = sb("battn", [128, 8])
        for c in range(4):
            P.dma("sp", [], ["battn"], battn[:, c:c + 1], b_attn_d.ap()[c * 128:(c + 1) * 128, :])
        for g in range(2):
            for hf in range(2):
                P.dma("sp", [], ["battn"], battn[hf * 64:(hf + 1) * 64, 4 + g:5 + g],
                      b_attn_d.ap()[512 + g * 64:512 + (g + 1) * 64, :])
        P.dma("sp", [], ["battn"], battn[:, 6:7], b_attn_d.ap()[640:768, :])
        sinks = sb("sinks", [128, 8])
        P.dma("sp", [], ["sinks"], sinks[:], sinks_d.ap().partition_broadcast(128))
        maskA = sb("maskA", [128, 256])
        mask1 = sb("mask1", [128, 256])
        P.dma("sp", [], ["maskA"], maskA[:], maskA_d.ap())
        P.dma("sp", [], ["mask1"], mask1[:], mask1_d.ap())
        mark("a1")
        VT = sb("VT", [128, NTH], BF16)
        Vtok = sb("Vtok", [128, NT, 128], BF16)
        Vpad = [sb(f"Vpad{i}", [128, NT, 128], BF16) for i in range(2)]
        Qg = sb("Qg", [128, 2, NTOK], BF16)
        KP = [sb(f"KP{i}", [128, NTH], BF16) for i in range(2)]
        for par in range(2):
            P.op("dve", "memset", [], [("KP", par)], KP[par][:], 0.0)
        AO = sb("AO", [128, 2, NTOK], BF16)
        sm = [sb(f"sm{i}", [128, 4, 256]) for i in range(2)]
        Pb = [sb(f"Pb{i}", [128, 4, 256], BF16) for i in range(2)]
        PT = [sb(f"PT{i}", [128, 8, 128], BF16) for i in range(2)]
        st = [sb(f"st{i}", [128, 16]) for i in range(2)]

        tok_blocks = [(0, 128)] + [(128 + 512 * j, 512) for j in range(4)]

        def proj_to(colspec, dst_fn, bias_ap, res, blocks):
            wi = load_wtile(colspec)
            for bi, (tok0, T) in enumerate(blocks):
                bank = 4 + (bi % 4)
                proj(wi, tok0, T, bank)
                P.op("act", "activation", [("ps", bank), "battn"], [res],
                     out=dst_fn(tok0, T), in_=psf(bank, 0, T), func=AF.Identity, bias=bias_ap, scale=1.0)

        proj_to([(ACOL + 640, 128)], lambda t0, T: VT[:, t0:t0 + T], battn[:, 6:7], "VT", tok_blocks)
        mark("a2")
        for n in range(NT):
            bank = 4 + (n % 4)
            P.op("pe", "transpose", ["VT", "ident_b"], [("ps", bank)], out=psb(bank, 0, 128),
                 in_=VT[:, n * 128:(n + 1) * 128], identity=ident_b[:])
            P.op("act", "copy", [("ps", bank)], [("Vtok", n)], out=Vtok[:, n, :], in_=psb(bank, 0, 128))
        mark("a3")
        for g in range(2 if do_attn else 0):
            for par in range(2):
                P.op("dve", "memset", [], [("Vpad", par)], Vpad[par][:], 0.0)
                P.op("dve", "tensor_copy", [("Vtok", n) for n in range(NT)], [("Vpad", par)],
                     out=Vpad[par][:, :, par * 64:(par + 1) * 64], in_=Vtok[:, :, g * 64:(g + 1) * 64])
            for lc in range(2):
                proj_to([(ACOL + (2 * g + lc) * 128, 128)],
                        lambda t0, T, lc=lc: Qg[:, lc, t0 - 128:t0 - 128 + T], battn[:, 2 * g + lc:2 * g + lc + 1],
                        ("Qg", lc), tok_blocks[1:])
            wi_k = load_wtile([(ACOL + 512 + g * 64, 64), (ACOL + 512 + g * 64, 64)])
            for bi, (tok0, T_) in enumerate(tok_blocks):
                bank = 4 + (bi % 4)
                proj(wi_k, tok0, T_, bank)
                for par in range(2):
                    ps_ = slice(par * 64, (par + 1) * 64)
                    P.op("act", "activation", [("ps", bank), "battn"], [("KP", par)],
                         out=KP[par][ps_, tok0:tok0 + T_], in_=psf(bank, 0, T_)[ps_], func=AF.Identity,
                         bias=battn[ps_, 4 + g:5 + g], scale=1.0)
            mark("a4")

            def attn_stage(n, k, g=g):
                u = n % 2
                q0 = (n - 1) * 128
                sb0 = 4 + 2 * u
                mx, dd, es_, den = st[u][:, 0:4], st[u][:, 4:8], st[u][:, 8:12], st[u][:, 12:16]
                if k == 0:
                    for j in range(4):
                        lc, par = j // 2, j % 2
                        bank = sb0 + j // 2
                        P.op("pe", "matmul", [("Qg", lc), ("KP", par)], [("ps", bank)],
                             psf(bank, (j % 2) * 256, (j % 2) * 256 + 256),
                             lhsT=Qg[:, lc, q0:q0 + 128],
                             rhs=KP[par][:, (n - 1) * 128:(n + 1) * 128], start=True, stop=True)
                elif k == 1:
                    mk = mask1 if n == 1 else maskA
                    mkn = "mask1" if n == 1 else "maskA"
                    for hb_ in range(2):
                        P.op("dve", "scalar_tensor_tensor", [("ps", sb0 + hb_), mkn], [("sm", u)],
                             out=sm[u][:, 2 * hb_:2 * hb_ + 2, :],
                             in0=psf(sb0 + hb_).rearrange("p (h k) -> p h k", h=2),
                             scalar=0.125, in1=mk[:].unsqueeze(1).to_broadcast([128, 2, 256]),
                             op0=ALU.mult, op1=ALU.add)
                    P.op("dve", "tensor_reduce", [("sm", u)], [("st", u)], out=mx, in_=sm[u][:], axis=AX.X, op=ALU.max)
                    P.op("dve", "tensor_tensor", [("st", u), "sinks"], [("st", u)], out=mx, in0=mx,
                         in1=sinks[:, 4 * g:4 * g + 4], op=ALU.max)
                    P.op("dve", "tensor_tensor", [("st", u), "sinks"], [("st", u)], out=dd,
                         in0=sinks[:, 4 * g:4 * g + 4], in1=mx, op=ALU.subtract)
                    P.op("dve", "tensor_tensor", [("sm", u), ("st", u)], [("sm", u)], out=sm[u][:], in0=sm[u][:],
                         in1=mx.unsqueeze(2).to_broadcast([128, 4, 256]), op=ALU.subtract)
                elif k == 2:
                    P.op("act", "activation", [("sm", u)], [("sm", u)], out=sm[u][:], in_=sm[u][:], func=AF.Exp)
                    P.op("act", "activation", [("st", u)], [("st", u)], out=es_, in_=dd, func=AF.Exp)
                elif k == 3:
                    P.op("dve", "tensor_reduce", [("sm", u)], [("st", u)], out=den, in_=sm[u][:], axis=AX.X, op=ALU.add)
                    P.op("dve", "tensor_tensor", [("st", u)], [("st", u)], out=den, in0=den, in1=es_, op=ALU.add)
                    P.op("dve", "reciprocal", [("st", u)], [("st", u)], out=den, in_=den)
                    P.op("dve", "tensor_tensor", [("sm", u), ("st", u)], [("Pb", u)], out=Pb[u][:], in0=sm[u][:],
                         in1=den.unsqueeze(2).to_broadcast([128, 4, 256]), op=ALU.mult)
                elif k == 4:
                    tb = sb0
                    for j in range(4):
                        for kh in range(2):
                            P.op("pe", "transpose", [("Pb", u), "ident_b"], [("ps", tb)],
                                 out=psb(tb, (j * 2 + kh) * 128, (j * 2 + kh + 1) * 128),
                                 in_=Pb[u][:, j, kh * 128:(kh + 1) * 128], identity=ident_b[:])
                    P.op("act", "copy", [("ps", tb)], [("PT", u)], out=PT[u][:],
                         in_=psb(tb).rearrange("p (c t) -> p c t", c=8))
                elif k == 5:
                    ab = sb0 + 1
                    for lc in range(2):
                        idx = 0
                        for par in range(2):
                            j = lc * 2 + par
                            for kh in range(2):
                                P.op("pe", "matmul", [("Vpad", par), ("PT", u)], [("ps", ab)],
                                     psf(ab, lc * 128, (lc + 1) * 128),
                                     lhsT=Vpad[par][:, n - 1 + kh, :], rhs=PT[u][:, j * 2 + kh, :],
                                     start=(idx == 0), stop=(idx == 3))
                                idx += 1
                    P.op("act", "copy", [("ps", ab)], [("AO", n)], out=AO[:, :, q0:q0 + 128],
                         in_=psf(ab, 0, 256).rearrange("p (c t) -> p c t", c=2))

            for n0 in range(1, NT, 2):
                for k in range(6):
                    for n in (n0, n0 + 1):
                        attn_stage(n, k)
            mark("a6")
            wis = [load_wout(512 + (2 * g + lc) * 128) for lc in range(2)]
            wout_accum([(lambda n, lc=lc: AO[:, lc, (n - 1) * 128:n * 128], wis[lc], lambda n: [("AO", n)]) for lc in range(2)])
        barrier()
        A.off = mark_mix
        C0 = float(np.exp(-0.5))
        pv = sb("pv", [128, 48])
        SMX, W0, A0, KK_, KA_, RK_, LNW, LNB = 0, 14, 18, 22, 26, 30, 34, 38
        for c in range(14):
            P.dma("sp", [], ["pv"], pv[:, SMX + c:SMX + c + 1], rw_d["rwkv_shift_mix"].ap()[c * 128:(c + 1) * 128, :])
        for nm, col in (("rwkv_w0", W0), ("rwkv_a0", A0), ("rwkv_k_k", KK_), ("rwkv_k_a", KA_),
                        ("rwkv_r_k", RK_), ("rwkv_ln_w", LNW), ("rwkv_ln_b", LNB)):
            for j in range(4):
                P.dma("sp", [], ["pv"], pv[:, col + j:col + j + 1], rw_d[nm].ap()[j * 128:(j + 1) * 128, :])
        W2P = sb("W2P", [128, 512], BF16)
        A2P = sb("A2P", [128, 512], BF16)
        G2 = sb("G2", [128, 512], BF16)
        P.op("dve", "memset", [], ["W2A2"], W2P[:], 0.0)
        P.op("dve", "memset", [], ["W2A2"], A2P[:], 0.0)
        P.dma("pool", [], ["W2A2"], W2P[0:64, :], rw_d["rwkv_w2"].ap())
        P.dma("pool", [], ["W2A2"], A2P[64:128, :], rw_d["rwkv_a2"].ap())
        P.dma("pool", [], ["G2"], G2[:], rw_d["rwkv_g2"].ap())
        MK = sb("MK", [128, 384], BF16)
        P.dma("pool", [], ["MK"], MK[:], cmask_d.ap())
        bones_f = sb("bones_f", [128, 128])
        bones_b = sb("bones_b", [128, 128], BF16)
        bones_s = sb("bones_s", [128, 128])
        P.dma("sp", [], ["bones_f"], bones_f[:], bones_d.ap())
        P.op("act", "copy", ["bones_f"], ["bones_b"], out=bones_b[:], in_=bones_f[:])
        P.op("act", "mul", ["bones_f"], ["bones_s"], out=bones_s[:], in_=bones_f[:], mul=1.0 / 64)
        scanm = sb("scanm", [128, 512], BF16)
        P.dma("pool", [], ["scanm"], scanm[:], scanm_d.ap().partition_broadcast(128))
        fm = sb("fm", [128, 8])
        P.dma("sp", [], ["fm"], fm[:], fm_d.ap().partition_broadcast(128))
        gneps = sb("gneps", [128, 1])
        P.op("dve", "memset", [], ["gneps"], gneps[:], 64e-5)
        WA = sb("WA", [128, NTOK], BF16)
        GL = sb("GL", [128, NTOK], BF16)
        pf = [sb("pf0", [128, 513])]
        ga_off = A.off
        T = [sb(f"T{i}", [128, 512]) for i in range(10)]
        tb16 = [sb(f"tb16_{i}", [128, 512], BF16) for i in range(2)]
        A_save = A.off
        A.off = ga_off
        GA = sb("GA", [128, 4, 256])
        LT = sb("LT", [128, 3, 128])
        A.off = (A.off + 2047) // 2048 * 2048 if False else A.off
        HP_ = None
        A.off = A_save
        GA_RES = [("T", i) for i in range(3)]
        AR = sb("AR", [128, 8, 256], BF16)
        EXP = {nm: sb(nm, [128, 8, 128], BF16) for nm in ("BT", "KT", "BH", "KH", "VE")}
        P.op("dve", "memset", [], ["AR"], AR[:], 0.0)
        for nm in EXP:
            P.op("dve", "memset", [], [nm], EXP[nm][:], 0.0)
        NB = [sb(f"NB{i}", [128, 384], BF16) for i in range(4)]
        KB = [sb(f"KB{i}", [128, 256], BF16) for i in range(4)]
        TK = [sb(f"TK{i}", [128, 256], BF16) for i in range(4)]
        Zt = [sb(f"Z{i}", [128, 256], BF16) for i in range(4)]
        NM = [[sb(f"NM{i}_{k}", [128, 256], BF16) for k in range(2)] for i in range(4)]
        Qt = [[sb(f"Qt{p}_{i}", [128, 128], BF16) for i in range(4)] for p in range(2)]
        GT = [[sb(f"GT{p}_{i}", [128, 128], BF16) for i in range(4)] for p in range(2)]
        Hb = [[sb(f"Hb{p}_{i}", [128, 128], BF16) for i in range(4)] for p in range(2)]
        SS = [[sb(f"SS{p}_{i}", [128, 256], BF16) for i in range(4)] for p in range(2)]
        WC8 = [sb(f"WC8_{i}", [128, 8]) for i in range(2)]
        grp_ctr = [0]
        pendingB = [[]]
        SF = sb("SF", [128, 256])
        OL = sb("OL", [128, NTOK])
        QH = sb("QH", [128, NTOK], BF16)
        BON = sb("BON", [128, NTOK], BF16)
        MX = nc.alloc_sbuf_tensor_at("MX_alias", [128, NTOK], BF16, offset=SB_BASE)
        Sst = sb("Sst", [128, 128])
        Sstb = sb("Sstb", [128, 128], BF16)
        Pti = sb("Pti", [128, 128])
        dtmp = sb("dtmp", [128, 128])
        bank_ctr = [0]

        cap_ctr = [0]

        def nb():
            if P.capture is not None:
                cap_ctr[0] = (cap_ctr[0] + 1) % 2
                return 6 + cap_ctr[0]
            bank_ctr[0] = (bank_ctr[0] + 1) % 6
            return bank_ctr[0]

        def v3(ap):
            return ap.rearrange("p (c t) -> p c t", c=8)

        pf_ctr = [0]

        def proj_shift(wi, smx_col, tb, dst, dst_res):
            u = 0
            tok0 = 128 + tb * 512
            bank = nb()
            proj(wi, tok0, 512, bank)
            P.op("act", "copy", [("ps", bank)], [("pf", u)], out=pf[u][:, 1:513], in_=psf(bank))
            bank2 = nb()
            for kc in range(8):
                P.op("pe", "matmul", [("WT", wi), ("HT", tok0 // 128 - 1)], [("ps", bank2)], psf(bank2, 0, 1),
                     lhsT=WT[wi][:, kc, :], rhs=HT[:, kc, tok0 - 1:tok0], start=(kc == 0), stop=(kc == 7))
            P.op("act", "copy", [("ps", bank2)], [("pf", u)], out=pf[u][:, 0:1], in_=psf(bank2, 0, 1))
            P.op("dve", "tensor_tensor", [("pf", u)], [dst_res], out=dst, in0=pf[u][:, 0:512], in1=pf[u][:, 1:513],
                 op=ALU.subtract)
            P.op("dve", "scalar_tensor_tensor", [("pf", u), dst_res, "pv"], [dst_res], out=dst, in0=dst,
                 scalar=pv[:, SMX + smx_col:SMX + smx_col + 1], in1=pf[u][:, 1:513], op0=ALU.mult, op1=ALU.add)

        mark("r1")
        wl1 = load_wtile([(1536, 128)])
        wl2 = load_wtile([(1664, 128)])
        for tb in range(4):
            cs_ = slice(tb * 512, (tb + 1) * 512)
            proj_shift(wl1, 12, tb, T[0][:], ("T", 0))
            P.op("act", "activation", [("T", 0)], [("WA", tb)], out=WA[0:64, cs_], in_=T[0][0:64, :], func=AF.Tanh)
            P.op("act", "copy", [("T", 0)], [("WA", tb)], out=WA[64:128, cs_], in_=T[0][64:128, :])
            proj_shift(wl2, 13, tb, T[1][:], ("T", 1))
            P.op("act", "activation", [("T", 1)], [("GL", tb)], out=GL[:, cs_], in_=T[1][:], func=AF.Sigmoid)

        mark("r2")
        def ew(q, name, reads, writes, **kw):
            P.op(q, name, reads, writes, **kw)

        chunk_ctr = [0]
        for j in range(4 if do_rwkv else 0):
            wr_ = load_wtile([(j * 128, 128)])
            wk_ = load_wtile([(512 + j * 128, 128)])
            wv_ = load_wtile([(1024 + j * 128, 128)])
            r_s, k_s, v_s, sgw, cs, a_t, kkn, kp, Ep, tmp = [T[i][:] for i in range(10)]
            R_ = lambda i: ("T", i)
            b_t, Eex, Em, Eh = k_s, sgw, cs, a_t

            def prep_main(tb, j=j, wr_=wr_, wk_=wk_, wv_=wv_):
                cs_ = slice(tb * 512, (tb + 1) * 512)
                proj_shift(wr_, j, tb, r_s, R_(0))
                proj_shift(wk_, 4 + j, tb, k_s, R_(1))
                proj_shift(wv_, 8 + j, tb, v_s, R_(2))
                bw = nb()
                P.op("pe", "matmul", ["W2A2", ("WA", tb)], [("ps", bw)], psf(bw), lhsT=W2P[:, j * 128:(j + 1) * 128],
                     rhs=WA[:, cs_], start=True, stop=True)
                P.op("act", "activation", [("ps", bw), "pv"], [R_(3)], out=sgw, in_=psf(bw), func=AF.Sigmoid,
                     bias=pv[:, W0 + j:W0 + j + 1], scale=1.0)
                ba = nb()
                P.op("pe", "matmul", ["W2A2", ("WA", tb)], [("ps", ba)], psf(ba), lhsT=A2P[:, j * 128:(j + 1) * 128],
                     rhs=WA[:, cs_], start=True, stop=True)
                P.op("act", "activation", [("ps", ba), "pv"], [R_(5)], out=a_t, in_=psf(ba), func=AF.Sigmoid,
                     bias=pv[:, A0 + j:A0 + j + 1], scale=1.0)
                P.op("dve", "tensor_scalar", [R_(1), "pv"], [R_(6)], out=kkn, in0=k_s, scalar1=pv[:, KK_ + j:KK_ + j + 1],
                     scalar2=None, op0=ALU.mult)
                P.op("act", "activation", [R_(6)], [("tb16", 0)], out=tb16[0][:], in_=kkn, func=AF.Square)
                bq = nb()
                P.op("pe", "matmul", ["bones_b", ("tb16", 0)], [("ps", bq)], psf(bq), lhsT=bones_b[:], rhs=tb16[0][:],
                     start=True, stop=True)
                P.op("dve", "tensor_scalar", [("ps", bq)], [R_(9)], out=tmp, in0=psf(bq), scalar1=1e-24, scalar2=None,
                     op0=ALU.max)
                P.op("act", "activation", [R_(9)], [R_(9)], out=tmp, in_=tmp, func=AF.Sqrt)
                P.op("dve", "reciprocal", [R_(9)], [R_(9)], out=tmp, in_=tmp)
                P.op("dve", "tensor_tensor", [R_(6), R_(9)], [R_(6)], out=kkn, in0=kkn, in1=tmp, op=ALU.mult)
                P.op("dve", "tensor_scalar", [R_(5), "pv"], [R_(7)], out=kp, in0=a_t, scalar1=-1.0,
                     scalar2=pv[:, KA_ + j:KA_ + j + 1], op0=ALU.add, op1=ALU.mult)
                P.op("dve", "scalar_tensor_tensor", [R_(7), R_(1)], [R_(7)], out=kp, in0=kp, scalar=1.0, in1=k_s,
                     op0=ALU.add, op1=ALU.mult)
                b_t = k_s
                P.op("dve", "tensor_tensor", [R_(6), R_(5)], [R_(1)], out=b_t, in0=kkn, in1=a_t, op=ALU.mult)
                P.op("dve", "scalar_tensor_tensor", [R_(0), R_(7), "pv"], [("tb16", 1)], out=tb16[1][:], in0=r_s,
                     scalar=pv[:, RK_ + j:RK_ + j + 1], in1=kp, op0=ALU.mult, op1=ALU.mult)
                bb = nb()
                P.op("pe", "matmul", ["bones_b", ("tb16", 1)], [("ps", bb)], psf(bb), lhsT=bones_b[:], rhs=tb16[1][:],
                     start=True, stop=True)
                P.op("dve", "tensor_tensor", [("ps", bb), R_(2)], [("BON", tb)], out=BON[:, cs_], in0=psf(bb), in1=v_s,
                     op=ALU.mult)
                P.op("dve", "tensor_tensor_scan", [R_(3), "scanm"], [R_(4)], out=cs, data0=scanm[:], data1=sgw, initial=0.0,
                     op0=ALU.mult, op1=ALU.add)
                P.op("dve", "tensor_tensor", [R_(4), R_(3)], [R_(3)], out=sgw, in0=cs, in1=sgw, op=ALU.subtract)
                P.op("act", "activation", [R_(3)], [R_(3)], out=sgw, in_=sgw, func=AF.Exp, scale=-C0)
                Eex = sgw
                P.op("act", "activation", [R_(4)], [R_(8)], out=Ep, in_=cs, func=AF.Exp, scale=-C0)
                P.op("act", "activation", [R_(4)], [R_(4)], out=cs, in_=cs, func=AF.Exp, scale=C0)
                Em = cs
                Eh = a_t
                P.op("dve", "tensor_tensor", [R_(4), R_(8)], [R_(5)], out=v3(Eh), in0=v3(Em),
                     in1=v3(Ep)[:, :, 63:64].to_broadcast([128, 8, 64]), op=ALU.mult)
                mark("r3")

            def expansions(tb):
                for hf in range(2):
                    ps_ = slice(hf * 64, (hf + 1) * 64)
                    cA = slice(hf * 64, (hf + 1) * 64)
                    cR = slice(128 + hf * 64, 128 + (hf + 1) * 64)
                    q = "dve" if hf == 0 else "pool"
                    ew("dve", "scalar_tensor_tensor", [R_(6), R_(3)], ["AR"], out=AR[ps_, :, cA], in0=v3(kkn)[ps_], scalar=-1.0,
                       in1=v3(Eex)[ps_], op0=ALU.mult, op1=ALU.mult)
                    ew(q, "tensor_tensor", [R_(0), R_(8)], ["AR"], out=AR[ps_, :, cR], in0=v3(r_s)[ps_], in1=v3(Ep)[ps_],
                       op=ALU.mult)
                    ew(q, "tensor_tensor", [R_(1), R_(4)], ["BT"], out=EXP["BT"][ps_, :, cA], in0=v3(b_t)[ps_],
                       in1=v3(Em)[ps_], op=ALU.mult)
                    ew(q, "tensor_tensor", [R_(7), R_(4)], ["KT"], out=EXP["KT"][ps_, :, cA], in0=v3(kp)[ps_],
                       in1=v3(Em)[ps_], op=ALU.mult)
                    ew(q, "tensor_tensor", [R_(1), R_(5)], ["BH"], out=EXP["BH"][ps_, :, cA], in0=v3(b_t)[ps_],
                       in1=v3(Eh)[ps_], op=ALU.mult)
                    ew(q, "tensor_tensor", [R_(7), R_(5)], ["KH"], out=EXP["KH"][ps_, :, cA], in0=v3(kp)[ps_],
                       in1=v3(Eh)[ps_], op=ALU.mult)
                    ew(q, "tensor_copy", [R_(2)], ["VE"], out=EXP["VE"][ps_, :, cA], in_=v3(v_s)[ps_])
                mark("r4")

            def chunk_groups(tb, prep_thunks, j=j):
                n_prep = len(prep_thunks)
                per_unit = (n_prep + 79) // 80
                prep_pos = [0]
                WCt = WC8[tb % 2]
                P.op("dve", "tensor_copy", [R_(8)], [("WC8", tb % 2)], out=WCt[:], in_=v3(Ep)[:, :, 63])
                for g4 in range(2):
                    gi = grp_ctr[0]
                    grp_ctr[0] += 1
                    pset = gi % 2
                    unitsA = []
                    for stage_grp in ((0, 1), (2,), (3,), (4,), (5,), (6,), (7,), (8,), (9, 10, 11)):
                        for s_ in range(4):
                            for stage_i in stage_grp:
                                unitsA.append((stage_i, s_))
                    ctxs = []
                    for s_ in range(4):
                        c = g4 * 4 + s_
                        ctxs.append(dict(c=c, tcol=tb * 512 + c * 64, ARc=AR[:, c, :],
                                         BTc=EXP["BT"][:, c, :], KTc=EXP["KT"][:, c, :], BHc=EXP["BH"][:, c, :],
                                         KHc=EXP["KH"][:, c, :], VEc=EXP["VE"][:, c, :], nmi=0))
                    first_grp = (tb == 0 and g4 == 0)
                    if first_grp:
                        P.op("dve", "memset", [], [("Sst", pset, 0)], SS[pset][0][:, 0:128], 0.0)
                        P.op("dve", "tensor_copy", ["ident_b"], [("Sst", pset, 0)], out=SS[pset][0][:, 128:256], in_=ident_b[:])

                    def unitA(stage_i, s_, tb=tb, pset=pset, ctxs=ctxs, WCt=WCt):
                        cx = ctxs[s_]
                        ARc, BTc, KTc, BHc, KHc, VEc = cx["ARc"], cx["BTc"], cx["KTc"], cx["BHc"], cx["KHc"], cx["VEc"]
                        if stage_i == 0:
                            b1, b2, b3 = nb(), nb(), nb()
                            P.op("pe", "matmul", ["BT", "AR"], [("ps", b1)], psf(b1, 0, 256), lhsT=BTc, rhs=ARc, start=True, stop=True)
                            P.op("pe", "matmul", ["KT", "AR"], [("ps", b2)], psf(b2, 0, 256), lhsT=KTc, rhs=ARc, start=True, stop=True)
                            P.op("pe", "matmul", ["AR", "BT"], [("ps", b3)], psf(b3, 0, 128), lhsT=ARc[:, 0:128], rhs=BTc,
                                 start=True, stop=True)
                            P.op("dve", "tensor_tensor", [("ps", b1), "MK"], [("NB", s_)], out=NB[s_][:, 0:256], in0=psf(b1, 0, 256),
                                 in1=MK[:, 0:256], op=ALU.mult)
                            P.op("dve", "tensor_tensor", [("ps", b2), "MK"], [("KB", s_)], out=KB[s_][:], in0=psf(b2, 0, 256),
                                 in1=MK[:, 0:256], op=ALU.mult)
                            P.op("dve", "tensor_tensor", [("ps", b3), "MK"], [("NM", s_, 0)], out=NM[s_][0][:, 128:256], in0=psf(b3, 0, 128),
                                 in1=MK[:, 256:384], op=ALU.mult)
                        elif stage_i == 1:
                            b4 = nb()
                            for ti, src in enumerate((VEc, KHc, ARc[:, 0:128], BHc)):
                                P.op("pe", "transpose", ["VE", "KH", "AR", "BH", "ident_b"], [("ps", b4)],
                                     out=psb(b4, ti * 128, (ti + 1) * 128), in_=src, identity=ident_b[:])
                            P.op("act", "copy", [("ps", b4)], [("TK", s_)], out=TK[s_][:], in_=psb(b4, 0, 256))
                            P.op("act", "copy", [("ps", b4)], [("Z", s_)], out=Zt[s_][:, 128:256], in_=psb(b4, 256, 384))
                            P.op("act", "copy", [("ps", b4)], [("NB", s_)], out=NB[s_][:, 256:384], in_=psb(b4, 384, 512))
                        elif stage_i == 2:
                            b5 = nb()
                            P.op("pe", "matmul", [("KB", s_), ("TK", s_)], [("ps", b5)], psf(b5, 0, 128), lhsT=KB[s_][:, 0:128],
                                 rhs=TK[s_][:, 0:128], start=True, stop=True)
                            P.op("act", "copy", [("ps", b5)], [("Z", s_)], out=Zt[s_][:, 0:128], in_=psf(b5, 0, 128))
                            P.op("pool", "tensor_copy", [("NB", s_)], [("NM", s_, 0)], out=NM[s_][0][:, 0:128], in_=NB[s_][:, 0:128])
                            cx["nmi"] = 0
                        elif 3 <= stage_i <= 8:
                            lvl = stage_i - 3
                            nmi = cx["nmi"]
                            Ncur, Mcur = NM[s_][nmi][:, 0:128], NM[s_][nmi][:, 128:256]
                            bz = nb()
                            P.op("pe", "matmul", [("NM", s_, nmi), ("Z", s_)], [("ps", bz)], psf(bz, 0, 256), lhsT=Ncur, rhs=Zt[s_][:],
                                 start=True, stop=True)
                            P.op("dve", "tensor_tensor", [("ps", bz), ("Z", s_)], [("Z", s_)], out=Zt[s_][:], in0=psf(bz, 0, 256),
                                 in1=Zt[s_][:], op=ALU.add)
                            if lvl < 5:
                                bs = nb()
                                P.op("pe", "matmul", [("NM", s_, nmi)], [("ps", bs)], psf(bs, 0, 128), lhsT=Mcur, rhs=Ncur, start=True, stop=True)
                                P.op("pe", "matmul", [("NM", s_, nmi)], [("ps", bs)], psf(bs, 128, 256), lhsT=Ncur, rhs=Mcur, start=True, stop=True)
                                P.op("act", "copy", [("ps", bs)], [("NM", s_, 1 - nmi)], out=NM[s_][1 - nmi][:], in_=psf(bs, 0, 256))
                                cx["nmi"] = 1 - nmi
                        elif stage_i == 9:
                            bqg = nb()
                            P.op("pe", "matmul", [("Z", s_), ("NB", s_)], [("ps", bqg)], psf(bqg, 0, 256), lhsT=Zt[s_][:, 128:256],
                                 rhs=NB[s_][:, 128:384], start=True, stop=True)
                            P.op("dve", "tensor_tensor", [("ps", bqg), "AR"], [("Qt", pset, s_)], out=Qt[pset][s_][:], in0=psf(bqg, 0, 128),
                                 in1=ARc[:, 128:256], op=ALU.add)
                            P.op("dve", "scalar_tensor_tensor", [("ps", bqg), "ident_f", ("WC8", tb % 2)], [("GT", pset, s_)],
                                 out=GT[pset][s_][:], in0=ident_f[:], scalar=WCt[:, cx["c"]:cx["c"] + 1], in1=psf(bqg, 128, 256),
                                 op0=ALU.mult, op1=ALU.add)
                        elif stage_i == 10:
                            bhh = nb()
                            P.op("pe", "matmul", [("NB", s_), ("Z", s_)], [("ps", bhh)], psf(bhh, 0, 128), lhsT=NB[s_][:, 256:384],
                                 rhs=Zt[s_][:, 0:128], start=True, stop=False)
                            P.op("pe", "matmul", [("TK", s_)], [("ps", bhh)], psf(bhh, 0, 128), lhsT=TK[s_][:, 128:256], rhs=TK[s_][:, 0:128],
                                 start=False, stop=True)
                            P.op("act", "copy", [("ps", bhh)], [("Hb", pset, s_)], out=Hb[pset][s_][:], in_=psf(bhh, 0, 128))
                        elif stage_i == 11:
                            bo = nb()
                            P.op("pe", "matmul", [("Z", s_), ("NB", s_)], [("ps", bo)], psf(bo, 0, 128), lhsT=Zt[s_][:, 0:128],
                                 rhs=NB[s_][:, 128:256], start=True, stop=False)
                            P.op("pe", "matmul", [("TK", s_), ("KB", s_)], [("ps", bo)], psf(bo, 0, 128), lhsT=TK[s_][:, 0:128],
                                 rhs=KB[s_][:, 128:256], start=False, stop=True)
                            for hf in range(2):
                                ps_ = slice(hf * 64, (hf + 1) * 64)
                                P.op("act", "copy", [("ps", bo)], [("OL", cx["tcol"] // 64)], out=OL[ps_, cx["tcol"]:cx["tcol"] + 64],
                                     in_=psf(bo, hf * 64, (hf + 1) * 64)[ps_])

                    def unitB(kind, s_, tb=tb, g4=g4, pset=pset, ctxs=ctxs):
                        cx = ctxs[s_]
                        Sin = SS[pset][s_]
                        if kind == 0:
                            if s_ < 3:
                                Sout, sres = SS[pset][s_ + 1], ("Sst", pset, s_ + 1)
                            else:
                                Sout, sres = SS[1 - pset][0], ("Sst", 1 - pset, 0)
                            bsn = nb()
                            P.op("pe", "matmul", [("GT", pset, s_), ("Sst", pset, s_)], [("ps", bsn)], psf(bsn, 0, 128),
                                 lhsT=GT[pset][s_][:], rhs=Sin[:, 0:128], start=True, stop=False)
                            P.op("pe", "matmul", [("Hb", pset, s_), "ident_b"], [("ps", bsn)], psf(bsn, 0, 128),
                                 lhsT=ident_b[:], rhs=Hb[pset][s_][:], start=False, stop=True)
                            P.op("pe", "matmul", [("GT", pset, s_), ("Sst", pset, s_)], [("ps", bsn)], psf(bsn, 128, 256),
                                 lhsT=GT[pset][s_][:], rhs=Sin[:, 128:256], start=True, stop=True)
                            P.op("act", "copy", [("ps", bsn)], [sres], out=Sout[:], in_=psf(bsn, 0, 256))
                            if tb == 3 and g4 == 1 and s_ == 3 and "nosf" not in stage:
                                P.op("dve", "tensor_copy", [("ps", bsn)], ["SF"], out=SF[:], in_=psf(bsn, 0, 256))
                        else:
                            tcol = cx["tcol"]
                            bo = nb()
                            P.op("pe", "matmul", [("Sst", pset, s_), ("Qt", pset, s_)], [("ps", bo)], psf(bo, 0, 128),
                                 lhsT=Sin[:, 0:128], rhs=Qt[pset][s_][:], start=True, stop=True)
                            bh_ = nb()
                            P.op("pe", "matmul", [("Sst", pset, s_), ("Qt", pset, s_)], [("ps", bh_)], psf(bh_, 0, 128),
                                 lhsT=Sin[:, 128:256], rhs=Qt[pset][s_][:], start=True, stop=True)
                            for hf in range(2):
                                ps_ = slice(hf * 64, (hf + 1) * 64)
                                P.op("dve", "tensor_tensor", [("ps", bo), ("OL", tcol // 64)], [("OL", tcol // 64)],
                                     out=OL[ps_, tcol:tcol + 64], in0=psf(bo, hf * 64, (hf + 1) * 64)[ps_],
                                     in1=OL[ps_, tcol:tcol + 64], op=ALU.add)
                                P.op("act", "copy", [("ps", bh_)], [("QH", tcol // 64)], out=QH[ps_, tcol:tcol + 64],
                                     in_=psf(bh_, hf * 64, (hf + 1) * 64)[ps_])

                    unitsB_prev = pendingB[0]
                    nA, nB = len(unitsA), len(unitsB_prev)
                    bi_ = 0
                    for ai, (st_i, s_) in enumerate(unitsA):
                        unitA(st_i, s_)
                        for _ in range(per_unit):
                            if prep_pos[0] < n_prep:
                                P.replay(prep_thunks[prep_pos[0]])
                                prep_pos[0] += 1
                        if nB and ai % 6 == 5 and bi_ < nB:
                            unitsB_prev[bi_]()
                            bi_ += 1
                    while bi_ < nB:
                        unitsB_prev[bi_]()
                        bi_ += 1
                    newB = []
                    for s_ in range(4):
                        newB.append(lambda s_=s_, f=unitB: f(0, s_))
                    for s_ in range(4):
                        newB.append(lambda s_=s_, f=unitB: f(1, s_))
                    pendingB[0] = newB
                while prep_pos[0] < n_prep:
                    P.replay(prep_thunks[prep_pos[0]])
                    prep_pos[0] += 1

            prep_main(0)
            expansions(0)
            for tb in range(4):
                thunks = []
                if tb < 3:
                    P.capture = []
                    prep_main(tb + 1)
                    thunks = P.capture
                    P.capture = None
                chunk_groups(tb, thunks)
                if tb < 3:
                    expansions(tb + 1)
            for f in pendingB[0]:
                f()
            pendingB[0] = []

            mark("r5")
            if use_cc:
                P.dma("sp", ["SF"], [("ccin", j)], cc_in[j].ap(), SF[:])
                P.coll("pool", [("ccin", j)], [("ccg", j)], "AllGather", ALU.bypass,
                       replica_groups=[[0, 1, 2, 3], [4, 5, 6, 7]], ins=[cc_in[j].ap().opt()], outs=[cc_out[j].ap().opt()])
            else:
                P.dma("sp", ["SF"], [("sfdbg", j)], sf_dbg.ap()[:, j * 256:(j + 1) * 256], SF[:])
            P.dma("sp", [("ccg", j)], GA_RES, GA[:], ccg_view(j))
            for i in range(3):
                bt_ = nb()
                P.op("pe", "transpose", GA_RES + ["ident_f"], [("ps", bt_)], out=psf(bt_, 0, 128), in_=GA[:, i, 128:256],
                     identity=ident_f[:])
                P.op("dve", "tensor_scalar", [("ps", bt_), "fm"] + GA_RES, GA_RES, out=LT[:, i, :], in0=psf(bt_, 0, 128),
                     scalar1=fm[:, i:i + 1], scalar2=None, op0=ALU.mult)
                P.op("dve", "scalar_tensor_tensor", ["ident_f", "fm"] + GA_RES, GA_RES, out=LT[:, i, :], in0=ident_f[:],
                     scalar=fm[:, 4 + i:5 + i], in1=LT[:, i, :], op0=ALU.mult, op1=ALU.add)
                P.op("dve", "tensor_scalar", ["fm"] + GA_RES, GA_RES, out=GA[:, i, 0:128], in0=GA[:, i, 0:128],
                     scalar1=fm[:, i:i + 1], scalar2=None, op0=ALU.mult)
            P.op("dve", "tensor_copy", GA_RES, ["Sst"], out=Sst[:], in_=GA[:, 0, 0:128])
            for i in range(1, 3):
                bm_ = nb()
                P.op("pe", "matmul", GA_RES + ["Sst"], [("ps", bm_)], psf(bm_, 0, 128), lhsT=LT[:, i, :], rhs=Sst[:],
                     start=True, stop=False)
                P.op("pe", "matmul", GA_RES + ["ident_f"], [("ps", bm_)], psf(bm_, 0, 128), lhsT=ident_f[:], rhs=GA[:, i, 0:128],
                     start=False, stop=True)
                P.op("dve", "tensor_copy", [("ps", bm_)], ["Sst"], out=Sst[:], in_=psf(bm_, 0, 128))
            mark("r6")
            P.op("act", "copy", ["Sst"], ["Sstb"], out=Sstb[:], in_=Sst[:])
            for tb in range(4):
                cs_ = slice(tb * 512, (tb + 1) * 512)
                O_, cen, sq = T[4][:], T[5][:], T[6][:]
                bf_ = nb()
                P.op("pe", "matmul", ["Sstb"] + [("QH", tb * 8 + c_) for c_ in range(8)], [("ps", bf_)], psf(bf_), lhsT=Sstb[:], rhs=QH[:, cs_], start=True, stop=True)
                P.op("dve", "tensor_tensor", [("ps", bf_)] + [("OL", tb * 8 + c_) for c_ in range(8)], [("T", 4)], out=O_, in0=psf(bf_), in1=OL[:, cs_], op=ALU.add)
                bmu = nb()
                P.op("pe", "matmul", ["bones_s", ("T", 4)], [("ps", bmu)], psf(bmu), lhsT=bones_s[:], rhs=O_, start=True, stop=True)
                P.op("dve", "tensor_tensor", [("ps", bmu), ("T", 4)], [("T", 5)], out=cen, in0=O_, in1=psf(bmu), op=ALU.subtract)
                P.op("act", "activation", [("T", 5)], [("T", 6)], out=sq, in_=cen, func=AF.Square)
                bvar = nb()
                P.op("pe", "matmul", ["bones_s", ("T", 6)], [("ps", bvar)], psf(bvar), lhsT=bones_s[:], rhs=sq, start=True, stop=True)
                P.op("act", "activation", [("ps", bvar), "gneps"], [("T", 6)], out=sq, in_=psf(bvar), func=AF.Sqrt,
                     bias=gneps[:, 0:1], scale=1.0)
                P.op("dve", "reciprocal", [("T", 6)], [("T", 6)], out=sq, in_=sq)
                P.op("dve", "tensor_tensor", [("T", 5), ("T", 6)], [("T", 5)], out=cen, in0=cen, in1=sq, op=ALU.mult)
                P.op("dve", "tensor_scalar", [("T", 5), "pv"], [("T", 5)], out=cen, in0=cen, scalar1=pv[:, LNW + j:LNW + j + 1],
                     scalar2=pv[:, LNB + j:LNB + j + 1], op0=ALU.mult, op1=ALU.add)
                P.op("dve", "tensor_tensor", [("T", 5), ("BON", tb)], [("T", 5)], out=cen, in0=cen, in1=BON[:, cs_], op=ALU.add)
                bg_ = nb()
                P.op("pe", "matmul", ["G2", ("GL", tb)], [("ps", bg_)], psf(bg_), lhsT=G2[:, j * 128:(j + 1) * 128], rhs=GL[:, cs_],
                     start=True, stop=True)
                P.op("dve", "tensor_tensor", [("T", 5), ("ps", bg_)], [("MX", tb), ("X", 0)], out=MX[:, cs_], in0=cen, in1=psf(bg_), op=ALU.mult)
            mark("r7")
            wi_ = load_wout(j * 128)
            wout_accum([(lambda n: MX[:, (n - 1) * 128:n * 128], wi_, lambda n: [("MX", (n - 1) // 4)])])
        barrier()

    if not stage.startswith(("noffn", "mix")):
        load_gain("norm_ffn2")
        for i in range(1, NT):
            rmsnorm_tile(i, norm_to_HT)
        ffn("ffn2", range(1, NT))

    P.enabled = True
    barrier()
    A.off = phase_mark
    ob = [sb(f"ob{i}", [128, D]) for i in range(2)]
    load_gain("norm_final")

    def norm_to_out(i, col):
        b = i % 2
        P.op("dve", "scalar_tensor_tensor", [("X", i), ("rstd", col), "gbc"], [("ob", b)],
             out=ob[b][:], in0=X[:, i, :], scalar=rstd[:, col:col + 1],
             in1=gbc[:], op0=ALU.mult, op1=ALU.mult)
        P.dma("sp", [("ob", b)], [("out", i)], out.ap()[(i - 1) * 128:i * 128, :], ob[b][:])

    for i in range(1, NT):
        rmsnorm_tile(i, norm_to_out)
    P.wait_all("sp")

    P.emit()
    es.close()
    return nc


_NC_CACHE = {}


def _prep_inputs(inputs):
    x = np.ascontiguousarray(inputs["x"], dtype=np.float32)
    B, S, _ = x.shape
    shared = {"ident": np.eye(128, dtype=np.float32)}
    qi = np.arange(128)[:, None]
    kj = np.arange(256)[None, :]
    valid = np.where(kj < 128, kj > qi, (kj - 128) <= qi)
    maskA = np.where(valid, 0.0, NEG).astype(np.float32)
    mask0 = maskA.copy()
    mask0[:, :128] = NEG
    shared["maskA"] = maskA
    shared["w_in"] = np.ascontiguousarray(inputs["w_in"][0], dtype=np.float32)
    shared["w_out"] = np.ascontiguousarray(inputs["w_out"][0], dtype=np.float32)
    shared["b_in_attn"] = np.ascontiguousarray(inputs["b_in_attn"].reshape(768, 1), dtype=np.float32)
    idx = np.arange(128)
    hh, ii = idx // 64, idx % 64
    same = hh[:, None] == hh[None, :]
    su = same & (ii[:, None] < ii[None, :])
    ui = same & (ii[:, None] <= ii[None, :])
    sl = same & (ii[:, None] > ii[None, :])
    shared["cmask"] = np.concatenate([su, ui, sl], axis=1).astype(np.float32)
    shared["bones"] = same.astype(np.float32)
    scanm = np.ones((1, 512), np.float32)
    scanm[:, ::64] = 0.0
    shared["scanm"] = scanm
    shared["rwkv_shift_mix"] = np.ascontiguousarray(inputs["rwkv_shift_mix"].reshape(1792, 1), dtype=np.float32)
    for nm in ("rwkv_w0", "rwkv_a0", "rwkv_k_k", "rwkv_k_a", "rwkv_r_k", "rwkv_ln_w", "rwkv_ln_b"):
        shared[nm] = np.ascontiguousarray(inputs[nm].reshape(512, 1), dtype=np.float32)
    shared["rwkv_w2"] = np.ascontiguousarray(inputs["rwkv_w2"][0], dtype=np.float32)
    shared["rwkv_a2"] = np.ascontiguousarray(inputs["rwkv_a2"][0], dtype=np.float32)
    shared["rwkv_g2"] = np.ascontiguousarray(inputs["rwkv_g2"][0], dtype=np.float32)
    shared["attn_sinks"] = np.ascontiguousarray(inputs["attn_sinks"].reshape(1, 8), dtype=np.float32)
    for nm in ("ffn1_gate", "ffn1_up", "ffn1_down", "ffn2_gate", "ffn2_up", "ffn2_down"):
        shared[nm] = np.ascontiguousarray(inputs[nm][0], dtype=np.float32)
    for nm in ("norm_ffn1", "norm_mix", "norm_ffn2"):
        shared[nm] = np.ascontiguousarray(inputs[nm].reshape(1, D), dtype=np.float32)
    shared["norm_final"] = np.ascontiguousarray(inputs["norm_final"].reshape(1, D), dtype=np.float32)
    in_maps = []
    for c in range(NCORES):
        b, s = c // 4, c % 4
        xs = np.zeros((NTH, D), np.float32)
        if s > 0:
            xs[:128] = x[b, s * NTOK - 128:s * NTOK]
        xs[128:] = x[b, s * NTOK:(s + 1) * NTOK]
        m = dict(shared)
        m["xs"] = xs
        m["mask1"] = mask0 if s == 0 else maskA
        fmv = np.zeros((1, 8), np.float32)
        for i in range(3):
            fmv[0, i] = 1.0 if i < s else 0.0
            fmv[0, 4 + i] = 1.0 - fmv[0, i]
        m["fm"] = fmv
        in_maps.append(m)
    return in_maps


STAGE = "full"


def kernel(**inputs):
    if STAGE not in _NC_CACHE:
        _NC_CACHE[STAGE] = build(STAGE)
    nc = _NC_CACHE[STAGE]
    in_maps = _prep_inputs(inputs)
    if "ccin" in STAGE:
        for m in in_maps:
            m["ccg"] = np.zeros((4 * 128, 1024), np.float32)
        res = run_bass_kernel_spmd(nc, in_maps, core_ids=list(range(NCORES)))
        ccg = np.concatenate([np.asarray(r["sf_dbg"], dtype=np.float32).reshape(128, 1024) for r in res.results], axis=0)
        for c, m in enumerate(in_maps):
            m["ccg"] = ccg[(c // 4) * 512:(c // 4 + 1) * 512]
    res = run_bass_kernel_spmd(nc, in_maps, core_ids=list(range(NCORES)))
    outs = [np.asarray(r["out"], dtype=np.float32).reshape(NTOK, D) for r in res.results]
    full = np.stack([np.concatenate(outs[b * 4:(b + 1) * 4], axis=0) for b in range(2)], axis=0)
    return full
```

```python
import contextlib
import numpy as np
import concourse.bass as bass
import concourse.mybir as mybir
from concourse.bass_utils import run_bass_kernel_spmd

F32 = mybir.dt.float32
BF16 = mybir.dt.bfloat16
AF = mybir.ActivationFunctionType
ALU = mybir.AluOpType
AX = mybir.AxisListType

D = 1024
DFF = 2816
NTOK = 2048
NT = 17
NTH = NT * 128
NCORES = 8
SEM_LIM = 20000
NLANES = 12
CC_INC = 1


class _Op:
    __slots__ = ("q", "fn", "deps", "need_inc", "semidx", "semval", "lane", "laneval", "idx")


class Prog:
    CE = ("pe", "act", "dve", "pool")
    QS = ("pe", "act", "dve", "pool", "sp")

    def __init__(self, nc):
        self.nc = nc
        self.ops = {q: [] for q in self.QS}
        self.last_w = {}
        self.readers = {}
        self.lane_rr = {q: 0 for q in self.QS}
        self.lane_cnt = {}
        self.lane_last = {}
        self.enabled = True
        self.capture = None
        self._open_grp = None

    def _deps(self, q, reads, writes, is_dma):
        deps = set()
        for r in reads:
            ev = self.last_w.get(r)
            if ev is not None:
                deps.add(ev)
        for w in writes:
            ev = self.last_w.get(w)
            if ev is not None:
                deps.add(ev)
            for ev in self.readers.get(w, ()):
                deps.add(ev)
        if q == "pe" and not is_dma:
            deps = {e for e in deps if not (e[0] == "c" and e[1] == "pe")}
        return deps

    def _commit(self, ev, reads, writes):
        for r in reads:
            self.readers.setdefault(r, []).append(ev)
        for w in writes:
            self.last_w[w] = ev
            self.readers[w] = []

    def replay(self, th):
        kind, a, kw = th
        if kind == "op":
            self.op(*a, **kw)
        elif kind == "grp":
            for it in a:
                self.op(*it[1], **it[2])
        else:
            self.dma(*a)

    def op(self, q, name, reads, writes, *args, **kw):
        if not self.enabled:
            return None
        if self.capture is not None:
            item = ("op", (q, name, reads, writes) + args, kw)
            if self._open_grp is not None:
                self._open_grp.append(item)
                if kw.get("stop", True):
                    self._open_grp = None
            elif name == "matmul" and not kw.get("stop", True):
                self._open_grp = [item]
                self.capture.append(("grp", self._open_grp, {}))
            else:
                self.capture.append(item)
            return None
        o = _Op()
        fn = (name, args, kw)
        writes = list(writes) + [r for r in reads if isinstance(r, tuple) and r[0] == "ps"]
        o.q, o.fn, o.need_inc, o.lane = q, fn, False, None
        o.deps = self._deps(q, reads, writes, False)
        o.idx = len(self.ops[q])
        self.ops[q].append(o)
        self._commit(("c", q, o.idx), reads, writes)
        return o

    def dma(self, q, reads, writes, out, in_):
        if not self.enabled:
            return _Op()
        if self.capture is not None:
            self.capture.append(("dma", (q, reads, writes, out, in_), {}))
            return _Op()
        o = _Op()
        fn = ("dma_start", (), {"out": out, "in_": in_})
        o.q, o.fn, o.need_inc = q, fn, False
        o.deps = self._deps(q, reads, writes, True)
        lane = self.lane_rr[q]
        self.lane_rr[q] = (lane + 1) % NLANES
        key = (q, lane)
        prev = self.lane_last.get(key)
        if prev is not None:
            o.deps.add(prev)
        cnt = self.lane_cnt.get(key, 0) + 16
        self.lane_cnt[key] = cnt
        o.lane, o.laneval = key, cnt
        o.idx = len(self.ops[q])
        self.ops[q].append(o)
        ev = ("d", key, cnt)
        self.lane_last[key] = ev
        self._commit(ev, reads, writes)
        return o

    def coll(self, q, reads, writes, *args, **kw):
        if not self.enabled:
            return _Op()
        o = _Op()
        o.q, o.fn, o.need_inc = q, ("collective_compute", args, kw), False
        o.deps = self._deps(q, reads, writes, True)
        key = ("cc", len(self.lane_cnt))
        self.lane_cnt[key] = CC_INC
        o.lane, o.laneval = key, CC_INC
        o.idx = len(self.ops[q])
        self.ops[q].append(o)
        ev = ("d", key, CC_INC)
        self.lane_last[key] = ev
        self._commit(ev, reads, writes)
        return o

    def barrier(self, q):
        if not self.enabled:
            return
        o = _Op()
        o.q, o.fn, o.need_inc, o.lane = q, None, False, None
        o.deps = set(self.lane_last.values())
        for e in self.CE:
            if self.ops[e]:
                last = [x for x in self.ops[e] if x.fn is not None and x.lane is None]
                if last:
                    o.deps.add(("c", e, last[-1].idx))
        o.idx = len(self.ops[q])
        self.ops[q].append(o)

    def wait_all(self, q):
        o = _Op()
        o.q, o.fn, o.need_inc, o.lane = q, None, False, None
        o.deps = set(self.lane_last.values())
        o.idx = len(self.ops[q])
        self.ops[q].append(o)

    def emit(self):
        nc = self.nc
        for q in self.QS:
            for o in self.ops[q]:
                for ev in o.deps:
                    if ev[0] == "c":
                        self.ops[ev[1]][ev[2]].need_inc = True
        nsem = {}
        for e in self.CE:
            cnt = 0
            for o in self.ops[e]:
                if o.need_inc:
                    o.semidx, o.semval = cnt // SEM_LIM, cnt % SEM_LIM + 1
                    cnt += 1
            nsem[e] = (cnt + SEM_LIM - 1) // SEM_LIM
        with contextlib.ExitStack() as st:
            sems = {}
            for e in self.CE:
                for k in range(nsem[e]):
                    sems[("c", e, k)] = st.enter_context(nc.semaphore(f"s_{e}_{k}"))
            for key in self.lane_cnt:
                sems[("d",) + key] = st.enter_context(nc.semaphore(f"l_{key[0]}_{key[1]}"))
            block = st.enter_context(nc.Block())

            def run(q, eng):
                waited = {}
                for o in self.ops[q]:
                    need = {}
                    for ev in o.deps:
                        if ev[0] == "c":
                            src = self.ops[ev[1]][ev[2]]
                            k, v = ("c", ev[1], src.semidx), src.semval
                        else:
                            k, v = ("d",) + ev[1], ev[2]
                        if need.get(k, 0) < v:
                            need[k] = v
                    for k, v in need.items():
                        if waited.get(k, 0) < v:
                            eng.wait_ge(sems[k], v)
                            waited[k] = v
                    if o.fn is None:
                        continue
                    name, args, kw = o.fn
                    ins = getattr(eng, name)(*args, **kw)
                    if o.lane is not None:
                        ins.then_inc(sems[("d",) + o.lane], CC_INC if o.lane[0] == "cc" else 16)
                    elif o.need_inc:
                        ins.then_inc(sems[("c", q, o.semidx)], 1)

            if self.ops["pe"]:
                @block.tensor
                def _(eng):
                    run("pe", eng)
            if self.ops["act"]:
                @block.scalar
                def _(eng):
                    run("act", eng)
            if self.ops["dve"]:
                @block.vector
                def _(eng):
                    run("dve", eng)
            if self.ops["pool"]:
                @block.gpsimd
                def _(eng):
                    run("pool", eng)
            if self.ops["sp"]:
                @block.sync
                def _(eng):
                    run("sp", eng)


FFN_GROUPS = [(0, 6), (6, 6), (12, 5), (17, 5)]
SB_BASE = 16512
SB_LIMIT = 16512 + 212800
_DTSZ = {F32: 4, BF16: 2}
NEG = -30000.0


class Arena:
    def __init__(self, nc):
        self.nc, self.off, self.n = nc, SB_BASE, 0

    def alloc(self, name, shape, dt=F32):
        n = _DTSZ[dt]
        for d in shape[1:]:
            n *= d
        off = (self.off + 31) // 32 * 32
        assert off + n <= SB_LIMIT, (name, off + n - SB_LIMIT)
        self.off = off + n
        self.n += 1
        return self.nc.alloc_sbuf_tensor_at(f"{name}_{self.n}", list(shape), dt, offset=off)


def build(stage="full"):
    nc = bass.Bass("TRN2", target_bir_lowering=False)
    P = Prog(nc)
    es = contextlib.ExitStack()
    A = Arena(nc)
    sb = A.alloc

    def dram(name, shape, kind="ExternalInput", dt=F32):
        return nc.dram_tensor(name, list(shape), dt, kind=kind)

    xs = dram("xs", [NTH, D])
    out = dram("out", [NTOK, D], kind="ExternalOutput")
    ident_d = dram("ident", [128, 128])
    maskA_d = dram("maskA", [128, 256])
    mask1_d = dram("mask1", [128, 256])
    wdram = {}
    for nm, shp in (("ffn1_gate", [D, DFF]), ("ffn1_up", [D, DFF]), ("ffn1_down", [DFF, D]),
                    ("ffn2_gate", [D, DFF]), ("ffn2_up", [D, DFF]), ("ffn2_down", [DFF, D]),
                    ("w_in", [D, 2560]), ("w_out", [D, D])):
        wdram[nm] = dram(nm, shp)
    vec_d = {}
    for nm in ("norm_ffn1", "norm_mix", "norm_ffn2", "norm_final"):
        vec_d[nm] = dram(nm, [1, D])
    b_attn_d = dram("b_in_attn", [768, 1])
    sinks_d = dram("attn_sinks", [1, 8])
    rw_d = {}
    rw_d["rwkv_shift_mix"] = dram("rwkv_shift_mix", [1792, 1])
    for nm in ("rwkv_w0", "rwkv_a0", "rwkv_k_k", "rwkv_k_a", "rwkv_r_k", "rwkv_ln_w", "rwkv_ln_b"):
        rw_d[nm] = dram(nm, [512, 1])
    rw_d["rwkv_w2"] = dram("rwkv_w2", [64, 512])
    rw_d["rwkv_a2"] = dram("rwkv_a2", [64, 512])
    rw_d["rwkv_g2"] = dram("rwkv_g2", [128, 512])
    cmask_d = dram("cmask", [128, 384])
    bones_d = dram("bones", [128, 128])
    scanm_d = dram("scanm", [1, 512])
    fm_d = dram("fm", [1, 8])
    use_cc = "ccin" not in stage
    if use_cc:
        cc_in = [dram(f"cc_in{j}", [128, 256], kind="Internal") for j in range(4)]
        cc_out = [dram(f"cc_out{j}", [4 * 128, 256], kind="Internal") for j in range(4)]

        def ccg_view(j):
            return cc_out[j].ap().rearrange("(r p) f -> p r f", p=128)
    else:
        ccg_d = dram("ccg", [4 * 128, 1024])
        sf_dbg = dram("sf_dbg", [128, 1024], kind="ExternalOutput")

        def ccg_view(j):
            return ccg_d.ap()[:, j * 256:(j + 1) * 256].rearrange("(r p) f -> p r f", p=128)

    X = sb("X", [128, NT, D])
    HT = sb("HT", [128, 8, NTH], BF16)
    ss = sb("ss", [128, 4 * NT])
    rstd = sb("rstd", [128, 4 * NT])
    eps_t = sb("eps_t", [128, 1])
    ident_f = sb("ident_f", [128, 128])
    ident_b = sb("ident_b", [128, 128], BF16)
    norm_mark = A.off
    gbc = sb("gbc", [128, D])
    hb = [sb(f"hb{i}", [128, D], BF16) for i in range(2)]
    junk = sb("junk", [128, D], BF16)
    phase_mark = A.off

    PSALL = es.enter_context(nc.psum_tensor("psall", [128, 4096], F32))
    PSALLB = PSALL.bitcast(BF16)

    def psf(bank, a=0, b=512):
        return PSALL[:, bank * 512 + a:bank * 512 + b]

    def psb(bank, a=0, b=1024):
        return PSALLB[:, bank * 1024 + a:bank * 1024 + b]

    def barrier():
        for q in ("pe", "act", "dve", "pool", "sp"):
            P.barrier(q)

    def mark(name):
        if ("stop_" + name + "_") in (stage + "_"):
            P.enabled = False

    P.dma("sp", [], ["ident_f"], ident_f[:], ident_d.ap())
    P.op("act", "copy", ["ident_f"], ["ident_b"], out=ident_b[:], in_=ident_f[:])
    P.op("dve", "memset", [], ["ss"], ss[:], 0.0)
    P.op("dve", "memset", [], ["eps_t"], eps_t[:], 1e-5)

    norm_ctr = [0]

    def load_gain(name):
        P.dma("sp", [], ["gbc"], gbc[:], vec_d[name].ap().partition_broadcast(128))

    def rmsnorm_tile(i, dst_fn):
        k = norm_ctr[0]
        norm_ctr[0] += 1
        col = k % (4 * NT)
        P.op("act", "activation", [("X", i), "ss"], ["junk", ("ss", col)],
             out=junk[:], in_=X[:, i, :], func=AF.Square, accum_out=ss[:, col:col + 1])
        P.op("act", "activation", [("ss", col), "eps_t"], [("rstd", col)],
             out=rstd[:, col:col + 1], in_=ss[:, col:col + 1], func=AF.Sqrt, bias=eps_t[:, 0:1], scale=1.0 / D)
        P.op("dve", "reciprocal", [("rstd", col)], [("rstd", col)],
             out=rstd[:, col:col + 1], in_=rstd[:, col:col + 1])
        dst_fn(i, col)

    def norm_to_HT(i, col):
        b = i % 2
        P.op("dve", "scalar_tensor_tensor", [("X", i), ("rstd", col), "gbc"], [("hb", b)],
             out=hb[b][:], in0=X[:, i, :], scalar=rstd[:, col:col + 1],
             in1=gbc[:], op0=ALU.mult, op1=ALU.mult)
        bank = 4 + (i % 4)
        for c in range(8):
            P.op("pe", "transpose", [("hb", b), "ident_b"], [("ps", bank)],
                 out=psb(bank, c * 128, (c + 1) * 128),
                 in_=hb[b][:, c * 128:(c + 1) * 128], identity=ident_b[:])
        P.op("act", "copy", [("ps", bank)], [("HT", i)],
             out=HT[:, :, i * 128:(i + 1) * 128],
             in_=psb(bank).rearrange("p (c t) -> p c t", c=8))

    def ffn(prefix, tiles):
        A.off = phase_mark
        Wg = [sb(f"Wg{i}", [128, 8, 768], BF16) for i in range(2)]
        Wu = [sb(f"Wu{i}", [128, 8, 768], BF16) for i in range(2)]
        Wd = [sb(f"Wd{i}", [128, 6, D], BF16) for i in range(2)]
        sg = [sb(f"sg{i}", [128, 256]) for i in range(2)]
        aT = [sb(f"aT{i}", [128, 256], BF16) for i in range(3)]
        gate, up, down = wdram[prefix + "_gate"], wdram[prefix + "_up"], wdram[prefix + "_down"]
        blocks = []
        t = list(tiles)
        if len(t) % 2 == 1:
            blocks.append(t[:1])
            t = t[1:]
        for j in range(0, len(t), 2):
            blocks.append(t[j:j + 2])
        cnt = 0
        for gi, (f0, nf) in enumerate(FFN_GROUPS):
            wb = gi % 2
            w = nf * 128
            for c in range(8):
                P.dma("pool", [], [("Wg", wb)], Wg[wb][:, c, 0:w],
                      gate.ap()[c * 128:(c + 1) * 128, f0 * 128:f0 * 128 + w])
            for c in range(8):
                P.dma("pool", [], [("Wu", wb)], Wu[wb][:, c, 0:w],
                      up.ap()[c * 128:(c + 1) * 128, f0 * 128:f0 * 128 + w])
            for c in range(nf):
                P.dma("pool", [], [("Wd", wb)], Wd[wb][:, c, :],
                      down.ap()[(f0 + c) * 128:(f0 + c + 1) * 128, :])
            for blk in blocks:
                T = 128 * len(blk)
                tok0 = blk[0] * 128
                htr = [("HT", t_) for t_ in blk]

                def gu(fc, k):
                    gb = 4 + (k % 2)
                    ub = 6 + (k % 2)
                    for kc in range(8):
                        P.op("pe", "matmul", [("Wg", wb)] + htr, [("ps", gb)],
                             psf(gb, 0, T), lhsT=Wg[wb][:, kc, fc * 128:(fc + 1) * 128],
                             rhs=HT[:, kc, tok0:tok0 + T], start=(kc == 0), stop=(kc == 7))
                    for kc in range(8):
                        P.op("pe", "matmul", [("Wu", wb)] + htr, [("ps", ub)],
                             psf(ub, 0, T), lhsT=Wu[wb][:, kc, fc * 128:(fc + 1) * 128],
                             rhs=HT[:, kc, tok0:tok0 + T], start=(kc == 0), stop=(kc == 7))
                    s_ = k % 2
                    a_ = k % 3
                    P.op("act", "activation", [("ps", gb)], [("sg", s_)],
                         out=sg[s_][:, 0:T], in_=psf(gb, 0, T), func=AF.Silu)
                    P.op("dve", "tensor_tensor", [("sg", s_), ("ps", ub)], [("aT", a_)],
                         out=aT[a_][:, 0:T], in0=sg[s_][:, 0:T], in1=psf(ub, 0, T), op=ALU.mult)

                def dn_(fc, k):
                    a_ = k % 3
                    for ti, tt in enumerate(blk):
                        for dh in range(2):
                            bank = 2 * ti + dh
                            P.op("pe", "matmul", [("aT", a_), ("Wd", wb)], [("ps", bank)],
                                 psf(bank), lhsT=aT[a_][:, ti * 128:(ti + 1) * 128],
                                 rhs=Wd[wb][:, fc, dh * 512:(dh + 1) * 512],
                                 start=(fc == 0), stop=(fc == nf - 1))

                ks = list(range(cnt, cnt + nf))
                cnt += nf
                gu(0, ks[0])
                for fc in range(nf):
                    if fc + 1 < nf:
                        gu(fc + 1, ks[fc + 1])
                    dn_(fc, ks[fc])
                for ti, tt in enumerate(blk):
                    for dh in range(2):
                        bank = 2 * ti + dh
                        P.op("dve", "scalar_tensor_tensor", [("ps", bank), ("X", tt)], [("X", tt)],
                             out=X[:, tt, dh * 512:(dh + 1) * 512], in0=psf(bank), scalar=0.5,
                             in1=X[:, tt, dh * 512:(dh + 1) * 512], op0=ALU.mult, op1=ALU.add)

    load_gain("norm_ffn1")
    for i in range(NT):
        P.dma("sp", [], [("X", i)], X[:, i, :], xs.ap()[i * 128:(i + 1) * 128, :])
    if not stage.startswith(("noffn", "mix")):
        for i in range(NT):
            rmsnorm_tile(i, norm_to_HT)
        ffn("ffn1", range(NT))
        barrier()

    w_in, w_out = wdram["w_in"], wdram["w_out"]
    if not stage.startswith("noffn"):
        load_gain("norm_mix")
        for i in range(NT):
            rmsnorm_tile(i, norm_to_HT)
        barrier()
        A.off = norm_mark
        WT = [sb(f"WT{i}", [128, 8, 128], BF16) for i in range(3)]
        wt_ctr = [0]

        def load_wtile(colspec):
            i = wt_ctr[0] % 3
            wt_ctr[0] += 1
            for kc in range(8):
                off = 0
                for (c0, n) in colspec:
                    P.dma("pool", [], [("WT", i)], WT[i][:, kc, off:off + n],
                          w_in.ap()[kc * 128:(kc + 1) * 128, c0:c0 + n])
                    off += n
            return i

        def proj(wi, tok0, T, bank, off=0):
            for kc in range(8):
                P.op("pe", "matmul", [("WT", wi)] + [("HT", t_) for t_ in range(tok0 // 128, (tok0 + T + 127) // 128)],
                     [("ps", bank)], psf(bank, off, off + T), lhsT=WT[wi][:, kc, :],
                     rhs=HT[:, kc, tok0:tok0 + T], start=(kc == 0), stop=(kc == 7))

        WO = [sb(f"WO{i}", [128, D], BF16) for i in range(2)]
        wo_ctr = [0]

        def load_wout(row0):
            i = wo_ctr[0] % 2
            wo_ctr[0] += 1
            P.dma("pool", [], [("WO", i)], WO[i][:], w_out.ap()[row0:row0 + 128, :])
            return i

        def wout_accum(srcs):
            for n in range(1, NT):
                for dh in range(2):
                    bank = 2 * (n % 2) + dh
                    for j, (fn, wi, res) in enumerate(srcs):
                        P.op("pe", "matmul", res(n) + [("WO", wi)], [("ps", bank)], psf(bank),
                             lhsT=fn(n), rhs=WO[wi][:, dh * 512:(dh + 1) * 512],
                             start=(j == 0), stop=(j == len(srcs) - 1))
                    P.op("dve", "tensor_tensor", [("ps", bank), ("X", n)], [("X", n)],
                         out=X[:, n, dh * 512:(dh + 1) * 512], in0=psf(bank),
                         in1=X[:, n, dh * 512:(dh + 1) * 512], op=ALU.add)

        mark_mix = A.off
        do_attn = 'noattn' not in stage
        do_rwkv = 'norwkv' not in stage
        ACOL = 1792
        battn = sb("battn", [128, 8])
        for c in range(4):
            P.dma("sp", [], ["battn"], battn[:, c:c + 1], b_attn_d.ap()[c * 128:(c + 1) * 128, :])
        for g in range(2):
            for hf in range(2):
                P.dma("sp", [], ["battn"], battn[hf * 64:(hf + 1) * 64, 4 + g:5 + g],
                      b_attn_d.ap()[512 + g * 64:512 + (g + 1) * 64, :])
        P.dma("sp", [], ["battn"], battn[:, 6:7], b_attn_d.ap()[640:768, :])
        sinks = sb("sinks", [128, 8])
        P.dma("sp", [], ["sinks"], sinks[:], sinks_d.ap().partition_broadcast(128))
        maskA = sb("maskA", [128, 256], BF16)
        mask1 = sb("mask1", [128, 256], BF16)
        P.dma("pool", [], ["maskA"], maskA[:], maskA_d.ap())
        P.dma("pool", [], ["mask1"], mask1[:], mask1_d.ap())
        battn_s = sb("battn_s", [128, 4])
        P.op("act", "mul", ["battn"], ["battn_s"], out=battn_s[:], in_=battn[:, 0:4], mul=0.125)
        mark("a1")
        VT = sb("VT", [128, NTH], BF16)
        Vtok = sb("Vtok", [128, NT, 128], BF16)
        Vpad = [sb(f"Vpad{i}", [128, NT, 128], BF16) for i in range(2)]
        Qg = sb("Qg", [128, 2, NTOK], BF16)
        KP = [sb(f"KP{i}", [128, NTH], BF16) for i in range(2)]
        for par in range(2):
            P.op("dve", "memset", [], [("KP", par)], KP[par][:], 0.0)
        AO = sb("AO", [128, 2, NTOK], BF16)
        sm = [sb(f"sm{i}", [128, 4, 256]) for i in range(4)]
        Pb = [sb(f"Pb{i}", [128, 4, 256], BF16) for i in range(4)]
        PT = [sb(f"PT{i}", [128, 8, 128], BF16) for i in range(4)]
        st = [sb(f"st{i}", [128, 20]) for i in range(4)]

        tok_blocks = [(0, 128)] + [(128 + 512 * j, 512) for j in range(4)]

        def proj_to(colspec, dst_fn, bias_ap, res, blocks, scale=1.0):
            wi = load_wtile(colspec)
            for bi, (tok0, T) in enumerate(blocks):
                bank = 4 + (bi % 4)
                proj(wi, tok0, T, bank)
                P.op("act", "activation", [("ps", bank), "battn", "battn_s"], [res],
                     out=dst_fn(tok0, T), in_=psf(bank, 0, T), func=AF.Identity, bias=bias_ap, scale=scale)

        proj_to([(ACOL + 640, 128)], lambda t0, T: VT[:, t0:t0 + T], battn[:, 6:7], "VT", tok_blocks)
        mark("a2")
        for n in range(NT):
            bank = 4 + (n % 4)
            P.op("pe", "transpose", ["VT", "ident_b"], [("ps", bank)], out=psb(bank, 0, 128),
                 in_=VT[:, n * 128:(n + 1) * 128], identity=ident_b[:])
            P.op("act", "copy", [("ps", bank)], [("Vtok", n)], out=Vtok[:, n, :], in_=psb(bank, 0, 128))
        mark("a3")
        for g in range(2 if do_attn else 0):
            for par in range(2):
                P.op("dve", "memset", [], [("Vpad", par)], Vpad[par][:], 0.0)
                P.op("dve", "tensor_copy", [("Vtok", n) for n in range(NT)], [("Vpad", par)],
                     out=Vpad[par][:, :, par * 64:(par + 1) * 64], in_=Vtok[:, :, g * 64:(g + 1) * 64])
            for lc in range(2):
                proj_to([(ACOL + (2 * g + lc) * 128, 128)],
                        lambda t0, T, lc=lc: Qg[:, lc, t0 - 128:t0 - 128 + T], battn_s[:, 2 * g + lc:2 * g + lc + 1],
                        ("Qg", lc), tok_blocks[1:], scale=0.125)
            wi_k = load_wtile([(ACOL + 512 + g * 64, 64), (ACOL + 512 + g * 64, 64)])
            for bi, (tok0, T_) in enumerate(tok_blocks):
                bank = 4 + (bi % 4)
                proj(wi_k, tok0, T_, bank)
                for par in range(2):
                    ps_ = slice(par * 64, (par + 1) * 64)
                    P.op("act", "activation", [("ps", bank), "battn"], [("KP", par)],
                         out=KP[par][ps_, tok0:tok0 + T_], in_=psf(bank, 0, T_)[ps_], func=AF.Identity,
                         bias=battn[ps_, 4 + g:5 + g], scale=1.0)
            mark("a4")

            def attn_stage(n, k, g=g):
                u = n % 4
                q0 = (n - 1) * 128
                sb0 = 2 * u
                mx, dd, es_, den = st[u][:, 0:4], st[u][:, 4:8], st[u][:, 8:12], st[u][:, 12:16]
                mk = mask1 if n == 1 else maskA
                mkn = "mask1" if n == 1 else "maskA"
                nmx = st[u][:, 16:20]
                if k == 0:
                    for j in range(4):
                        lc, par = j // 2, j % 2
                        bank = sb0 + j // 2
                        P.op("pe", "matmul", [("Qg", lc), ("KP", par)], [("ps", bank)],
                             psf(bank, (j % 2) * 256, (j % 2) * 256 + 256),
                             lhsT=Qg[:, lc, q0:q0 + 128],
                             rhs=KP[par][:, (n - 1) * 128:(n + 1) * 128], start=True, stop=False)
                        P.op("pe", "matmul", ["ident_b", mkn], [("ps", bank)],
                             psf(bank, (j % 2) * 256, (j % 2) * 256 + 256),
                             lhsT=ident_b[:], rhs=mk[:], start=False, stop=True)
                elif k == 1:
                    for hb_ in range(2):
                        P.op("dve", "tensor_reduce", [("ps", sb0 + hb_)], [("st", u)], out=mx[:, 2 * hb_:2 * hb_ + 2],
                             in_=psf(sb0 + hb_).rearrange("p (h k) -> p h k", h=2), axis=AX.X, op=ALU.max)
                    P.op("dve", "tensor_tensor", [("st", u), "sinks"], [("st", u)], out=mx, in0=mx,
                         in1=sinks[:, 4 * g:4 * g + 4], op=ALU.max)
                    P.op("dve", "tensor_tensor", [("st", u), "sinks"], [("st", u)], out=dd,
                         in0=sinks[:, 4 * g:4 * g + 4], in1=mx, op=ALU.subtract)
                    P.op("dve", "tensor_scalar", [("st", u)], [("st", u)], out=nmx, in0=mx, scalar1=-1.0, scalar2=None,
                         op0=ALU.mult)
                elif k == 2:
                    P.op("dve", "memset", [], [("st", u)], den, 0.0)
                    for j in range(4):
                        bank = sb0 + j // 2
                        P.op("act", "activation", [("ps", bank), ("st", u)], [("sm", u), ("st", u)], out=sm[u][:, j, :],
                             in_=psf(bank, (j % 2) * 256, (j % 2) * 256 + 256), func=AF.Exp, bias=nmx[:, j:j + 1], scale=1.0,
                             accum_out=den[:, j:j + 1])
                    P.op("act", "activation", [("st", u)], [("st", u)], out=es_, in_=dd, func=AF.Exp)
                elif k == 3:
                    P.op("dve", "tensor_tensor", [("st", u)], [("st", u)], out=den, in0=den, in1=es_, op=ALU.add)
                    P.op("dve", "reciprocal", [("st", u)], [("st", u)], out=den, in_=den)
                    P.op("dve", "tensor_tensor", [("sm", u), ("st", u)], [("Pb", u)], out=Pb[u][:], in0=sm[u][:],
                         in1=den.unsqueeze(2).to_broadcast([128, 4, 256]), op=ALU.mult)
                elif k == 4:
                    tb = sb0
                    for j in range(4):
                        for kh in range(2):
                            P.op("pe", "transpose", [("Pb", u), "ident_b"], [("ps", tb)],
                                 out=psb(tb, (j * 2 + kh) * 128, (j * 2 + kh + 1) * 128),
                                 in_=Pb[u][:, j, kh * 128:(kh + 1) * 128], identity=ident_b[:])
                    P.op("act", "copy", [("ps", tb)], [("PT", u)], out=PT[u][:],
                         in_=psb(tb).rearrange("p (c t) -> p c t", c=8))
                elif k == 5:
                    ab = sb0 + 1
                    for lc in range(2):
                        idx = 0
                        for par in range(2):
                            j = lc * 2 + par
                            for kh in range(2):
                                P.op("pe", "matmul", [("Vpad", par), ("PT", u)], [("ps", ab)],
                                     psf(ab, lc * 128, (lc + 1) * 128),
                                     lhsT=Vpad[par][:, n - 1 + kh, :], rhs=PT[u][:, j * 2 + kh, :],
                                     start=(idx == 0), stop=(idx == 3))
                                idx += 1
                    P.op("act", "copy", [("ps", ab)], [("AO", n)], out=AO[:, :, q0:q0 + 128],
                         in_=psf(ab, 0, 256).rearrange("p (c t) -> p c t", c=2))

            for n0 in range(1, NT, 4):
                for k in range(6):
                    for n in range(n0, n0 + 4):
                        attn_stage(n, k)
            mark("a6")
            wis = [load_wout(512 + (2 * g + lc) * 128) for lc in range(2)]
            wout_accum([(lambda n, lc=lc: AO[:, lc, (n - 1) * 128:n * 128], wis[lc], lambda n: [("AO", n)]) for lc in range(2)])
        barrier()
        A.off = mark_mix
        C0 = float(np.exp(-0.5))
        pv = sb("pv", [128, 48])
        SMX, W0, A0, KK_, KA_, RK_, LNW, LNB = 0, 14, 18, 22, 26, 30, 34, 38
        for c in range(14):
            P.dma("sp", [], ["pv"], pv[:, SMX + c:SMX + c + 1], rw_d["rwkv_shift_mix"].ap()[c * 128:(c + 1) * 128, :])
        for nm, col in (("rwkv_w0", W0), ("rwkv_a0", A0), ("rwkv_k_k", KK_), ("rwkv_k_a", KA_),
                        ("rwkv_r_k", RK_), ("rwkv_ln_w", LNW), ("rwkv_ln_b", LNB)):
            for j in range(4):
                P.dma("sp", [], ["pv"], pv[:, col + j:col + j + 1], rw_d[nm].ap()[j * 128:(j + 1) * 128, :])
        W2P = sb("W2P", [128, 512], BF16)
        A2P = sb("A2P", [128, 512], BF16)
        G2 = sb("G2", [128, 512], BF16)
        P.op("dve", "memset", [], ["W2A2"], W2P[:], 0.0)
        P.op("dve", "memset", [], ["W2A2"], A2P[:], 0.0)
        P.dma("pool", [], ["W2A2"], W2P[0:64, :], rw_d["rwkv_w2"].ap())
        P.dma("pool", [], ["W2A2"], A2P[64:128, :], rw_d["rwkv_a2"].ap())
        P.dma("pool", [], ["G2"], G2[:], rw_d["rwkv_g2"].ap())
        MK = sb("MK", [128, 384], BF16)
        P.dma("pool", [], ["MK"], MK[:], cmask_d.ap())
        bones_f = sb("bones_f", [128, 128])
        bones_b = sb("bones_b", [128, 128], BF16)
        bones_s = sb("bones_s", [128, 128])
        P.dma("sp", [], ["bones_f"], bones_f[:], bones_d.ap())
        P.op("act", "copy", ["bones_f"], ["bones_b"], out=bones_b[:], in_=bones_f[:])
        P.op("act", "mul", ["bones_f"], ["bones_s"], out=bones_s[:], in_=bones_f[:], mul=1.0 / 64)
        scanm = sb("scanm", [128, 512], BF16)
        P.dma("pool", [], ["scanm"], scanm[:], scanm_d.ap().partition_broadcast(128))
        fm = sb("fm", [128, 8])
        P.dma("sp", [], ["fm"], fm[:], fm_d.ap().partition_broadcast(128))
        gneps = sb("gneps", [128, 1])
        P.op("dve", "memset", [], ["gneps"], gneps[:], 64e-5)
        WA = sb("WA", [128, NTOK], BF16)
        GL = sb("GL", [128, NTOK], BF16)
        pf = [sb("pf0", [128, 513])]
        ga_off = A.off
        T = [sb(f"T{i}", [128, 512]) for i in range(10)]
        tb16 = [sb(f"tb16_{i}", [128, 512], BF16) for i in range(2)]
        A_save = A.off
        A.off = ga_off
        GA = sb("GA", [128, 4, 256])
        LT = sb("LT", [128, 3, 128])
        A.off = (A.off + 2047) // 2048 * 2048 if False else A.off
        HP_ = None
        A.off = A_save
        GA_RES = [("T", i) for i in range(3)]
        AR = sb("AR", [128, 8, 256], BF16)
        EXP = {nm: sb(nm, [128, 8, 128], BF16) for nm in ("BT", "KT", "BH", "KH", "VE")}
        P.op("dve", "memset", [], ["AR"], AR[:], 0.0)
        for nm in EXP:
            P.op("dve", "memset", [], [nm], EXP[nm][:], 0.0)
        NB = [sb(f"NB{i}", [128, 384], BF16) for i in range(4)]
        KB = [sb(f"KB{i}", [128, 256], BF16) for i in range(4)]
        TK = [sb(f"TK{i}", [128, 256], BF16) for i in range(4)]
        Zt = [sb(f"Z{i}", [128, 256], BF16) for i in range(4)]
        NM = [[sb(f"NM{i}_{k}", [128, 256], BF16) for k in range(2)] for i in range(4)]
        Qt = [[sb(f"Qt{p}_{i}", [128, 128], BF16) for i in range(4)] for p in range(2)]
        GT = [[sb(f"GT{p}_{i}", [128, 128], BF16) for i in range(4)] for p in range(2)]
        Hb = [[sb(f"Hb{p}_{i}", [128, 128], BF16) for i in range(4)] for p in range(2)]
        SS = [[sb(f"SS{p}_{i}", [128, 256], BF16) for i in range(4)] for p in range(2)]
        WC8 = [sb(f"WC8_{i}", [128, 8]) for i in range(2)]
        grp_ctr = [0]
        pendingB = [[]]
        SF = sb("SF", [128, 256])
        OL = sb("OL", [128, NTOK])
        QH = sb("QH", [128, NTOK], BF16)
        BON = sb("BON", [128, NTOK], BF16)
        MX = nc.alloc_sbuf_tensor_at("MX_alias", [128, NTOK], BF16, offset=SB_BASE)
        Sst = sb("Sst", [128, 128])
        Sstb = sb("Sstb", [128, 128], BF16)
        Pti = sb("Pti", [128, 128])
        dtmp = sb("dtmp", [128, 128])
        bank_ctr = [0]

        cap_ctr = [0]

        def nb():
            if P.capture is not None:
                cap_ctr[0] = (cap_ctr[0] + 1) % 2
                return 6 + cap_ctr[0]
            bank_ctr[0] = (bank_ctr[0] + 1) % 6
            return bank_ctr[0]

        def v3(ap):
            return ap.rearrange("p (c t) -> p c t", c=8)

        pf_ctr = [0]

        def proj_shift(wi, smx_col, tb, dst, dst_res):
            u = 0
            tok0 = 128 + tb * 512
            bank = nb()
            proj(wi, tok0, 512, bank)
            P.op("act", "copy", [("ps", bank)], [("pf", u)], out=pf[u][:, 1:513], in_=psf(bank))
            bank2 = nb()
            for kc in range(8):
                P.op("pe", "matmul", [("WT", wi), ("HT", tok0 // 128 - 1)], [("ps", bank2)], psf(bank2, 0, 1),
                     lhsT=WT[wi][:, kc, :], rhs=HT[:, kc, tok0 - 1:tok0], start=(kc == 0), stop=(kc == 7))
            P.op("act", "copy", [("ps", bank2)], [("pf", u)], out=pf[u][:, 0:1], in_=psf(bank2, 0, 1))
            P.op("dve", "tensor_tensor", [("pf", u)], [dst_res], out=dst, in0=pf[u][:, 0:512], in1=pf[u][:, 1:513],
                 op=ALU.subtract)
            P.op("dve", "scalar_tensor_tensor", [("pf", u), dst_res, "pv"], [dst_res], out=dst, in0=dst,
                 scalar=pv[:, SMX + smx_col:SMX + smx_col + 1], in1=pf[u][:, 1:513], op0=ALU.mult, op1=ALU.add)

        mark("r1")
        wl1 = load_wtile([(1536, 128)])
        wl2 = load_wtile([(1664, 128)])
        for tb in range(4):
            cs_ = slice(tb * 512, (tb + 1) * 512)
            proj_shift(wl1, 12, tb, T[0][:], ("T", 0))
            P.op("act", "activation", [("T", 0)], [("WA", tb)], out=WA[0:64, cs_], in_=T[0][0:64, :], func=AF.Tanh)
            P.op("act", "copy", [("T", 0)], [("WA", tb)], out=WA[64:128, cs_], in_=T[0][64:128, :])
            proj_shift(wl2, 13, tb, T[1][:], ("T", 1))
            P.op("act", "activation", [("T", 1)], [("GL", tb)], out=GL[:, cs_], in_=T[1][:], func=AF.Sigmoid)

        mark("r2")
        def ew(q, name, reads, writes, **kw):
            P.op(q, name, reads, writes, **kw)

        chunk_ctr = [0]
        for j in range(4 if do_rwkv else 0):
            wr_ = load_wtile([(j * 128, 128)])
            wk_ = load_wtile([(512 + j * 128, 128)])
            wv_ = load_wtile([(1024 + j * 128, 128)])
            r_s, k_s, v_s, sgw, cs, a_t, kkn, kp, Ep, tmp = [T[i][:] for i in range(10)]
            R_ = lambda i: ("T", i)
            b_t, Eex, Em, Eh = k_s, sgw, cs, a_t

            def prep_main(tb, j=j, wr_=wr_, wk_=wk_, wv_=wv_):
                cs_ = slice(tb * 512, (tb + 1) * 512)
                proj_shift(wr_, j, tb, r_s, R_(0))
                proj_shift(wk_, 4 + j, tb, k_s, R_(1))
                proj_shift(wv_, 8 + j, tb, v_s, R_(2))
                bw = nb()
                P.op("pe", "matmul", ["W2A2", ("WA", tb)], [("ps", bw)], psf(bw), lhsT=W2P[:, j * 128:(j + 1) * 128],
                     rhs=WA[:, cs_], start=True, stop=True)
                P.op("act", "activation", [("ps", bw), "pv"], [R_(3)], out=sgw, in_=psf(bw), func=AF.Sigmoid,
                     bias=pv[:, W0 + j:W0 + j + 1], scale=1.0)
                ba = nb()
                P.op("pe", "matmul", ["W2A2", ("WA", tb)], [("ps", ba)], psf(ba), lhsT=A2P[:, j * 128:(j + 1) * 128],
                     rhs=WA[:, cs_], start=True, stop=True)
                P.op("act", "activation", [("ps", ba), "pv"], [R_(5)], out=a_t, in_=psf(ba), func=AF.Sigmoid,
                     bias=pv[:, A0 + j:A0 + j + 1], scale=1.0)
                P.op("dve", "tensor_scalar", [R_(1), "pv"], [R_(6)], out=kkn, in0=k_s, scalar1=pv[:, KK_ + j:KK_ + j + 1],
                     scalar2=None, op0=ALU.mult)
                P.op("act", "activation", [R_(6)], [("tb16", 0)], out=tb16[0][:], in_=kkn, func=AF.Square)
                bq = nb()
                P.op("pe", "matmul", ["bones_b", ("tb16", 0)], [("ps", bq)], psf(bq), lhsT=bones_b[:], rhs=tb16[0][:],
                     start=True, stop=True)
                P.op("dve", "tensor_scalar", [("ps", bq)], [R_(9)], out=tmp, in0=psf(bq), scalar1=1e-24, scalar2=None,
                     op0=ALU.max)
                P.op("act", "activation", [R_(9)], [R_(9)], out=tmp, in_=tmp, func=AF.Sqrt)
                P.op("dve", "reciprocal", [R_(9)], [R_(9)], out=tmp, in_=tmp)
                P.op("dve", "tensor_tensor", [R_(6), R_(9)], [R_(6)], out=kkn, in0=kkn, in1=tmp, op=ALU.mult)
                P.op("dve", "tensor_scalar", [R_(5), "pv"], [R_(7)], out=kp, in0=a_t, scalar1=-1.0,
                     scalar2=pv[:, KA_ + j:KA_ + j + 1], op0=ALU.add, op1=ALU.mult)
                P.op("dve", "scalar_tensor_tensor", [R_(7), R_(1)], [R_(7)], out=kp, in0=kp, scalar=1.0, in1=k_s,
                     op0=ALU.add, op1=ALU.mult)
                b_t = k_s
                P.op("dve", "tensor_tensor", [R_(6), R_(5)], [R_(1)], out=b_t, in0=kkn, in1=a_t, op=ALU.mult)
                P.op("dve", "scalar_tensor_tensor", [R_(0), R_(7), "pv"], [("tb16", 1)], out=tb16[1][:], in0=r_s,
                     scalar=pv[:, RK_ + j:RK_ + j + 1], in1=kp, op0=ALU.mult, op1=ALU.mult)
                bb = nb()
                P.op("pe", "matmul", ["bones_b", ("tb16", 1)], [("ps", bb)], psf(bb), lhsT=bones_b[:], rhs=tb16[1][:],
                     start=True, stop=True)
                P.op("dve", "tensor_tensor", [("ps", bb), R_(2)], [("BON", tb)], out=BON[:, cs_], in0=psf(bb), in1=v_s,
                     op=ALU.mult)
                P.op("dve", "tensor_tensor_scan", [R_(3), "scanm"], [R_(4)], out=cs, data0=scanm[:], data1=sgw, initial=0.0,
                     op0=ALU.mult, op1=ALU.add)
                P.op("dve", "tensor_tensor", [R_(4), R_(3)], [R_(3)], out=sgw, in0=cs, in1=sgw, op=ALU.subtract)
                P.op("act", "activation", [R_(3)], [R_(3)], out=sgw, in_=sgw, func=AF.Exp, scale=-C0)
                Eex = sgw
                P.op("act", "activation", [R_(4)], [R_(8)], out=Ep, in_=cs, func=AF.Exp, scale=-C0)
                P.op("act", "activation", [R_(4)], [R_(4)], out=cs, in_=cs, func=AF.Exp, scale=C0)
                Em = cs
                Eh = a_t
                P.op("dve", "tensor_tensor", [R_(4), R_(8)], [R_(5)], out=v3(Eh), in0=v3(Em),
                     in1=v3(Ep)[:, :, 63:64].to_broadcast([128, 8, 64]), op=ALU.mult)
                mark("r3")

            def expansions(tb):
                for hf in range(2):
                    ps_ = slice(hf * 64, (hf + 1) * 64)
                    cA = slice(hf * 64, (hf + 1) * 64)
                    cR = slice(128 + hf * 64, 128 + (hf + 1) * 64)
                    q = "dve" if hf == 0 else "pool"
                    ew("dve", "scalar_tensor_tensor", [R_(6), R_(3)], ["AR"], out=AR[ps_, :, cA], in0=v3(kkn)[ps_], scalar=-1.0,
                       in1=v3(Eex)[ps_], op0=ALU.mult, op1=ALU.mult)
                    ew(q, "tensor_tensor", [R_(0), R_(8)], ["AR"], out=AR[ps_, :, cR], in0=v3(r_s)[ps_], in1=v3(Ep)[ps_],
                       op=ALU.mult)
                    ew(q, "tensor_tensor", [R_(1), R_(4)], ["BT"], out=EXP["BT"][ps_, :, cA], in0=v3(b_t)[ps_],
                       in1=v3(Em)[ps_], op=ALU.mult)
                    ew(q, "tensor_tensor", [R_(7), R_(4)], ["KT"], out=EXP["KT"][ps_, :, cA], in0=v3(kp)[ps_],
                       in1=v3(Em)[ps_], op=ALU.mult)
                    ew(q, "tensor_tensor", [R_(1), R_(5)], ["BH"], out=EXP["BH"][ps_, :, cA], in0=v3(b_t)[ps_],
                       in1=v3(Eh)[ps_], op=ALU.mult)
                    ew(q, "tensor_tensor", [R_(7), R_(5)], ["KH"], out=EXP["KH"][ps_, :, cA], in0=v3(kp)[ps_],
                       in1=v3(Eh)[ps_], op=ALU.mult)
                    ew(q, "tensor_copy", [R_(2)], ["VE"], out=EXP["VE"][ps_, :, cA], in_=v3(v_s)[ps_])
                mark("r4")

            def chunk_groups(tb, prep_thunks, j=j):
                n_prep = len(prep_thunks)
                per_unit = (n_prep + 79) // 80
                prep_pos = [0]
                WCt = WC8[tb % 2]
                P.op("dve", "tensor_copy", [R_(8)], [("WC8", tb % 2)], out=WCt[:], in_=v3(Ep)[:, :, 63])
                for g4 in range(2):
                    gi = grp_ctr[0]
                    grp_ctr[0] += 1
                    pset = gi % 2
                    unitsA = []
                    for stage_grp in ((0, 1), (2,), (3,), (4,), (5,), (6,), (7,), (8,), (9, 10, 11)):
                        for s_ in range(4):
                            for stage_i in stage_grp:
                                unitsA.append((stage_i, s_))
                    ctxs = []
                    for s_ in range(4):
                        c = g4 * 4 + s_
                        ctxs.append(dict(c=c, tcol=tb * 512 + c * 64, ARc=AR[:, c, :],
                                         BTc=EXP["BT"][:, c, :], KTc=EXP["KT"][:, c, :], BHc=EXP["BH"][:, c, :],
                                         KHc=EXP["KH"][:, c, :], VEc=EXP["VE"][:, c, :], nmi=0))
                    first_grp = (tb == 0 and g4 == 0)
                    if first_grp:
                        P.op("dve", "memset", [], [("Sst", pset, 0)], SS[pset][0][:, 0:128], 0.0)
                        P.op("dve", "tensor_copy", ["ident_b"], [("Sst", pset, 0)], out=SS[pset][0][:, 128:256], in_=ident_b[:])

                    def unitA(stage_i, s_, tb=tb, pset=pset, ctxs=ctxs, WCt=WCt):
                        cx = ctxs[s_]
                        ARc, BTc, KTc, BHc, KHc, VEc = cx["ARc"], cx["BTc"], cx["KTc"], cx["BHc"], cx["KHc"], cx["VEc"]
                        if stage_i == 0:
                            b1, b2, b3 = nb(), nb(), nb()
                            P.op("pe", "matmul", ["BT", "AR"], [("ps", b1)], psf(b1, 0, 256), lhsT=BTc, rhs=ARc, start=True, stop=True)
                            P.op("pe", "matmul", ["KT", "AR"], [("ps", b2)], psf(b2, 0, 256), lhsT=KTc, rhs=ARc, start=True, stop=True)
                            P.op("pe", "matmul", ["AR", "BT"], [("ps", b3)], psf(b3, 0, 128), lhsT=ARc[:, 0:128], rhs=BTc,
                                 start=True, stop=True)
                            P.op("dve", "tensor_tensor", [("ps", b1), "MK"], [("NB", s_)], out=NB[s_][:, 0:256], in0=psf(b1, 0, 256),
                                 in1=MK[:, 0:256], op=ALU.mult)
                            P.op("dve", "tensor_tensor", [("ps", b2), "MK"], [("KB", s_)], out=KB[s_][:], in0=psf(b2, 0, 256),
                                 in1=MK[:, 0:256], op=ALU.mult)
                            P.op("dve", "tensor_tensor", [("ps", b3), "MK"], [("NM", s_, 0)], out=NM[s_][0][:, 128:256], in0=psf(b3, 0, 128),
                                 in1=MK[:, 256:384], op=ALU.mult)
                        elif stage_i == 1:
                            b4 = nb()
                            for ti, src in enumerate((VEc, KHc, ARc[:, 0:128], BHc)):
                                P.op("pe", "transpose", ["VE", "KH", "AR", "BH", "ident_b"], [("ps", b4)],
                                     out=psb(b4, ti * 128, (ti + 1) * 128), in_=src, identity=ident_b[:])
                            P.op("act", "copy", [("ps", b4)], [("TK", s_)], out=TK[s_][:], in_=psb(b4, 0, 256))
                            P.op("act", "copy", [("ps", b4)], [("Z", s_)], out=Zt[s_][:, 128:256], in_=psb(b4, 256, 384))
                            P.op("act", "copy", [("ps", b4)], [("NB", s_)], out=NB[s_][:, 256:384], in_=psb(b4, 384, 512))
                        elif stage_i == 2:
                            b5 = nb()
                            P.op("pe", "matmul", [("KB", s_), ("TK", s_)], [("ps", b5)], psf(b5, 0, 128), lhsT=KB[s_][:, 0:128],
                                 rhs=TK[s_][:, 0:128], start=True, stop=True)
                            P.op("act", "copy", [("ps", b5)], [("Z", s_)], out=Zt[s_][:, 0:128], in_=psf(b5, 0, 128))
                            P.op("pool", "tensor_copy", [("NB", s_)], [("NM", s_, 0)], out=NM[s_][0][:, 0:128], in_=NB[s_][:, 0:128])
                            cx["nmi"] = 0
                        elif 3 <= stage_i <= 8:
                            lvl = stage_i - 3
                            nmi = cx["nmi"]
                            Ncur, Mcur = NM[s_][nmi][:, 0:128], NM[s_][nmi][:, 128:256]
                            bz = nb()
                            if False:
                                P.op("pe", "matmul", ["ident_b", ("Z", s_)], [("ps", bz)], psf(bz, 0, 256), lhsT=ident_b[:], rhs=Zt[s_][:],
                                     start=True, stop=False)
                                P.op("pe", "matmul", [("NM", s_, nmi), ("Z", s_)], [("ps", bz)], psf(bz, 0, 256), lhsT=Ncur, rhs=Zt[s_][:],
                                     start=False, stop=True)
                                P.op("act", "copy", [("ps", bz)], [("Z", s_)], out=Zt[s_][:], in_=psf(bz, 0, 256))
                            else:
                                P.op("pe", "matmul", [("NM", s_, nmi), ("Z", s_)], [("ps", bz)], psf(bz, 0, 256), lhsT=Ncur, rhs=Zt[s_][:],
                                     start=True, stop=True)
                                P.op("dve", "tensor_tensor", [("ps", bz), ("Z", s_)], [("Z", s_)], out=Zt[s_][:], in0=psf(bz, 0, 256),
                                     in1=Zt[s_][:], op=ALU.add)
                            if lvl < 5:
                                bs = nb()
                                P.op("pe", "matmul", [("NM", s_, nmi)], [("ps", bs)], psf(bs, 0, 128), lhsT=Mcur, rhs=Ncur, start=True, stop=True)
                                P.op("pe", "matmul", [("NM", s_, nmi)], [("ps", bs)], psf(bs, 128, 256), lhsT=Ncur, rhs=Mcur, start=True, stop=True)
                                P.op("act", "copy", [("ps", bs)], [("NM", s_, 1 - nmi)], out=NM[s_][1 - nmi][:], in_=psf(bs, 0, 256))
                                cx["nmi"] = 1 - nmi
                        elif stage_i == 9:
                            bqg = nb()
                            P.op("pe", "matmul", [("Z", s_), ("NB", s_)], [("ps", bqg)], psf(bqg, 0, 256), lhsT=Zt[s_][:, 128:256],
                                 rhs=NB[s_][:, 128:384], start=True, stop=True)
                            P.op("dve", "tensor_tensor", [("ps", bqg), "AR"], [("Qt", pset, s_)], out=Qt[pset][s_][:], in0=psf(bqg, 0, 128),
                                 in1=ARc[:, 128:256], op=ALU.add)
                            P.op("dve", "scalar_tensor_tensor", [("ps", bqg), "ident_f", ("WC8", tb % 2)], [("GT", pset, s_)],
                                 out=GT[pset][s_][:], in0=ident_f[:], scalar=WCt[:, cx["c"]:cx["c"] + 1], in1=psf(bqg, 128, 256),
                                 op0=ALU.mult, op1=ALU.add)
                        elif stage_i == 10:
                            bhh = nb()
                            P.op("pe", "matmul", [("NB", s_), ("Z", s_)], [("ps", bhh)], psf(bhh, 0, 128), lhsT=NB[s_][:, 256:384],
                                 rhs=Zt[s_][:, 0:128], start=True, stop=False)
                            P.op("pe", "matmul", [("TK", s_)], [("ps", bhh)], psf(bhh, 0, 128), lhsT=TK[s_][:, 128:256], rhs=TK[s_][:, 0:128],
                                 start=False, stop=True)
                            P.op("act", "copy", [("ps", bhh)], [("Hb", pset, s_)], out=Hb[pset][s_][:], in_=psf(bhh, 0, 128))
                        elif stage_i == 11:
                            bo = nb()
                            P.op("pe", "matmul", [("Z", s_), ("NB", s_)], [("ps", bo)], psf(bo, 0, 128), lhsT=Zt[s_][:, 0:128],
                                 rhs=NB[s_][:, 128:256], start=True, stop=False)
                            P.op("pe", "matmul", [("TK", s_), ("KB", s_)], [("ps", bo)], psf(bo, 0, 128), lhsT=TK[s_][:, 0:128],
                                 rhs=KB[s_][:, 128:256], start=False, stop=True)
                            for hf in range(2):
                                ps_ = slice(hf * 64, (hf + 1) * 64)
                                P.op("act", "copy", [("ps", bo)], [("OL", cx["tcol"] // 64)], out=OL[ps_, cx["tcol"]:cx["tcol"] + 64],
                                     in_=psf(bo, hf * 64, (hf + 1) * 64)[ps_])

                    def unitB(kind, s_, tb=tb, g4=g4, pset=pset, ctxs=ctxs):
                        cx = ctxs[s_]
                        Sin = SS[pset][s_]
                        if kind == 0:
                            if s_ < 3:
                                Sout, sres = SS[pset][s_ + 1], ("Sst", pset, s_ + 1)
                            else:
                                Sout, sres = SS[1 - pset][0], ("Sst", 1 - pset, 0)
                            bsn = nb()
                            P.op("pe", "matmul", [("GT", pset, s_), ("Sst", pset, s_)], [("ps", bsn)], psf(bsn, 0, 128),
                                 lhsT=GT[pset][s_][:], rhs=Sin[:, 0:128], start=True, stop=False)
                            P.op("pe", "matmul", [("Hb", pset, s_), "ident_b"], [("ps", bsn)], psf(bsn, 0, 128),
                                 lhsT=ident_b[:], rhs=Hb[pset][s_][:], start=False, stop=True)
                            P.op("pe", "matmul", [("GT", pset, s_), ("Sst", pset, s_)], [("ps", bsn)], psf(bsn, 128, 256),
                                 lhsT=GT[pset][s_][:], rhs=Sin[:, 128:256], start=True, stop=True)
                            P.op("act", "copy", [("ps", bsn)], [sres], out=Sout[:], in_=psf(bsn, 0, 256))
                            if tb == 3 and g4 == 1 and s_ == 3 and "nosf" not in stage:
                                P.op("dve", "tensor_copy", [("ps", bsn)], ["SF"], out=SF[:], in_=psf(bsn, 0, 256))
                        else:
                            tcol = cx["tcol"]
                            bo = nb()
                            P.op("pe", "matmul", [("Sst", pset, s_), ("Qt", pset, s_)], [("ps", bo)], psf(bo, 0, 128),
                                 lhsT=Sin[:, 0:128], rhs=Qt[pset][s_][:], start=True, stop=True)
                            bh_ = nb()
                            P.op("pe", "matmul", [("Sst", pset, s_), ("Qt", pset, s_)], [("ps", bh_)], psf(bh_, 0, 128),
                                 lhsT=Sin[:, 128:256], rhs=Qt[pset][s_][:], start=True, stop=True)
                            for hf in range(2):
                                ps_ = slice(hf * 64, (hf + 1) * 64)
                                P.op("dve", "tensor_tensor", [("ps", bo), ("OL", tcol // 64)], [("OL", tcol // 64)],
                                     out=OL[ps_, tcol:tcol + 64], in0=psf(bo, hf * 64, (hf + 1) * 64)[ps_],
                                     in1=OL[ps_, tcol:tcol + 64], op=ALU.add)
                                P.op("act", "copy", [("ps", bh_)], [("QH", tcol // 64)], out=QH[ps_, tcol:tcol + 64],
                                     in_=psf(bh_, hf * 64, (hf + 1) * 64)[ps_])

                    unitsB_prev = pendingB[0]
                    nA, nB = len(unitsA), len(unitsB_prev)
                    bi_ = 0
                    for ai, (st_i, s_) in enumerate(unitsA):
                        unitA(st_i, s_)
                        for _ in range(per_unit):
                            if prep_pos[0] < n_prep:
                                P.replay(prep_thunks[prep_pos[0]])
                                prep_pos[0] += 1
                        if nB and ai % 6 == 5 and bi_ < nB:
                            unitsB_prev[bi_]()
                            bi_ += 1
                    while bi_ < nB:
                        unitsB_prev[bi_]()
                        bi_ += 1
                    newB = []
                    for s_ in range(4):
                        newB.append(lambda s_=s_, f=unitB: f(0, s_))
                    for s_ in range(4):
                        newB.append(lambda s_=s_, f=unitB: f(1, s_))
                    pendingB[0] = newB
                while prep_pos[0] < n_prep:
                    P.replay(prep_thunks[prep_pos[0]])
                    prep_pos[0] += 1

            prep_main(0)
            expansions(0)
            for tb in range(4):
                thunks = []
                if tb < 3:
                    P.capture = []
                    prep_main(tb + 1)
                    thunks = P.capture
                    P.capture = None
                chunk_groups(tb, thunks)
                if tb < 3:
                    expansions(tb + 1)
            for f in pendingB[0]:
                f()
            pendingB[0] = []

            mark("r5")
            if use_cc:
                P.dma("sp", ["SF"], [("ccin", j)], cc_in[j].ap(), SF[:])
                P.coll("pool", [("ccin", j)], [("ccg", j)], "AllGather", ALU.bypass,
                       replica_groups=[[0, 1, 2, 3], [4, 5, 6, 7]], ins=[cc_in[j].ap().opt()], outs=[cc_out[j].ap().opt()])
            else:
                P.dma("sp", ["SF"], [("sfdbg", j)], sf_dbg.ap()[:, j * 256:(j + 1) * 256], SF[:])
            P.dma("sp", [("ccg", j)], GA_RES, GA[:], ccg_view(j))
            for i in range(3):
                bt_ = nb()
                P.op("pe", "transpose", GA_RES + ["ident_f"], [("ps", bt_)], out=psf(bt_, 0, 128), in_=GA[:, i, 128:256],
                     identity=ident_f[:])
                P.op("dve", "tensor_scalar", [("ps", bt_), "fm"] + GA_RES, GA_RES, out=LT[:, i, :], in0=psf(bt_, 0, 128),
                     scalar1=fm[:, i:i + 1], scalar2=None, op0=ALU.mult)
                P.op("dve", "scalar_tensor_tensor", ["ident_f", "fm"] + GA_RES, GA_RES, out=LT[:, i, :], in0=ident_f[:],
                     scalar=fm[:, 4 + i:5 + i], in1=LT[:, i, :], op0=ALU.mult, op1=ALU.add)
                P.op("dve", "tensor_scalar", ["fm"] + GA_RES, GA_RES, out=GA[:, i, 0:128], in0=GA[:, i, 0:128],
                     scalar1=fm[:, i:i + 1], scalar2=None, op0=ALU.mult)
            P.op("dve", "tensor_copy", GA_RES, ["Sst"], out=Sst[:], in_=GA[:, 0, 0:128])
            for i in range(1, 3):
                bm_ = nb()
                P.op("pe", "matmul", GA_RES + ["Sst"], [("ps", bm_)], psf(bm_, 0, 128), lhsT=LT[:, i, :], rhs=Sst[:],
                     start=True, stop=False)
                P.op("pe", "matmul", GA_RES + ["ident_f"], [("ps", bm_)], psf(bm_, 0, 128), lhsT=ident_f[:], rhs=GA[:, i, 0:128],
                     start=False, stop=True)
                P.op("dve", "tensor_copy", [("ps", bm_)], ["Sst"], out=Sst[:], in_=psf(bm_, 0, 128))
            mark("r6")
            P.op("act", "copy", ["Sst"], ["Sstb"], out=Sstb[:], in_=Sst[:])
            for tb in range(4):
                cs_ = slice(tb * 512, (tb + 1) * 512)
                O_, cen, sq = T[4][:], T[5][:], T[6][:]
                bf_ = nb()
                P.op("pe", "matmul", ["Sstb"] + [("QH", tb * 8 + c_) for c_ in range(8)], [("ps", bf_)], psf(bf_), lhsT=Sstb[:], rhs=QH[:, cs_], start=True, stop=True)
                P.op("dve", "tensor_tensor", [("ps", bf_)] + [("OL", tb * 8 + c_) for c_ in range(8)], [("T", 4)], out=O_, in0=psf(bf_), in1=OL[:, cs_], op=ALU.add)
                bmu = nb()
                P.op("pe", "matmul", ["bones_s", ("T", 4)], [("ps", bmu)], psf(bmu), lhsT=bones_s[:], rhs=O_, start=True, stop=True)
                P.op("dve", "tensor_tensor", [("ps", bmu), ("T", 4)], [("T", 5)], out=cen, in0=O_, in1=psf(bmu), op=ALU.subtract)
                P.op("act", "activation", [("T", 5)], [("T", 6)], out=sq, in_=cen, func=AF.Square)
                bvar = nb()
                P.op("pe", "matmul", ["bones_s", ("T", 6)], [("ps", bvar)], psf(bvar), lhsT=bones_s[:], rhs=sq, start=True, stop=True)
                P.op("act", "activation", [("ps", bvar), "gneps"], [("T", 6)], out=sq, in_=psf(bvar), func=AF.Sqrt,
                     bias=gneps[:, 0:1], scale=1.0)
                P.op("dve", "reciprocal", [("T", 6)], [("T", 6)], out=sq, in_=sq)
                P.op("dve", "tensor_tensor", [("T", 5), ("T", 6)], [("T", 5)], out=cen, in0=cen, in1=sq, op=ALU.mult)
                P.op("dve", "tensor_scalar", [("T", 5), "pv"], [("T", 5)], out=cen, in0=cen, scalar1=pv[:, LNW + j:LNW + j + 1],
                     scalar2=pv[:, LNB + j:LNB + j + 1], op0=ALU.mult, op1=ALU.add)
                P.op("dve", "tensor_tensor", [("T", 5), ("BON", tb)], [("T", 5)], out=cen, in0=cen, in1=BON[:, cs_], op=ALU.add)
                bg_ = nb()
                P.op("pe", "matmul", ["G2", ("GL", tb)], [("ps", bg_)], psf(bg_), lhsT=G2[:, j * 128:(j + 1) * 128], rhs=GL[:, cs_],
                     start=True, stop=True)
                P.op("dve", "tensor_tensor", [("T", 5), ("ps", bg_)], [("MX", tb), ("X", 0)], out=MX[:, cs_], in0=cen, in1=psf(bg_), op=ALU.mult)
            mark("r7")
            wi_ = load_wout(j * 128)
            wout_accum([(lambda n: MX[:, (n - 1) * 128:n * 128], wi_, lambda n: [("MX", (n - 1) // 4)])])
        barrier()

    if not stage.startswith(("noffn", "mix")):
        load_gain("norm_ffn2")
        for i in range(1, NT):
            rmsnorm_tile(i, norm_to_HT)
        ffn("ffn2", range(1, NT))

    P.enabled = True
    barrier()
    A.off = phase_mark
    ob = [sb(f"ob{i}", [128, D]) for i in range(2)]
    load_gain("norm_final")

    def norm_to_out(i, col):
        b = i % 2
        P.op("dve", "scalar_tensor_tensor", [("X", i), ("rstd", col), "gbc"], [("ob", b)],
             out=ob[b][:], in0=X[:, i, :], scalar=rstd[:, col:col + 1],
             in1=gbc[:], op0=ALU.mult, op1=ALU.mult)
        P.dma("sp", [("ob", b)], [("out", i)], out.ap()[(i - 1) * 128:i * 128, :], ob[b][:])

    for i in range(1, NT):
        rmsnorm_tile(i, norm_to_out)
    P.wait_all("sp")

    P.emit()
    es.close()
    return nc


_NC_CACHE = {}


def _prep_inputs(inputs):
    x = np.ascontiguousarray(inputs["x"], dtype=np.float32)
    B, S, _ = x.shape
    shared = {"ident": np.eye(128, dtype=np.float32)}
    qi = np.arange(128)[:, None]
    kj = np.arange(256)[None, :]
    valid = np.where(kj < 128, kj > qi, (kj - 128) <= qi)
    maskA = np.where(valid, 0.0, NEG).astype(np.float32)
    mask0 = maskA.copy()
    mask0[:, :128] = NEG
    shared["maskA"] = maskA
    shared["w_in"] = np.ascontiguousarray(inputs["w_in"][0], dtype=np.float32)
    shared["w_out"] = np.ascontiguousarray(inputs["w_out"][0], dtype=np.float32)
    shared["b_in_attn"] = np.ascontiguousarray(inputs["b_in_attn"].reshape(768, 1), dtype=np.float32)
    idx = np.arange(128)
    hh, ii = idx // 64, idx % 64
    same = hh[:, None] == hh[None, :]
    su = same & (ii[:, None] < ii[None, :])
    ui = same & (ii[:, None] <= ii[None, :])
    sl = same & (ii[:, None] > ii[None, :])
    shared["cmask"] = np.concatenate([su, ui, sl], axis=1).astype(np.float32)
    shared["bones"] = same.astype(np.float32)
    scanm = np.ones((1, 512), np.float32)
    scanm[:, ::64] = 0.0
    shared["scanm"] = scanm
    shared["rwkv_shift_mix"] = np.ascontiguousarray(inputs["rwkv_shift_mix"].reshape(1792, 1), dtype=np.float32)
    for nm in ("rwkv_w0", "rwkv_a0", "rwkv_k_k", "rwkv_k_a", "rwkv_r_k", "rwkv_ln_w", "rwkv_ln_b"):
        shared[nm] = np.ascontiguousarray(inputs[nm].reshape(512, 1), dtype=np.float32)
    shared["rwkv_w2"] = np.ascontiguousarray(inputs["rwkv_w2"][0], dtype=np.float32)
    shared["rwkv_a2"] = np.ascontiguousarray(inputs["rwkv_a2"][0], dtype=np.float32)
    shared["rwkv_g2"] = np.ascontiguousarray(inputs["rwkv_g2"][0], dtype=np.float32)
    shared["attn_sinks"] = np.ascontiguousarray(inputs["attn_sinks"].reshape(1, 8), dtype=np.float32)
    for nm in ("ffn1_gate", "ffn1_up", "ffn1_down", "ffn2_gate", "ffn2_up", "ffn2_down"):
        shared[nm] = np.ascontiguousarray(inputs[nm][0], dtype=np.float32)
    for nm in ("norm_ffn1", "norm_mix", "norm_ffn2"):
        shared[nm] = np.ascontiguousarray(inputs[nm].reshape(1, D), dtype=np.float32)
    shared["norm_final"] = np.ascontiguousarray(inputs["norm_final"].reshape(1, D), dtype=np.float32)
    in_maps = []
    for c in range(NCORES):
        b, s = c // 4, c % 4
        xs = np.zeros((NTH, D), np.float32)
        if s > 0:
            xs[:128] = x[b, s * NTOK - 128:s * NTOK]
        xs[128:] = x[b, s * NTOK:(s + 1) * NTOK]
        m = dict(shared)
        m["xs"] = xs
        m["mask1"] = mask0 if s == 0 else maskA
        fmv = np.zeros((1, 8), np.float32)
        for i in range(3):
            fmv[0, i] = 1.0 if i < s else 0.0
            fmv[0, 4 + i] = 1.0 - fmv[0, i]
        m["fm"] = fmv
        in_maps.append(m)
    return in_maps


STAGE = "full"


def kernel(**inputs):
    if STAGE not in _NC_CACHE:
        _NC_CACHE[STAGE] = build(STAGE)
    nc = _NC_CACHE[STAGE]
    in_maps = _prep_inputs(inputs)
    if "ccin" in STAGE:
        for m in in_maps:
            m["ccg"] = np.zeros((4 * 128, 1024), np.float32)
        res = run_bass_kernel_spmd(nc, in_maps, core_ids=list(range(NCORES)))
        ccg = np.concatenate([np.asarray(r["sf_dbg"], dtype=np.float32).reshape(128, 1024) for r in res.results], axis=0)
        for c, m in enumerate(in_maps):
            m["ccg"] = ccg[(c // 4) * 512:(c // 4 + 1) * 512]
    res = run_bass_kernel_spmd(nc, in_maps, core_ids=list(range(NCORES)))
    outs = [np.asarray(r["out"], dtype=np.float32).reshape(NTOK, D) for r in res.results]
    full = np.stack([np.concatenate(outs[b * 4:(b + 1) * 4], axis=0) for b in range(2)], axis=0)
    return full
```

```python
import contextlib
import numpy as np
import concourse.bass as bass
import concourse.mybir as mybir
from concourse.bass_utils import run_bass_kernel_spmd

F32 = mybir.dt.float32
BF16 = mybir.dt.bfloat16
AF = mybir.ActivationFunctionType
ALU = mybir.AluOpType
AX = mybir.AxisListType

D = 1024
DFF = 2816
NTOK = 2048
NT = 17
NTH = NT * 128
NCORES = 8
SEM_LIM = 20000
NLANES = 12
CC_INC = 1


class _Op:
    __slots__ = ("q", "fn", "deps", "need_inc", "semidx", "semval", "lane", "laneval", "idx")


class Prog:
    CE = ("pe", "act", "dve", "pool")
    QS = ("pe", "act", "dve", "pool", "sp")

    def __init__(self, nc):
        self.nc = nc
        self.ops = {q: [] for q in self.QS}
        self.last_w = {}
        self.readers = {}
        self.lane_rr = {q: 0 for q in self.QS}
        self.lane_cnt = {}
        self.lane_last = {}
        self.enabled = True
        self.capture = None
        self._open_grp = None

    def _deps(self, q, reads, writes, is_dma):
        deps = set()
        for r in reads:
            ev = self.last_w.get(r)
            if ev is not None:
                deps.add(ev)
        for w in writes:
            ev = self.last_w.get(w)
            if ev is not None:
                deps.add(ev)
            for ev in self.readers.get(w, ()):
                deps.add(ev)
        if q == "pe" and not is_dma:
            deps = {e for e in deps if not (e[0] == "c" and e[1] == "pe")}
        return deps

    def _commit(self, ev, reads, writes):
        for r in reads:
            self.readers.setdefault(r, []).append(ev)
        for w in writes:
            self.last_w[w] = ev
            self.readers[w] = []

    def replay(self, th):
        kind, a, kw = th
        if kind == "op":
            self.op(*a, **kw)
        elif kind == "grp":
            for it in a:
                self.op(*it[1], **it[2])
        else:
            self.dma(*a)

    def op(self, q, name, reads, writes, *args, **kw):
        if not self.enabled:
            return None
        if self.capture is not None:
            item = ("op", (q, name, reads, writes) + args, kw)
            if self._open_grp is not None:
                self._open_grp.append(item)
                if kw.get("stop", True):
                    self._open_grp = None
            elif name == "matmul" and not kw.get("stop", True):
                self._open_grp = [item]
                self.capture.append(("grp", self._open_grp, {}))
            else:
                self.capture.append(item)
            return None
        o = _Op()
        fn = (name, args, kw)
        writes = list(writes) + [r for r in reads if isinstance(r, tuple) and r[0] == "ps"]
        o.q, o.fn, o.need_inc, o.lane = q, fn, False, None
        o.deps = self._deps(q, reads, writes, False)
        o.idx = len(self.ops[q])
        self.ops[q].append(o)
        self._commit(("c", q, o.idx), reads, writes)
        return o

    def dma(self, q, reads, writes, out, in_):
        if not self.enabled:
            return _Op()
        if self.capture is not None:
            self.capture.append(("dma", (q, reads, writes, out, in_), {}))
            return _Op()
        o = _Op()
        fn = ("dma_start", (), {"out": out, "in_": in_})
        o.q, o.fn, o.need_inc = q, fn, False
        o.deps = self._deps(q, reads, writes, True)
        lane = self.lane_rr[q]
        self.lane_rr[q] = (lane + 1) % NLANES
        key = (q, lane)
        prev = self.lane_last.get(key)
        if prev is not None:
            o.deps.add(prev)
        cnt = self.lane_cnt.get(key, 0) + 16
        self.lane_cnt[key] = cnt
        o.lane, o.laneval = key, cnt
        o.idx = len(self.ops[q])
        self.ops[q].append(o)
        ev = ("d", key, cnt)
        self.lane_last[key] = ev
        self._commit(ev, reads, writes)
        return o

    def coll(self, q, reads, writes, *args, **kw):
        if not self.enabled:
            return _Op()
        o = _Op()
        o.q, o.fn, o.need_inc = q, ("collective_compute", args, kw), False
        o.deps = self._deps(q, reads, writes, True)
        key = ("cc", len(self.lane_cnt))
        self.lane_cnt[key] = CC_INC
        o.lane, o.laneval = key, CC_INC
        o.idx = len(self.ops[q])
        self.ops[q].append(o)
        ev = ("d", key, CC_INC)
        self.lane_last[key] = ev
        self._commit(ev, reads, writes)
        return o

    def barrier(self, q):
        if not self.enabled:
            return
        o = _Op()
        o.q, o.fn, o.need_inc, o.lane = q, None, False, None
        o.deps = set(self.lane_last.values())
        for e in self.CE:
            if self.ops[e]:
                last = [x for x in self.ops[e] if x.fn is not None and x.lane is None]
                if last:
                    o.deps.add(("c", e, last[-1].idx))
        o.idx = len(self.ops[q])
        self.ops[q].append(o)

    def wait_all(self, q):
        o = _Op()
        o.q, o.fn, o.need_inc, o.lane = q, None, False, None
        o.deps = set(self.lane_last.values())
        o.idx = len(self.ops[q])
        self.ops[q].append(o)

    def emit(self):
        nc = self.nc
        for q in self.QS:
            for o in self.ops[q]:
                for ev in o.deps:
                    if ev[0] == "c":
                        self.ops[ev[1]][ev[2]].need_inc = True
        nsem = {}
        for e in self.CE:
            cnt = 0
            for o in self.ops[e]:
                if o.need_inc:
                    o.semidx, o.semval = cnt // SEM_LIM, cnt % SEM_LIM + 1
                    cnt += 1
            nsem[e] = (cnt + SEM_LIM - 1) // SEM_LIM
        with contextlib.ExitStack() as st:
            sems = {}
            for e in self.CE:
                for k in range(nsem[e]):
                    sems[("c", e, k)] = st.enter_context(nc.semaphore(f"s_{e}_{k}"))
            for key in self.lane_cnt:
                sems[("d",) + key] = st.enter_context(nc.semaphore(f"l_{key[0]}_{key[1]}"))
            block = st.enter_context(nc.Block())

            def run(q, eng):
                waited = {}
                for o in self.ops[q]:
                    need = {}
                    for ev in o.deps:
                        if ev[0] == "c":
                            src = self.ops[ev[1]][ev[2]]
                            k, v = ("c", ev[1], src.semidx), src.semval
                        else:
                            k, v = ("d",) + ev[1], ev[2]
                        if need.get(k, 0) < v:
                            need[k] = v
                    for k, v in need.items():
                        if waited.get(k, 0) < v:
                            eng.wait_ge(sems[k], v)
                            waited[k] = v
                    if o.fn is None:
                        continue
                    name, args, kw = o.fn
                    ins = getattr(eng, name)(*args, **kw)
                    if o.lane is not None:
                        ins.then_inc(sems[("d",) + o.lane], CC_INC if o.lane[0] == "cc" else 16)
                    elif o.need_inc:
                        ins.then_inc(sems[("c", q, o.semidx)], 1)

            if self.ops["pe"]:
                @block.tensor
                def _(eng):
                    run("pe", eng)
            if self.ops["act"]:
                @block.scalar
                def _(eng):
                    run("act", eng)
            if self.ops["dve"]:
                @block.vector
                def _(eng):
                    run("dve", eng)
            if self.ops["pool"]:
                @block.gpsimd
                def _(eng):
                    run("pool", eng)
            if self.ops["sp"]:
                @block.sync
                def _(eng):
                    run("sp", eng)


FFN_GROUPS = [(0, 6), (6, 6), (12, 5), (17, 5)]
SB_BASE = 16512
SB_LIMIT = 16512 + 212800
_DTSZ = {F32: 4, BF16: 2}
NEG = -30000.0


class Arena:
    def __init__(self, nc):
        self.nc, self.off, self.n = nc, SB_BASE, 0

    def alloc(self, name, shape, dt=F32):
        n = _DTSZ[dt]
        for d in shape[1:]:
            n *= d
        off = (self.off + 31) // 32 * 32
        assert off + n <= SB_LIMIT, (name, off + n - SB_LIMIT)
        self.off = off + n
        self.n += 1
        return self.nc.alloc_sbuf_tensor_at(f"{name}_{self.n}", list(shape), dt, offset=off)


def build(stage="full"):
    nc = bass.Bass("TRN2", target_bir_lowering=False)
    P = Prog(nc)
    es = contextlib.ExitStack()
    A = Arena(nc)
    sb = A.alloc

    def dram(name, shape, kind="ExternalInput", dt=F32):
        return nc.dram_tensor(name, list(shape), dt, kind=kind)

    xs = dram("xs", [NTH, D])
    out = dram("out", [NTOK, D], kind="ExternalOutput")
    ident_d = dram("ident", [128, 128])
    maskA_d = dram("maskA", [128, 256])
    mask1_d = dram("mask1", [128, 256])
    wdram = {}
    for nm, shp in (("ffn1_gate", [D, DFF]), ("ffn1_up", [D, DFF]), ("ffn1_down", [DFF, D]),
                    ("ffn2_gate", [D, DFF]), ("ffn2_up", [D, DFF]), ("ffn2_down", [DFF, D]),
                    ("w_in", [D, 2560]), ("w_out", [D, D])):
        wdram[nm] = dram(nm, shp)
    vec_d = {}
    for nm in ("norm_ffn1", "norm_mix", "norm_ffn2", "norm_final"):
        vec_d[nm] = dram(nm, [1, D])
    b_attn_d = dram("b_in_attn", [768, 1])
    sinks_d = dram("attn_sinks", [1, 8])
    rw_d = {}
    rw_d["rwkv_shift_mix"] = dram("rwkv_shift_mix", [1792, 1])
    for nm in ("rwkv_w0", "rwkv_a0", "rwkv_k_k", "rwkv_k_a", "rwkv_r_k", "rwkv_ln_w", "rwkv_ln_b"):
        rw_d[nm] = dram(nm, [512, 1])
    rw_d["rwkv_w2"] = dram("rwkv_w2", [64, 512])
    rw_d["rwkv_a2"] = dram("rwkv_a2", [64, 512])
    rw_d["rwkv_g2"] = dram("rwkv_g2", [128, 512])
    cmask_d = dram("cmask", [128, 384])
    bones_d = dram("bones", [128, 128])
    scanm_d = dram("scanm", [1, 512])
    fm_d = dram("fm", [1, 8])
    use_cc = "ccin" not in stage
    if use_cc:
        cc_in = [dram(f"cc_in{j}", [128, 256], kind="Internal") for j in range(4)]
        cc_out = [dram(f"cc_out{j}", [4 * 128, 256], kind="Internal") for j in range(4)]

        def ccg_view(j):
            return cc_out[j].ap().rearrange("(r p) f -> p r f", p=128)
    else:
        ccg_d = dram("ccg", [4 * 128, 1024])
        sf_dbg = dram("sf_dbg", [128, 1024], kind="ExternalOutput")

        def ccg_view(j):
            return ccg_d.ap()[:, j * 256:(j + 1) * 256].rearrange("(r p) f -> p r f", p=128)

    X = sb("X", [128, NT, D])
    HT = sb("HT", [128, 8, NTH], BF16)
    ss = sb("ss", [128, 4 * NT])
    rstd = sb("rstd", [128, 4 * NT])
    eps_t = sb("eps_t", [128, 1])
    ident_f = sb("ident_f", [128, 128])
    ident_b = sb("ident_b", [128, 128], BF16)
    norm_mark = A.off
    gbc = sb("gbc", [128, D])
    hb = [sb(f"hb{i}", [128, D], BF16) for i in range(2)]
    junk = sb("junk", [128, D], BF16)
    phase_mark = A.off

    PSALL = es.enter_context(nc.psum_tensor("psall", [128, 4096], F32))
    PSALLB = PSALL.bitcast(BF16)

    def psf(bank, a=0, b=512):
        return PSALL[:, bank * 512 + a:bank * 512 + b]

    def psb(bank, a=0, b=1024):
        return PSALLB[:, bank * 1024 + a:bank * 1024 + b]

    def barrier():
        for q in ("pe", "act", "dve", "pool", "sp"):
            P.barrier(q)

    def mark(name):
        if ("stop_" + name + "_") in (stage + "_"):
            P.enabled = False

    P.dma("sp", [], ["ident_f"], ident_f[:], ident_d.ap())
    P.op("act", "copy", ["ident_f"], ["ident_b"], out=ident_b[:], in_=ident_f[:])
    P.op("dve", "memset", [], ["ss"], ss[:], 0.0)
    P.op("dve", "memset", [], ["eps_t"], eps_t[:], 1e-5)

    norm_ctr = [0]

    def load_gain(name):
        P.dma("sp", [], ["gbc"], gbc[:], vec_d[name].ap().partition_broadcast(128))

    def rmsnorm_tile(i, dst_fn):
        k = norm_ctr[0]
        norm_ctr[0] += 1
        col = k % (4 * NT)
        P.op("act", "activation", [("X", i), "ss"], ["junk", ("ss", col)],
             out=junk[:], in_=X[:, i, :], func=AF.Square, accum_out=ss[:, col:col + 1])
        P.op("act", "activation", [("ss", col), "eps_t"], [("rstd", col)],
             out=rstd[:, col:col + 1], in_=ss[:, col:col + 1], func=AF.Sqrt, bias=eps_t[:, 0:1], scale=1.0 / D)
        P.op("dve", "reciprocal", [("rstd", col)], [("rstd", col)],
             out=rstd[:, col:col + 1], in_=rstd[:, col:col + 1])
        dst_fn(i, col)

    def norm_to_HT(i, col):
        b = i % 2
        P.op("dve", "scalar_tensor_tensor", [("X", i), ("rstd", col), "gbc"], [("hb", b)],
             out=hb[b][:], in0=X[:, i, :], scalar=rstd[:, col:col + 1],
             in1=gbc[:], op0=ALU.mult, op1=ALU.mult)
        bank = 4 + (i % 4)
        for c in range(8):
            P.op("pe", "transpose", [("hb", b), "ident_b"], [("ps", bank)],
                 out=psb(bank, c * 128, (c + 1) * 128),
                 in_=hb[b][:, c * 128:(c + 1) * 128], identity=ident_b[:])
        P.op("act", "copy", [("ps", bank)], [("HT", i)],
             out=HT[:, :, i * 128:(i + 1) * 128],
             in_=psb(bank).rearrange("p (c t) -> p c t", c=8))

    def ffn(prefix, tiles):
        A.off = phase_mark
        Wg = [sb(f"Wg{i}", [128, 8, 768], BF16) for i in range(2)]
        Wu = [sb(f"Wu{i}", [128, 8, 768], BF16) for i in range(2)]
        Wd = [sb(f"Wd{i}", [128, 6, D], BF16) for i in range(2)]
        sg = [sb(f"sg{i}", [128, 256]) for i in range(2)]
        aT = [sb(f"aT{i}", [128, 256], BF16) for i in range(3)]
        gate, up, down = wdram[prefix + "_gate"], wdram[prefix + "_up"], wdram[prefix + "_down"]
        blocks = []
        t = list(tiles)
        if len(t) % 2 == 1:
            blocks.append(t[:1])
            t = t[1:]
        for j in range(0, len(t), 2):
            blocks.append(t[j:j + 2])
        cnt = 0
        for gi, (f0, nf) in enumerate(FFN_GROUPS):
            wb = gi % 2
            w = nf * 128
            for c in range(8):
                P.dma("pool", [], [("Wg", wb)], Wg[wb][:, c, 0:w],
                      gate.ap()[c * 128:(c + 1) * 128, f0 * 128:f0 * 128 + w])
            for c in range(8):
                P.dma("pool", [], [("Wu", wb)], Wu[wb][:, c, 0:w],
                      up.ap()[c * 128:(c + 1) * 128, f0 * 128:f0 * 128 + w])
            for c in range(nf):
                P.dma("pool", [], [("Wd", wb)], Wd[wb][:, c, :],
                      down.ap()[(f0 + c) * 128:(f0 + c + 1) * 128, :])
            for blk in blocks:
                T = 128 * len(blk)
                tok0 = blk[0] * 128
                htr = [("HT", t_) for t_ in blk]

                def gu(fc, k):
                    gb = 4 + (k % 2)
                    ub = 6 + (k % 2)
                    for kc in range(8):
                        P.op("pe", "matmul", [("Wg", wb)] + htr, [("ps", gb)],
                             psf(gb, 0, T), lhsT=Wg[wb][:, kc, fc * 128:(fc + 1) * 128],
                             rhs=HT[:, kc, tok0:tok0 + T], start=(kc == 0), stop=(kc == 7))
                    for kc in range(8):
                        P.op("pe", "matmul", [("Wu", wb)] + htr, [("ps", ub)],
                             psf(ub, 0, T), lhsT=Wu[wb][:, kc, fc * 128:(fc + 1) * 128],
                             rhs=HT[:, kc, tok0:tok0 + T], start=(kc == 0), stop=(kc == 7))
                    s_ = k % 2
                    a_ = k % 3
                    P.op("act", "activation", [("ps", gb)], [("sg", s_)],
                         out=sg[s_][:, 0:T], in_=psf(gb, 0, T), func=AF.Silu)
                    P.op("dve", "tensor_tensor", [("sg", s_), ("ps", ub)], [("aT", a_)],
                         out=aT[a_][:, 0:T], in0=sg[s_][:, 0:T], in1=psf(ub, 0, T), op=ALU.mult)

                def dn_(fc, k):
                    a_ = k % 3
                    for ti, tt in enumerate(blk):
                        for dh in range(2):
                            bank = 2 * ti + dh
                            P.op("pe", "matmul", [("aT", a_), ("Wd", wb)], [("ps", bank)],
                                 psf(bank), lhsT=aT[a_][:, ti * 128:(ti + 1) * 128],
                                 rhs=Wd[wb][:, fc, dh * 512:(dh + 1) * 512],
                                 start=(fc == 0), stop=(fc == nf - 1))

                ks = list(range(cnt, cnt + nf))
                cnt += nf
                gu(0, ks[0])
                for fc in range(nf):
                    if fc + 1 < nf:
                        gu(fc + 1, ks[fc + 1])
                    dn_(fc, ks[fc])
                for ti, tt in enumerate(blk):
                    for dh in range(2):
                        bank = 2 * ti + dh
                        P.op("dve", "scalar_tensor_tensor", [("ps", bank), ("X", tt)], [("X", tt)],
                             out=X[:, tt, dh * 512:(dh + 1) * 512], in0=psf(bank), scalar=0.5,
                             in1=X[:, tt, dh * 512:(dh + 1) * 512], op0=ALU.mult, op1=ALU.add)

    load_gain("norm_ffn1")
    for i in range(NT):
        P.dma("sp", [], [("X", i)], X[:, i, :], xs.ap()[i * 128:(i + 1) * 128, :])
    if not stage.startswith(("noffn", "mix")):
        for i in range(NT):
            rmsnorm_tile(i, norm_to_HT)
        ffn("ffn1", range(NT))
        barrier()

    w_in, w_out = wdram["w_in"], wdram["w_out"]
    if not stage.startswith("noffn"):
        load_gain("norm_mix")
        for i in range(NT):
            rmsnorm_tile(i, norm_to_HT)
        barrier()
        A.off = norm_mark
        WT = [sb(f"WT{i}", [128, 8, 128], BF16) for i in range(3)]
        wt_ctr = [0]

        def load_wtile(colspec):
            i = wt_ctr[0] % 3
            wt_ctr[0] += 1
            for kc in range(8):
                off = 0
                for (c0, n) in colspec:
                    P.dma("pool", [], [("WT", i)], WT[i][:, kc, off:off + n],
                          w_in.ap()[kc * 128:(kc + 1) * 128, c0:c0 + n])
                    off += n
            return i

        def proj(wi, tok0, T, bank, off=0):
            for kc in range(8):
                P.op("pe", "matmul", [("WT", wi)] + [("HT", t_) for t_ in range(tok0 // 128, (tok0 + T + 127) // 128)],
                     [("ps", bank)], psf(bank, off, off + T), lhsT=WT[wi][:, kc, :],
                     rhs=HT[:, kc, tok0:tok0 + T], start=(kc == 0), stop=(kc == 7))

        WO = [sb(f"WO{i}", [128, D], BF16) for i in range(2)]
        wo_ctr = [0]

        def load_wout(row0):
            i = wo_ctr[0] % 2
            wo_ctr[0] += 1
            P.dma("pool", [], [("WO", i)], WO[i][:], w_out.ap()[row0:row0 + 128, :])
            return i

        def wout_accum(srcs):
            for n in range(1, NT):
                for dh in range(2):
                    bank = 2 * (n % 2) + dh
                    for j, (fn, wi, res) in enumerate(srcs):
                        P.op("pe", "matmul", res(n) + [("WO", wi)], [("ps", bank)], psf(bank),
                             lhsT=fn(n), rhs=WO[wi][:, dh * 512:(dh + 1) * 512],
                             start=(j == 0), stop=(j == len(srcs) - 1))
                    P.op("dve", "tensor_tensor", [("ps", bank), ("X", n)], [("X", n)],
                         out=X[:, n, dh * 512:(dh + 1) * 512], in0=psf(bank),
                         in1=X[:, n, dh * 512:(dh + 1) * 512], op=ALU.add)

        mark_mix = A.off
        do_attn = 'noattn' not in stage
        do_rwkv = 'norwkv' not in stage
        ACOL = 1792
        battn = sb("battn", [128, 8])
        for c in range(4):
            P.dma("sp", [], ["battn"], battn[:, c:c + 1], b_attn_d.ap()[c * 128:(c + 1) * 128, :])
        for g in range(2):
            for hf in range(2):
                P.dma("sp", [], ["battn"], battn[hf * 64:(hf + 1) * 64, 4 + g:5 + g],
                      b_attn_d.ap()[512 + g * 64:512 + (g + 1) * 64, :])
        P.dma("sp", [], ["battn"], battn[:, 6:7], b_attn_d.ap()[640:768, :])
        sinks = sb("sinks", [128, 8])
        P.dma("sp", [], ["sinks"], sinks[:], sinks_d.ap().partition_broadcast(128))
        maskA = sb("maskA", [128, 256], BF16)
        mask1 = sb("mask1", [128, 256], BF16)
        P.dma("pool", [], ["maskA"], maskA[:], maskA_d.ap())
        P.dma("pool", [], ["mask1"], mask1[:], mask1_d.ap())
        battn_s = sb("battn_s", [128, 4])
        P.op("act", "mul", ["battn"], ["battn_s"], out=battn_s[:], in_=battn[:, 0:4], mul=0.125)
        mark("a1")
        VT = sb("VT", [128, NTH], BF16)
        Vtok = sb("Vtok", [128, NT, 128], BF16)
        Vpad = [sb(f"Vpad{i}", [128, NT, 128], BF16) for i in range(2)]
        Qg = sb("Qg", [128, 2, NTOK], BF16)
        KP = [sb(f"KP{i}", [128, NTH], BF16) for i in range(2)]
        for par in range(2):
            P.op("dve", "memset", [], [("KP", par)], KP[par][:], 0.0)
        AO = sb("AO", [128, 2, NTOK], BF16)
        sm = [sb(f"sm{i}", [128, 4, 256]) for i in range(4)]
        Pb = [sb(f"Pb{i}", [128, 4, 256], BF16) for i in range(4)]
        PT = [sb(f"PT{i}", [128, 8, 128], BF16) for i in range(4)]
        st = [sb(f"st{i}", [128, 20]) for i in range(4)]

        tok_blocks = [(0, 128)] + [(128 + 512 * j, 512) for j in range(4)]

        def proj_to(colspec, dst_fn, bias_ap, res, blocks, scale=1.0):
            wi = load_wtile(colspec)
            for bi, (tok0, T) in enumerate(blocks):
                bank = 4 + (bi % 4)
                proj(wi, tok0, T, bank)
                P.op("act", "activation", [("ps", bank), "battn", "battn_s"], [res],
                     out=dst_fn(tok0, T), in_=psf(bank, 0, T), func=AF.Identity, bias=bias_ap, scale=scale)

        proj_to([(ACOL + 640, 128)], lambda t0, T: VT[:, t0:t0 + T], battn[:, 6:7], "VT", tok_blocks)
        mark("a2")
        for n in range(NT):
            bank = 4 + (n % 4)
            P.op("pe", "transpose", ["VT", "ident_b"], [("ps", bank)], out=psb(bank, 0, 128),
                 in_=VT[:, n * 128:(n + 1) * 128], identity=ident_b[:])
            P.op("act", "copy", [("ps", bank)], [("Vtok", n)], out=Vtok[:, n, :], in_=psb(bank, 0, 128))
        mark("a3")
        for g in range(2 if do_attn else 0):
            for par in range(2):
                P.op("dve", "memset", [], [("Vpad", par)], Vpad[par][:], 0.0)
                P.op("dve", "tensor_copy", [("Vtok", n) for n in range(NT)], [("Vpad", par)],
                     out=Vpad[par][:, :, par * 64:(par + 1) * 64], in_=Vtok[:, :, g * 64:(g + 1) * 64])
            for lc in range(2):
                proj_to([(ACOL + (2 * g + lc) * 128, 128)],
                        lambda t0, T, lc=lc: Qg[:, lc, t0 - 128:t0 - 128 + T], battn_s[:, 2 * g + lc:2 * g + lc + 1],
                        ("Qg", lc), tok_blocks[1:], scale=0.125)
            wi_k = load_wtile([(ACOL + 512 + g * 64, 64), (ACOL + 512 + g * 64, 64)])
            for bi, (tok0, T_) in enumerate(tok_blocks):
                bank = 4 + (bi % 4)
                proj(wi_k, tok0, T_, bank)
                for par in range(2):
                    ps_ = slice(par * 64, (par + 1) * 64)
                    P.op("act", "activation", [("ps", bank), "battn"], [("KP", par)],
                         out=KP[par][ps_, tok0:tok0 + T_], in_=psf(bank, 0, T_)[ps_], func=AF.Identity,
                         bias=battn[ps_, 4 + g:5 + g], scale=1.0)
            mark("a4")

            def attn_stage(n, k, g=g):
                u = n % 4
                q0 = (n - 1) * 128
                sb0 = 2 * u
                mx, dd, es_, den = st[u][:, 0:4], st[u][:, 4:8], st[u][:, 8:12], st[u][:, 12:16]
                mk = mask1 if n == 1 else maskA
                mkn = "mask1" if n == 1 else "maskA"
                nmx = st[u][:, 16:20]
                if k == 0:
                    for j in range(4):
                        lc, par = j // 2, j % 2
                        bank = sb0 + j // 2
                        P.op("pe", "matmul", [("Qg", lc), ("KP", par)], [("ps", bank)],
                             psf(bank, (j % 2) * 256, (j % 2) * 256 + 256),
                             lhsT=Qg[:, lc, q0:q0 + 128],
                             rhs=KP[par][:, (n - 1) * 128:(n + 1) * 128], start=True, stop=False)
                        P.op("pe", "matmul", ["ident_b", mkn], [("ps", bank)],
                             psf(bank, (j % 2) * 256, (j % 2) * 256 + 256),
                             lhsT=ident_b[:], rhs=mk[:], start=False, stop=True)
                elif k == 1:
                    for hb_ in range(2):
                        P.op("dve", "tensor_reduce", [("ps", sb0 + hb_)], [("st", u)], out=mx[:, 2 * hb_:2 * hb_ + 2],
                             in_=psf(sb0 + hb_).rearrange("p (h k) -> p h k", h=2), axis=AX.X, op=ALU.max)
                    P.op("dve", "tensor_tensor", [("st", u), "sinks"], [("st", u)], out=mx, in0=mx,
                         in1=sinks[:, 4 * g:4 * g + 4], op=ALU.max)
                    P.op("dve", "tensor_tensor", [("st", u), "sinks"], [("st", u)], out=dd,
                         in0=sinks[:, 4 * g:4 * g + 4], in1=mx, op=ALU.subtract)
                    P.op("dve", "tensor_scalar", [("st", u)], [("st", u)], out=nmx, in0=mx, scalar1=-1.0, scalar2=None,
                         op0=ALU.mult)
                elif k == 2:
                    P.op("dve", "memset", [], [("st", u)], den, 0.0)
                    for j in range(4):
                        bank = sb0 + j // 2
                        P.op("act", "activation", [("ps", bank), ("st", u)], [("sm", u), ("st", u)], out=sm[u][:, j, :],
                             in_=psf(bank, (j % 2) * 256, (j % 2) * 256 + 256), func=AF.Exp, bias=nmx[:, j:j + 1], scale=1.0,
                             accum_out=den[:, j:j + 1])
                    P.op("act", "activation", [("st", u)], [("st", u)], out=es_, in_=dd, func=AF.Exp)
                elif k == 3:
                    P.op("dve", "tensor_tensor", [("st", u)], [("st", u)], out=den, in0=den, in1=es_, op=ALU.add)
                    P.op("dve", "reciprocal", [("st", u)], [("st", u)], out=den, in_=den)
                    P.op("dve", "tensor_tensor", [("sm", u), ("st", u)], [("Pb", u)], out=Pb[u][:], in0=sm[u][:],
                         in1=den.unsqueeze(2).to_broadcast([128, 4, 256]), op=ALU.mult)
                elif k == 4:
                    tb = sb0
                    for j in range(4):
                        for kh in range(2):
                            P.op("pe", "transpose", [("Pb", u), "ident_b"], [("ps", tb)],
                                 out=psb(tb, (j * 2 + kh) * 128, (j * 2 + kh + 1) * 128),
                                 in_=Pb[u][:, j, kh * 128:(kh + 1) * 128], identity=ident_b[:])
                    P.op("act", "copy", [("ps", tb)], [("PT", u)], out=PT[u][:],
                         in_=psb(tb).rearrange("p (c t) -> p c t", c=8))
                elif k == 5:
                    ab = sb0 + 1
                    for lc in range(2):
                        idx = 0
                        for par in range(2):
                            j = lc * 2 + par
                            for kh in range(2):
                                P.op("pe", "matmul", [("Vpad", par), ("PT", u)], [("ps", ab)],
                                     psf(ab, lc * 128, (lc + 1) * 128),
                                     lhsT=Vpad[par][:, n - 1 + kh, :], rhs=PT[u][:, j * 2 + kh, :],
                                     start=(idx == 0), stop=(idx == 3))
                                idx += 1
                    P.op("act", "copy", [("ps", ab)], [("AO", n)], out=AO[:, :, q0:q0 + 128],
                         in_=psf(ab, 0, 256).rearrange("p (c t) -> p c t", c=2))

            for n0 in range(1, NT, 4):
                for k in range(6):
                    for n in range(n0, n0 + 4):
                        attn_stage(n, k)
            mark("a6")
            wis = [load_wout(512 + (2 * g + lc) * 128) for lc in range(2)]
            wout_accum([(lambda n, lc=lc: AO[:, lc, (n - 1) * 128:n * 128], wis[lc], lambda n: [("AO", n)]) for lc in range(2)])
        barrier()
        A.off = mark_mix
        C0 = float(np.exp(-0.5))
        pv = sb("pv", [128, 48])
        SMX, W0, A0, KK_, KA_, RK_, LNW, LNB = 0, 14, 18, 22, 26, 30, 34, 38
        for c in range(14):
            P.dma("sp", [], ["pv"], pv[:, SMX + c:SMX + c + 1], rw_d["rwkv_shift_mix"].ap()[c * 128:(c + 1) * 128, :])
        for nm, col in (("rwkv_w0", W0), ("rwkv_a0", A0), ("rwkv_k_k", KK_), ("rwkv_k_a", KA_),
                        ("rwkv_r_k", RK_), ("rwkv_ln_w", LNW), ("rwkv_ln_b", LNB)):
            for j in range(4):
                P.dma("sp", [], ["pv"], pv[:, col + j:col + j + 1], rw_d[nm].ap()[j * 128:(j + 1) * 128, :])
        W2P = sb("W2P", [128, 512], BF16)
        A2P = sb("A2P", [128, 512], BF16)
        G2 = sb("G2", [128, 512], BF16)
        P.op("dve", "memset", [], ["W2A2"], W2P[:], 0.0)
        P.op("dve", "memset", [], ["W2A2"], A2P[:], 0.0)
        P.dma("pool", [], ["W2A2"], W2P[0:64, :], rw_d["rwkv_w2"].ap())
        P.dma("pool", [], ["W2A2"], A2P[64:128, :], rw_d["rwkv_a2"].ap())
        P.dma("pool", [], ["G2"], G2[:], rw_d["rwkv_g2"].ap())
        MK = sb("MK", [128, 384], BF16)
        P.dma("pool", [], ["MK"], MK[:], cmask_d.ap())
        bones_f = sb("bones_f", [128, 128])
        bones_b = sb("bones_b", [128, 128], BF16)
        bones_s = sb("bones_s", [128, 128])
        P.dma("sp", [], ["bones_f"], bones_f[:], bones_d.ap())
        P.op("act", "copy", ["bones_f"], ["bones_b"], out=bones_b[:], in_=bones_f[:])
        P.op("act", "mul", ["bones_f"], ["bones_s"], out=bones_s[:], in_=bones_f[:], mul=1.0 / 64)
        scanm = sb("scanm", [128, 512], BF16)
        P.dma("pool", [], ["scanm"], scanm[:], scanm_d.ap().partition_broadcast(128))
        fm = sb("fm", [128, 8])
        P.dma("sp", [], ["fm"], fm[:], fm_d.ap().partition_broadcast(128))
        gneps = sb("gneps", [128, 1])
        P.op("dve", "memset", [], ["gneps"], gneps[:], 64e-5)
        WA = sb("WA", [128, NTOK], BF16)
        GL = sb("GL", [128, NTOK], BF16)
        pf = [sb("pf0", [128, 513])]
        ga_off = A.off
        T = [sb(f"T{i}", [128, 512]) for i in range(10)]
        tb16 = [sb(f"tb16_{i}", [128, 512], BF16) for i in range(2)]
        A_save = A.off
        A.off = ga_off
        GA = sb("GA", [128, 4, 256])
        LT = sb("LT", [128, 3, 128])
        A.off = (A.off + 2047) // 2048 * 2048 if False else A.off
        HP_ = None
        A.off = A_save
        GA_RES = [("T", i) for i in range(3)]
        AR = sb("AR", [128, 8, 256], BF16)
        EXP = {nm: sb(nm, [128, 8, 128], BF16) for nm in ("BT", "KT", "BH", "KH", "VE")}
        P.op("dve", "memset", [], ["AR"], AR[:], 0.0)
        for nm in EXP:
            P.op("dve", "memset", [], [nm], EXP[nm][:], 0.0)
        NB = [sb(f"NB{i}", [128, 384], BF16) for i in range(4)]
        KB = [sb(f"KB{i}", [128, 256], BF16) for i in range(4)]
        TK = [sb(f"TK{i}", [128, 256], BF16) for i in range(4)]
        Zt = [sb(f"Z{i}", [128, 256], BF16) for i in range(4)]
        NM = [[sb(f"NM{i}_{k}", [128, 256], BF16) for k in range(2)] for i in range(4)]
        Qt = [[sb(f"Qt{p}_{i}", [128, 128], BF16) for i in range(4)] for p in range(2)]
        GT = [[sb(f"GT{p}_{i}", [128, 128], BF16) for i in range(4)] for p in range(2)]
        Hb = [[sb(f"Hb{p}_{i}", [128, 128], BF16) for i in range(4)] for p in range(2)]
        SS = [[sb(f"SS{p}_{i}", [128, 256], BF16) for i in range(4)] for p in range(2)]
        WC8 = [sb(f"WC8_{i}", [128, 8]) for i in range(2)]
        grp_ctr = [0]
        pendingB = [[]]
        SF = sb("SF", [128, 256])
        OL = sb("OL", [128, NTOK])
        QH = sb("QH", [128, NTOK], BF16)
        BON = sb("BON", [128, NTOK - 512], BF16)
        BON0 = [sb(f"BON0_{i}", [128, 512], BF16) for i in range(2)]
        MX = nc.alloc_sbuf_tensor_at("MX_alias", [128, NTOK], BF16, offset=SB_BASE)
        Sst = sb("Sst", [128, 128])
        Sstb = sb("Sstb", [128, 128], BF16)
        bank_ctr = [0]

        cap_ctr = [0]

        def nb():
            if P.capture is not None:
                cap_ctr[0] = (cap_ctr[0] + 1) % 2
                return 6 + cap_ctr[0]
            bank_ctr[0] = (bank_ctr[0] + 1) % 6
            return bank_ctr[0]

        def v3(ap):
            return ap.rearrange("p (c t) -> p c t", c=8)

        pf_ctr = [0]

        def proj_shift(wi, smx_col, tb, dst, dst_res):
            u = 0
            tok0 = 128 + tb * 512
            bank = nb()
            proj(wi, tok0, 512, bank)
            P.op("act", "copy", [("ps", bank)], [("pf", u)], out=pf[u][:, 1:513], in_=psf(bank))
            bank2 = nb()
            for kc in range(8):
                P.op("pe", "matmul", [("WT", wi), ("HT", tok0 // 128 - 1)], [("ps", bank2)], psf(bank2, 0, 1),
                     lhsT=WT[wi][:, kc, :], rhs=HT[:, kc, tok0 - 1:tok0], start=(kc == 0), stop=(kc == 7))
            P.op("act", "copy", [("ps", bank2)], [("pf", u)], out=pf[u][:, 0:1], in_=psf(bank2, 0, 1))
            P.op("dve", "tensor_tensor", [("pf", u)], [dst_res], out=dst, in0=pf[u][:, 0:512], in1=pf[u][:, 1:513],
                 op=ALU.subtract)
            P.op("dve", "scalar_tensor_tensor", [("pf", u), dst_res, "pv"], [dst_res], out=dst, in0=dst,
                 scalar=pv[:, SMX + smx_col:SMX + smx_col + 1], in1=pf[u][:, 1:513], op0=ALU.mult, op1=ALU.add)

        mark("r1")
        wl1 = load_wtile([(1536, 128)])
        wl2 = load_wtile([(1664, 128)])
        for tb in range(4):
            cs_ = slice(tb * 512, (tb + 1) * 512)
            proj_shift(wl1, 12, tb, T[0][:], ("T", 0))
            P.op("act", "activation", [("T", 0)], [("WA", tb)], out=WA[0:64, cs_], in_=T[0][0:64, :], func=AF.Tanh)
            P.op("act", "copy", [("T", 0)], [("WA", tb)], out=WA[64:128, cs_], in_=T[0][64:128, :])
            proj_shift(wl2, 13, tb, T[1][:], ("T", 1))
            P.op("act", "activation", [("T", 1)], [("GL", tb)], out=GL[:, cs_], in_=T[1][:], func=AF.Sigmoid)

        mark("r2")
        def ew(q, name, reads, writes, **kw):
            P.op(q, name, reads, writes, **kw)

        chunk_ctr = [0]
        def setup_pair(j):
            wr_ = load_wtile([(j * 128, 128)])
            wk_ = load_wtile([(512 + j * 128, 128)])
            wv_ = load_wtile([(1024 + j * 128, 128)])
            r_s, k_s, v_s, sgw, cs, a_t, kkn, kp, Ep, tmp = [T[i][:] for i in range(10)]
            R_ = lambda i: ("T", i)
            b_t, Eex, Em, Eh = k_s, sgw, cs, a_t

            def prep_main(tb, j=j, wr_=wr_, wk_=wk_, wv_=wv_):
                cs_ = slice(tb * 512, (tb + 1) * 512)
                proj_shift(wr_, j, tb, r_s, R_(0))
                proj_shift(wk_, 4 + j, tb, k_s, R_(1))
                proj_shift(wv_, 8 + j, tb, v_s, R_(2))
                bw = nb()
                P.op("pe", "matmul", ["W2A2", ("WA", tb)], [("ps", bw)], psf(bw), lhsT=W2P[:, j * 128:(j + 1) * 128],
                     rhs=WA[:, cs_], start=True, stop=True)
                P.op("act", "activation", [("ps", bw), "pv"], [R_(3)], out=sgw, in_=psf(bw), func=AF.Sigmoid,
                     bias=pv[:, W0 + j:W0 + j + 1], scale=1.0)
                ba = nb()
                P.op("pe", "matmul", ["W2A2", ("WA", tb)], [("ps", ba)], psf(ba), lhsT=A2P[:, j * 128:(j + 1) * 128],
                     rhs=WA[:, cs_], start=True, stop=True)
                P.op("act", "activation", [("ps", ba), "pv"], [R_(5)], out=a_t, in_=psf(ba), func=AF.Sigmoid,
                     bias=pv[:, A0 + j:A0 + j + 1], scale=1.0)
                P.op("dve", "tensor_scalar", [R_(1), "pv"], [R_(6)], out=kkn, in0=k_s, scalar1=pv[:, KK_ + j:KK_ + j + 1],
                     scalar2=None, op0=ALU.mult)
                P.op("act", "activation", [R_(6)], [("tb16", 0)], out=tb16[0][:], in_=kkn, func=AF.Square)
                bq = nb()
                P.op("pe", "matmul", ["bones_b", ("tb16", 0)], [("ps", bq)], psf(bq), lhsT=bones_b[:], rhs=tb16[0][:],
                     start=True, stop=True)
                P.op("dve", "tensor_scalar", [("ps", bq)], [R_(9)], out=tmp, in0=psf(bq), scalar1=1e-24, scalar2=None,
                     op0=ALU.max)
                P.op("act", "activation", [R_(9)], [R_(9)], out=tmp, in_=tmp, func=AF.Sqrt)
                P.op("dve", "reciprocal", [R_(9)], [R_(9)], out=tmp, in_=tmp)
                P.op("dve", "tensor_tensor", [R_(6), R_(9)], [R_(6)], out=kkn, in0=kkn, in1=tmp, op=ALU.mult)
                P.op("dve", "tensor_scalar", [R_(5), "pv"], [R_(7)], out=kp, in0=a_t, scalar1=-1.0,
                     scalar2=pv[:, KA_ + j:KA_ + j + 1], op0=ALU.add, op1=ALU.mult)
                P.op("dve", "scalar_tensor_tensor", [R_(7), R_(1)], [R_(7)], out=kp, in0=kp, scalar=1.0, in1=k_s,
                     op0=ALU.add, op1=ALU.mult)
                b_t = k_s
                P.op("dve", "tensor_tensor", [R_(6), R_(5)], [R_(1)], out=b_t, in0=kkn, in1=a_t, op=ALU.mult)
                P.op("dve", "scalar_tensor_tensor", [R_(0), R_(7), "pv"], [("tb16", 1)], out=tb16[1][:], in0=r_s,
                     scalar=pv[:, RK_ + j:RK_ + j + 1], in1=kp, op0=ALU.mult, op1=ALU.mult)
                bb = nb()
                P.op("pe", "matmul", ["bones_b", ("tb16", 1)], [("ps", bb)], psf(bb), lhsT=bones_b[:], rhs=tb16[1][:],
                     start=True, stop=True)
                P.op("dve", "tensor_tensor", [("ps", bb), R_(2)], [("BON", tb) if tb else ("BON0", j % 2)], out=(BON[:, (tb - 1) * 512:tb * 512] if tb else BON0[j % 2][:]), in0=psf(bb), in1=v_s,
                     op=ALU.mult)
                P.op("dve", "tensor_tensor_scan", [R_(3), "scanm"], [R_(4)], out=cs, data0=scanm[:], data1=sgw, initial=0.0,
                     op0=ALU.mult, op1=ALU.add)
                P.op("dve", "tensor_tensor", [R_(4), R_(3)], [R_(3)], out=sgw, in0=cs, in1=sgw, op=ALU.subtract)
                P.op("act", "activation", [R_(3)], [R_(3)], out=sgw, in_=sgw, func=AF.Exp, scale=-C0)
                Eex = sgw
                P.op("act", "activation", [R_(4)], [R_(8)], out=Ep, in_=cs, func=AF.Exp, scale=-C0)
                P.op("act", "activation", [R_(4)], [R_(4)], out=cs, in_=cs, func=AF.Exp, scale=C0)
                Em = cs
                Eh = a_t
                P.op("dve", "tensor_tensor", [R_(4), R_(8)], [R_(5)], out=v3(Eh), in0=v3(Em),
                     in1=v3(Ep)[:, :, 63:64].to_broadcast([128, 8, 64]), op=ALU.mult)
                mark("r3")

            def expansions(tb):
                for hf in range(2):
                    ps_ = slice(hf * 64, (hf + 1) * 64)
                    cA = slice(hf * 64, (hf + 1) * 64)
                    cR = slice(128 + hf * 64, 128 + (hf + 1) * 64)
                    q = "dve" if hf == 0 else "pool"
                    ew("dve", "scalar_tensor_tensor", [R_(6), R_(3)], ["AR"], out=AR[ps_, :, cA], in0=v3(kkn)[ps_], scalar=-1.0,
                       in1=v3(Eex)[ps_], op0=ALU.mult, op1=ALU.mult)
                    ew(q, "tensor_tensor", [R_(0), R_(8)], ["AR"], out=AR[ps_, :, cR], in0=v3(r_s)[ps_], in1=v3(Ep)[ps_],
                       op=ALU.mult)
                    ew(q, "tensor_tensor", [R_(1), R_(4)], ["BT"], out=EXP["BT"][ps_, :, cA], in0=v3(b_t)[ps_],
                       in1=v3(Em)[ps_], op=ALU.mult)
                    ew(q, "tensor_tensor", [R_(7), R_(4)], ["KT"], out=EXP["KT"][ps_, :, cA], in0=v3(kp)[ps_],
                       in1=v3(Em)[ps_], op=ALU.mult)
                    ew(q, "tensor_tensor", [R_(1), R_(5)], ["BH"], out=EXP["BH"][ps_, :, cA], in0=v3(b_t)[ps_],
                       in1=v3(Eh)[ps_], op=ALU.mult)
                    ew(q, "tensor_tensor", [R_(7), R_(5)], ["KH"], out=EXP["KH"][ps_, :, cA], in0=v3(kp)[ps_],
                       in1=v3(Eh)[ps_], op=ALU.mult)
                    ew(q, "tensor_copy", [R_(2)], ["VE"], out=EXP["VE"][ps_, :, cA], in_=v3(v_s)[ps_])
                mark("r4")

            def chunk_groups(tb, prep_thunks, j=j):
                n_prep = len(prep_thunks)
                per_unit = (n_prep + 79) // 80
                prep_pos = [0]
                WCt = WC8[tb % 2]
                P.op("dve", "tensor_copy", [R_(8)], [("WC8", tb % 2)], out=WCt[:], in_=v3(Ep)[:, :, 63])
                for g4 in range(2):
                    gi = grp_ctr[0]
                    grp_ctr[0] += 1
                    pset = gi % 2
                    unitsA = []
                    for stage_grp in ((0, 1), (2,), (3,), (4,), (5,), (6,), (7,), (8,), (9, 10, 11)):
                        for s_ in range(4):
                            for stage_i in stage_grp:
                                unitsA.append((stage_i, s_))
                    ctxs = []
                    for s_ in range(4):
                        c = g4 * 4 + s_
                        ctxs.append(dict(c=c, tcol=tb * 512 + c * 64, ARc=AR[:, c, :],
                                         BTc=EXP["BT"][:, c, :], KTc=EXP["KT"][:, c, :], BHc=EXP["BH"][:, c, :],
                                         KHc=EXP["KH"][:, c, :], VEc=EXP["VE"][:, c, :], nmi=0))
                    first_grp = (tb == 0 and g4 == 0)
                    if first_grp:
                        P.op("dve", "memset", [], [("Sst", pset, 0)], SS[pset][0][:, 0:128], 0.0)
                        P.op("dve", "tensor_copy", ["ident_b"], [("Sst", pset, 0)], out=SS[pset][0][:, 128:256], in_=ident_b[:])

                    def unitA(stage_i, s_, tb=tb, pset=pset, ctxs=ctxs, WCt=WCt):
                        cx = ctxs[s_]
                        ARc, BTc, KTc, BHc, KHc, VEc = cx["ARc"], cx["BTc"], cx["KTc"], cx["BHc"], cx["KHc"], cx["VEc"]
                        if stage_i == 0:
                            b1, b2, b3 = nb(), nb(), nb()
                            P.op("pe", "matmul", ["BT", "AR"], [("ps", b1)], psf(b1, 0, 256), lhsT=BTc, rhs=ARc, start=True, stop=True)
                            P.op("pe", "matmul", ["KT", "AR"], [("ps", b2)], psf(b2, 0, 256), lhsT=KTc, rhs=ARc, start=True, stop=True)
                            P.op("pe", "matmul", ["AR", "BT"], [("ps", b3)], psf(b3, 0, 128), lhsT=ARc[:, 0:128], rhs=BTc,
                                 start=True, stop=True)
                            P.op("dve", "tensor_tensor", [("ps", b1), "MK"], [("NB", s_)], out=NB[s_][:, 0:256], in0=psf(b1, 0, 256),
                                 in1=MK[:, 0:256], op=ALU.mult)
                            P.op("dve", "tensor_tensor", [("ps", b2), "MK"], [("KB", s_)], out=KB[s_][:], in0=psf(b2, 0, 256),
                                 in1=MK[:, 0:256], op=ALU.mult)
                            P.op("dve", "tensor_tensor", [("ps", b3), "MK"], [("NM", s_, 0)], out=NM[s_][0][:, 128:256], in0=psf(b3, 0, 128),
                                 in1=MK[:, 256:384], op=ALU.mult)
                        elif stage_i == 1:
                            b4 = nb()
                            for ti, src in enumerate((VEc, KHc, ARc[:, 0:128], BHc)):
                                P.op("pe", "transpose", ["VE", "KH", "AR", "BH", "ident_b"], [("ps", b4)],
                                     out=psb(b4, ti * 128, (ti + 1) * 128), in_=src, identity=ident_b[:])
                            P.op("act", "copy", [("ps", b4)], [("TK", s_)], out=TK[s_][:], in_=psb(b4, 0, 256))
                            P.op("act", "copy", [("ps", b4)], [("Z", s_)], out=Zt[s_][:, 128:256], in_=psb(b4, 256, 384))
                            P.op("act", "copy", [("ps", b4)], [("NB", s_)], out=NB[s_][:, 256:384], in_=psb(b4, 384, 512))
                        elif stage_i == 2:
                            b5 = nb()
                            P.op("pe", "matmul", [("KB", s_), ("TK", s_)], [("ps", b5)], psf(b5, 0, 128), lhsT=KB[s_][:, 0:128],
                                 rhs=TK[s_][:, 0:128], start=True, stop=True)
                            P.op("act", "copy", [("ps", b5)], [("Z", s_)], out=Zt[s_][:, 0:128], in_=psf(b5, 0, 128))
                            P.op("pool", "tensor_copy", [("NB", s_)], [("NM", s_, 0)], out=NM[s_][0][:, 0:128], in_=NB[s_][:, 0:128])
                            cx["nmi"] = 0
                        elif 3 <= stage_i <= 8:
                            lvl = stage_i - 3
                            nmi = cx["nmi"]
                            Ncur, Mcur = NM[s_][nmi][:, 0:128], NM[s_][nmi][:, 128:256]
                            bz = nb()
                            if False:
                                P.op("pe", "matmul", ["ident_b", ("Z", s_)], [("ps", bz)], psf(bz, 0, 256), lhsT=ident_b[:], rhs=Zt[s_][:],
                                     start=True, stop=False)
                                P.op("pe", "matmul", [("NM", s_, nmi), ("Z", s_)], [("ps", bz)], psf(bz, 0, 256), lhsT=Ncur, rhs=Zt[s_][:],
                                     start=False, stop=True)
                                P.op("act", "copy", [("ps", bz)], [("Z", s_)], out=Zt[s_][:], in_=psf(bz, 0, 256))
                            else:
                                P.op("pe", "matmul", [("NM", s_, nmi), ("Z", s_)], [("ps", bz)], psf(bz, 0, 256), lhsT=Ncur, rhs=Zt[s_][:],
                                     start=True, stop=True)
                                P.op("dve", "tensor_tensor", [("ps", bz), ("Z", s_)], [("Z", s_)], out=Zt[s_][:], in0=psf(bz, 0, 256),
                                     in1=Zt[s_][:], op=ALU.add)
                            if lvl < 5:
                                bs = nb()
                                P.op("pe", "matmul", [("NM", s_, nmi)], [("ps", bs)], psf(bs, 0, 128), lhsT=Mcur, rhs=Ncur, start=True, stop=True)
                                P.op("pe", "matmul", [("NM", s_, nmi)], [("ps", bs)], psf(bs, 128, 256), lhsT=Ncur, rhs=Mcur, start=True, stop=True)
                                P.op("act", "copy", [("ps", bs)], [("NM", s_, 1 - nmi)], out=NM[s_][1 - nmi][:], in_=psf(bs, 0, 256))
                                cx["nmi"] = 1 - nmi
                        elif stage_i == 9:
                            bqg = nb()
                            P.op("pe", "matmul", [("Z", s_), ("NB", s_)], [("ps", bqg)], psf(bqg, 0, 256), lhsT=Zt[s_][:, 128:256],
                                 rhs=NB[s_][:, 128:384], start=True, stop=True)
                            P.op("dve", "tensor_tensor", [("ps", bqg), "AR"], [("Qt", pset, s_)], out=Qt[pset][s_][:], in0=psf(bqg, 0, 128),
                                 in1=ARc[:, 128:256], op=ALU.add)
                            P.op("dve", "scalar_tensor_tensor", [("ps", bqg), "ident_f", ("WC8", tb % 2)], [("GT", pset, s_)],
                                 out=GT[pset][s_][:], in0=ident_f[:], scalar=WCt[:, cx["c"]:cx["c"] + 1], in1=psf(bqg, 128, 256),
                                 op0=ALU.mult, op1=ALU.add)
                        elif stage_i == 10:
                            bhh = nb()
                            P.op("pe", "matmul", [("NB", s_), ("Z", s_)], [("ps", bhh)], psf(bhh, 0, 128), lhsT=NB[s_][:, 256:384],
                                 rhs=Zt[s_][:, 0:128], start=True, stop=False)
                            P.op("pe", "matmul", [("TK", s_)], [("ps", bhh)], psf(bhh, 0, 128), lhsT=TK[s_][:, 128:256], rhs=TK[s_][:, 0:128],
                                 start=False, stop=True)
                            P.op("act", "copy", [("ps", bhh)], [("Hb", pset, s_)], out=Hb[pset][s_][:], in_=psf(bhh, 0, 128))
                        elif stage_i == 11:
                            bo = nb()
                            P.op("pe", "matmul", [("Z", s_), ("NB", s_)], [("ps", bo)], psf(bo, 0, 128), lhsT=Zt[s_][:, 0:128],
                                 rhs=NB[s_][:, 128:256], start=True, stop=False)
                            P.op("pe", "matmul", [("TK", s_), ("KB", s_)], [("ps", bo)], psf(bo, 0, 128), lhsT=TK[s_][:, 0:128],
                                 rhs=KB[s_][:, 128:256], start=False, stop=True)
                            for hf in range(2):
                                ps_ = slice(hf * 64, (hf + 1) * 64)
                                P.op("act", "copy", [("ps", bo)], [("OL", cx["tcol"] // 64)], out=OL[ps_, cx["tcol"]:cx["tcol"] + 64],
                                     in_=psf(bo, hf * 64, (hf + 1) * 64)[ps_])

                    def unitB(kind, s_, tb=tb, g4=g4, pset=pset, ctxs=ctxs):
                        cx = ctxs[s_]
                        Sin = SS[pset][s_]
                        if kind == 0:
                            if s_ < 3:
                                Sout, sres = SS[pset][s_ + 1], ("Sst", pset, s_ + 1)
                            else:
                                Sout, sres = SS[1 - pset][0], ("Sst", 1 - pset, 0)
                            bsn = nb()
                            P.op("pe", "matmul", [("GT", pset, s_), ("Sst", pset, s_)], [("ps", bsn)], psf(bsn, 0, 128),
                                 lhsT=GT[pset][s_][:], rhs=Sin[:, 0:128], start=True, stop=False)
                            P.op("pe", "matmul", [("Hb", pset, s_), "ident_b"], [("ps", bsn)], psf(bsn, 0, 128),
                                 lhsT=ident_b[:], rhs=Hb[pset][s_][:], start=False, stop=True)
                            P.op("pe", "matmul", [("GT", pset, s_), ("Sst", pset, s_)], [("ps", bsn)], psf(bsn, 128, 256),
                                 lhsT=GT[pset][s_][:], rhs=Sin[:, 128:256], start=True, stop=True)
                            P.op("act", "copy", [("ps", bsn)], [sres], out=Sout[:], in_=psf(bsn, 0, 256))
                            if tb == 3 and g4 == 1 and s_ == 3 and "nosf" not in stage:
                                P.op("dve", "tensor_copy", [("ps", bsn)], ["SF"], out=SF[:], in_=psf(bsn, 0, 256))
                        else:
                            tcol = cx["tcol"]
                            bo = nb()
                            P.op("pe", "matmul", [("Sst", pset, s_), ("Qt", pset, s_)], [("ps", bo)], psf(bo, 0, 128),
                                 lhsT=Sin[:, 0:128], rhs=Qt[pset][s_][:], start=True, stop=True)
                            bh_ = nb()
                            P.op("pe", "matmul", [("Sst", pset, s_), ("Qt", pset, s_)], [("ps", bh_)], psf(bh_, 0, 128),
                                 lhsT=Sin[:, 128:256], rhs=Qt[pset][s_][:], start=True, stop=True)
                            for hf in range(2):
                                ps_ = slice(hf * 64, (hf + 1) * 64)
                                P.op("dve", "tensor_tensor", [("ps", bo), ("OL", tcol // 64)], [("OL", tcol // 64)],
                                     out=OL[ps_, tcol:tcol + 64], in0=psf(bo, hf * 64, (hf + 1) * 64)[ps_],
                                     in1=OL[ps_, tcol:tcol + 64], op=ALU.add)
                                P.op("act", "copy", [("ps", bh_)], [("QH", tcol // 64)], out=QH[ps_, tcol:tcol + 64],
                                     in_=psf(bh_, hf * 64, (hf + 1) * 64)[ps_])

                    unitsB_prev = pendingB[0]
                    nA, nB = len(unitsA), len(unitsB_prev)
                    bi_ = 0
                    for ai, (st_i, s_) in enumerate(unitsA):
                        unitA(st_i, s_)
                        for _ in range(per_unit):
                            if prep_pos[0] < n_prep:
                                P.replay(prep_thunks[prep_pos[0]])
                                prep_pos[0] += 1
                        if nB and ai % 6 == 5 and bi_ < nB:
                            unitsB_prev[bi_]()
                            bi_ += 1
                    while bi_ < nB:
                        unitsB_prev[bi_]()
                        bi_ += 1
                    newB = []
                    for s_ in range(4):
                        newB.append(lambda s_=s_, f=unitB: f(0, s_))
                    for s_ in range(4):
                        newB.append(lambda s_=s_, f=unitB: f(1, s_))
                    pendingB[0] = newB
                while prep_pos[0] < n_prep:
                    P.replay(prep_thunks[prep_pos[0]])
                    prep_pos[0] += 1

            return prep_main, expansions, chunk_groups

        EARLY_PREP = 'early' in stage
        pairs = {}
        if do_rwkv:
            pairs[0] = setup_pair(0)
            pairs[0][0](0)
            pairs[0][1](0)
        pending_wout = [None]

        def do_wout(jj):
            wi_ = load_wout(jj * 128)
            wout_accum([(lambda n: MX[:, (n - 1) * 128:n * 128], wi_, lambda n: [("MX", (n - 1) // 4)])])

        for j in range(4 if do_rwkv else 0):
            if j not in pairs:
                pairs[j] = setup_pair(j)
                pairs[j][0](0)
                pairs[j][1](0)
            elif not EARLY_PREP:
                pass
            prep_main, expansions, chunk_groups = pairs[j]
            for tb in range(4):
                thunks = []
                if tb < 3:
                    P.capture = []
                    prep_main(tb + 1)
                    thunks = P.capture
                    P.capture = None
                if tb == 3 and j < 3 and EARLY_PREP:
                    pairs[j + 1] = setup_pair(j + 1)
                chunk_groups(tb, thunks)
                if tb < 3:
                    expansions(tb + 1)
            for f in pendingB[0]:
                f()
            pendingB[0] = []

            mark("r5")
            if use_cc:
                P.dma("sp", ["SF"], [("ccin", j)], cc_in[j].ap(), SF[:])
                P.coll("pool", [("ccin", j)], [("ccg", j)], "AllGather", ALU.bypass,
                       replica_groups=[[0, 1, 2, 3], [4, 5, 6, 7]], ins=[cc_in[j].ap().opt()], outs=[cc_out[j].ap().opt()])
            else:
                P.dma("sp", ["SF"], [("sfdbg", j)], sf_dbg.ap()[:, j * 256:(j + 1) * 256], SF[:])
            if pending_wout[0] is not None:
                do_wout(pending_wout[0])
                pending_wout[0] = None
            if j < 3 and EARLY_PREP:
                pairs[j + 1][0](0)
                pairs[j + 1][1](0)
            P.dma("sp", [("ccg", j)], GA_RES, GA[:], ccg_view(j))
            for i in range(3):
                bt_ = nb()
                P.op("pe", "transpose", GA_RES + ["ident_f"], [("ps", bt_)], out=psf(bt_, 0, 128), in_=GA[:, i, 128:256],
                     identity=ident_f[:])
                P.op("dve", "tensor_scalar", [("ps", bt_), "fm"] + GA_RES, GA_RES, out=LT[:, i, :], in0=psf(bt_, 0, 128),
                     scalar1=fm[:, i:i + 1], scalar2=None, op0=ALU.mult)
                P.op("dve", "scalar_tensor_tensor", ["ident_f", "fm"] + GA_RES, GA_RES, out=LT[:, i, :], in0=ident_f[:],
                     scalar=fm[:, 4 + i:5 + i], in1=LT[:, i, :], op0=ALU.mult, op1=ALU.add)
                P.op("dve", "tensor_scalar", ["fm"] + GA_RES, GA_RES, out=GA[:, i, 0:128], in0=GA[:, i, 0:128],
                     scalar1=fm[:, i:i + 1], scalar2=None, op0=ALU.mult)
            P.op("dve", "tensor_copy", GA_RES, ["Sst"], out=Sst[:], in_=GA[:, 0, 0:128])
            for i in range(1, 3):
                bm_ = nb()
                P.op("pe", "matmul", GA_RES + ["Sst"], [("ps", bm_)], psf(bm_, 0, 128), lhsT=LT[:, i, :], rhs=Sst[:],
                     start=True, stop=False)
                P.op("pe", "matmul", GA_RES + ["ident_f"], [("ps", bm_)], psf(bm_, 0, 128), lhsT=ident_f[:], rhs=GA[:, i, 0:128],
                     start=False, stop=True)
                P.op("dve", "tensor_copy", [("ps", bm_)], ["Sst"], out=Sst[:], in_=psf(bm_, 0, 128))
            mark("r6")
            P.op("act", "copy", ["Sst"], ["Sstb"], out=Sstb[:], in_=Sst[:])
            for tb in range(4):
                cs_ = slice(tb * 512, (tb + 1) * 512)
                O_, cen, sq = T[4][:], T[5][:], T[6][:]
                bf_ = nb()
                P.op("pe", "matmul", ["Sstb"] + [("QH", tb * 8 + c_) for c_ in range(8)], [("ps", bf_)], psf(bf_), lhsT=Sstb[:], rhs=QH[:, cs_], start=True, stop=True)
                P.op("dve", "tensor_tensor", [("ps", bf_)] + [("OL", tb * 8 + c_) for c_ in range(8)], [("T", 4)], out=O_, in0=psf(bf_), in1=OL[:, cs_], op=ALU.add)
                bmu = nb()
                P.op("pe", "matmul", ["bones_s", ("T", 4)], [("ps", bmu)], psf(bmu), lhsT=bones_s[:], rhs=O_, start=True, stop=True)
                P.op("dve", "tensor_tensor", [("ps", bmu), ("T", 4)], [("T", 5)], out=cen, in0=O_, in1=psf(bmu), op=ALU.subtract)
                P.op("act", "activation", [("T", 5)], [("T", 6)], out=sq, in_=cen, func=AF.Square)
                bvar = nb()
                P.op("pe", "matmul", ["bones_s", ("T", 6)], [("ps", bvar)], psf(bvar), lhsT=bones_s[:], rhs=sq, start=True, stop=True)
                P.op("act", "activation", [("ps", bvar), "gneps"], [("T", 6)], out=sq, in_=psf(bvar), func=AF.Sqrt,
                     bias=gneps[:, 0:1], scale=1.0)
                P.op("dve", "reciprocal", [("T", 6)], [("T", 6)], out=sq, in_=sq)
                P.op("dve", "tensor_tensor", [("T", 5), ("T", 6)], [("T", 5)], out=cen, in0=cen, in1=sq, op=ALU.mult)
                P.op("dve", "tensor_scalar", [("T", 5), "pv"], [("T", 5)], out=cen, in0=cen, scalar1=pv[:, LNW + j:LNW + j + 1],
                     scalar2=pv[:, LNB + j:LNB + j + 1], op0=ALU.mult, op1=ALU.add)
                P.op("dve", "tensor_tensor", [("T", 5), ("BON", tb) if tb else ("BON0", j % 2)], [("T", 5)], out=cen, in0=cen,
                     in1=(BON[:, (tb - 1) * 512:tb * 512] if tb else BON0[j % 2][:]), op=ALU.add)
                bg_ = nb()
                P.op("pe", "matmul", ["G2", ("GL", tb)], [("ps", bg_)], psf(bg_), lhsT=G2[:, j * 128:(j + 1) * 128], rhs=GL[:, cs_],
                     start=True, stop=True)
                P.op("dve", "tensor_tensor", [("T", 5), ("ps", bg_)], [("MX", tb), ("X", 0)], out=MX[:, cs_], in0=cen, in1=psf(bg_), op=ALU.mult)
            mark("r7")
            pending_wout[0] = j
        if pending_wout[0] is not None:
            do_wout(pending_wout[0])
        barrier()

    if not stage.startswith(("noffn", "mix")):
        load_gain("norm_ffn2")
        for i in range(1, NT):
            rmsnorm_tile(i, norm_to_HT)
        ffn("ffn2", range(1, NT))

    P.enabled = True
    barrier()
    A.off = phase_mark
    ob = [sb(f"ob{i}", [128, D]) for i in range(2)]
    load_gain("norm_final")

    def norm_to_out(i, col):
        b = i % 2
        P.op("dve", "scalar_tensor_tensor", [("X", i), ("rstd", col), "gbc"], [("ob", b)],
             out=ob[b][:], in0=X[:, i, :], scalar=rstd[:, col:col + 1],
             in1=gbc[:], op0=ALU.mult, op1=ALU.mult)
        P.dma("sp", [("ob", b)], [("out", i)], out.ap()[(i - 1) * 128:i * 128, :], ob[b][:])

    for i in range(1, NT):
        rmsnorm_tile(i, norm_to_out)
    P.wait_all("sp")

    P.emit()
    es.close()
    return nc


_NC_CACHE = {}


def _prep_inputs(inputs):
    x = np.ascontiguousarray(inputs["x"], dtype=np.float32)
    B, S, _ = x.shape
    shared = {"ident": np.eye(128, dtype=np.float32)}
    qi = np.arange(128)[:, None]
    kj = np.arange(256)[None, :]
    valid = np.where(kj < 128, kj > qi, (kj - 128) <= qi)
    maskA = np.where(valid, 0.0, NEG).astype(np.float32)
    mask0 = maskA.copy()
    mask0[:, :128] = NEG
    shared["maskA"] = maskA
    shared["w_in"] = np.ascontiguousarray(inputs["w_in"][0], dtype=np.float32)
    shared["w_out"] = np.ascontiguousarray(inputs["w_out"][0], dtype=np.float32)
    shared["b_in_attn"] = np.ascontiguousarray(inputs["b_in_attn"].reshape(768, 1), dtype=np.float32)
    idx = np.arange(128)
    hh, ii = idx // 64, idx % 64
    same = hh[:, None] == hh[None, :]
    su = same & (ii[:, None] < ii[None, :])
    ui = same & (ii[:, None] <= ii[None, :])
    sl = same & (ii[:, None] > ii[None, :])
    shared["cmask"] = np.concatenate([su, ui, sl], axis=1).astype(np.float32)
    shared["bones"] = same.astype(np.float32)
    scanm = np.ones((1, 512), np.float32)
    scanm[:, ::64] = 0.0
    shared["scanm"] = scanm
    shared["rwkv_shift_mix"] = np.ascontiguousarray(inputs["rwkv_shift_mix"].reshape(1792, 1), dtype=np.float32)
    for nm in ("rwkv_w0", "rwkv_a0", "rwkv_k_k", "rwkv_k_a", "rwkv_r_k", "rwkv_ln_w", "rwkv_ln_b"):
        shared[nm] = np.ascontiguousarray(inputs[nm].reshape(512, 1), dtype=np.float32)
    shared["rwkv_w2"] = np.ascontiguousarray(inputs["rwkv_w2"][0], dtype=np.float32)
    shared["rwkv_a2"] = np.ascontiguousarray(inputs["rwkv_a2"][0], dtype=np.float32)
    shared["rwkv_g2"] = np.ascontiguousarray(inputs["rwkv_g2"][0], dtype=np.float32)
    shared["attn_sinks"] = np.ascontiguousarray(inputs["attn_sinks"].reshape(1, 8), dtype=np.float32)
    for nm in ("ffn1_gate", "ffn1_up", "ffn1_down", "ffn2_gate", "ffn2_up", "ffn2_down"):
        shared[nm] = np.ascontiguousarray(inputs[nm][0], dtype=np.float32)
    for nm in ("norm_ffn1", "norm_mix", "norm_ffn2"):
        shared[nm] = np.ascontiguousarray(inputs[nm].reshape(1, D), dtype=np.float32)
    shared["norm_final"] = np.ascontiguousarray(inputs["norm_final"].reshape(1, D), dtype=np.float32)
    in_maps = []
    for c in range(NCORES):
        b, s = c // 4, c % 4
        xs = np.zeros((NTH, D), np.float32)
        if s > 0:
            xs[:128] = x[b, s * NTOK - 128:s * NTOK]
        xs[128:] = x[b, s * NTOK:(s + 1) * NTOK]
        m = dict(shared)
        m["xs"] = xs
        m["mask1"] = mask0 if s == 0 else maskA
        fmv = np.zeros((1, 8), np.float32)
        for i in range(3):
            fmv[0, i] = 1.0 if i < s else 0.0
            fmv[0, 4 + i] = 1.0 - fmv[0, i]
        m["fm"] = fmv
        in_maps.append(m)
    return in_maps


STAGE = "full"


def kernel(**inputs):
    if STAGE not in _NC_CACHE:
        _NC_CACHE[STAGE] = build(STAGE)
    nc = _NC_CACHE[STAGE]
    in_maps = _prep_inputs(inputs)
    if "ccin" in STAGE:
        for m in in_maps:
            m["ccg"] = np.zeros((4 * 128, 1024), np.float32)
        res = run_bass_kernel_spmd(nc, in_maps, core_ids=list(range(NCORES)))
        ccg = np.concatenate([np.asarray(r["sf_dbg"], dtype=np.float32).reshape(128, 1024) for r in res.results], axis=0)
        for c, m in enumerate(in_maps):
            m["ccg"] = ccg[(c // 4) * 512:(c // 4 + 1) * 512]
    res = run_bass_kernel_spmd(nc, in_maps, core_ids=list(range(NCORES)))
    outs = [np.asarray(r["out"], dtype=np.float32).reshape(NTOK, D) for r in res.results]
    full = np.stack([np.concatenate(outs[b * 4:(b + 1) * 4], axis=0) for b in range(2)], axis=0)
    return full
```
